# Optimizing a Trainium2 kernel written in Bass

```python
import math
import jax, jax.numpy as jnp
from jax import lax
import numpy as np

D_MODEL = 1024
BATCH = 8
SEQ = 8192
DEPTH = 4

HEAD_DIM = 64
N_HEADS_A = 8
ROT_DIM_A = HEAD_DIM // 4
DILATED_BRANCHES = ((128, 1), (512, 4), (2048, 16))
FNET_GROUPS = 4
FNET_GROUP_DIM = 64
N_HEADS_C = 8
C_NOPE_DIM = 64
C_ROPE_DIM = 32
C_V_DIM = 64
C_Q_RANK = 256
C_KV_RANK = 128
S5_GROUPS = 16
S5_GROUP_DIM = 16
S5_STATE = 64
S5_MIN_STEP = 1e-3
S5_MAX_STEP = 1e-1
D_FF = 2816
ROPE_THETA = 500000.0
Q_BLOCK = 128
NORM_EPS = 1e-6
NEG_INF = -1e30
MAX_POS_OFFSET = 1024

A_WIDTH = N_HEADS_A * HEAD_DIM
B_WIDTH = FNET_GROUPS * FNET_GROUP_DIM
EVEN_IN = 3 * A_WIDTH + B_WIDTH
EVEN_OUT = A_WIDTH + B_WIDTH
C_QK_DIM = C_NOPE_DIM + C_ROPE_DIM
C_WIDTH = N_HEADS_C * C_V_DIM
D_WIDTH = S5_GROUPS * S5_GROUP_DIM
ODD_IN = C_Q_RANK + C_KV_RANK + C_ROPE_DIM + D_WIDTH
ODD_OUT = C_WIDTH + D_WIDTH

kernel_name = 'hybrid_dilated_fnet_mla_s5_encoder'


def rms_norm(x, g):
    xf = x.astype(jnp.float32)
    y = xf * lax.rsqrt(jnp.mean(xf * xf, axis=-1, keepdims=True) + NORM_EPS)
    return (y * g.astype(jnp.float32)).astype(x.dtype)


def swiglu(h, w_gate, w_up, w_down):
    return (jax.nn.silu(h @ w_gate) * (h @ w_up)) @ w_down


def rotary_tables(positions, rot_dim):
    inv_freq = ROPE_THETA ** (-jnp.arange(0, rot_dim, 2, dtype=jnp.float32) / rot_dim)
    ang = positions.astype(jnp.float32)[..., None] * inv_freq
    ang = jnp.concatenate([ang, ang], axis=-1)
    return jnp.cos(ang)[:, :, None, :], jnp.sin(ang)[:, :, None, :]


def apply_rotary(t, cos, sin):
    tf = t.astype(jnp.float32)
    half = t.shape[-1] // 2
    rot = jnp.concatenate([-tf[..., half:], tf[..., :half]], axis=-1)
    return (tf * cos + rot * sin).astype(t.dtype)


def dilated_window_branch(q, k, v, dilation, radius):
    bsz, s_len, n_h, e = q.shape
    L = s_len // dilation
    nb = -(-L // radius)
    lp = nb * radius
    pad = lp - L

    def to_sub(t):
        return t.reshape(bsz, L, dilation, n_h, e).transpose(0, 2, 1, 3, 4)

    qs = jnp.pad(to_sub(q), ((0, 0), (0, 0), (0, pad), (0, 0), (0, 0)))
    qs = qs.reshape(bsz, dilation, nb, radius, n_h, e)

    def neighbourhood(t):
        t = jnp.pad(to_sub(t), ((0, 0), (0, 0), (radius, pad + radius), (0, 0), (0, 0)))
        t = t.reshape(bsz, dilation, nb + 2, radius, n_h, e)
        return jnp.concatenate([t[:, :, :-2], t[:, :, 1:-1], t[:, :, 2:]], axis=3)

    kn = neighbourhood(k).astype(jnp.float32)
    vn = neighbourhood(v).astype(jnp.float32)
    s = jnp.einsum('bdnqhe,bdnkhe->bdnhqk', qs.astype(jnp.float32), kn) * (e ** -0.5)
    qi = jnp.arange(radius)
    ki = jnp.arange(3 * radius)
    band = jnp.abs(ki[None, :] - radius - qi[:, None]) <= radius
    key_pos = (jnp.arange(nb)[:, None] - 1) * radius + ki[None, :]
    valid = (key_pos >= 0) & (key_pos < L)
    mask = band[None, :, :] & valid[:, None, :]
    s = jnp.where(mask[:, None, :, :], s, NEG_INF)
    lse = jax.nn.logsumexp(s, axis=-1)
    p = jnp.exp(s - lse[..., None])
    o = jnp.einsum('bdnhqk,bdnkhe->bdnqhe', p, vn)
    o = o.reshape(bsz, dilation, lp, n_h, e)[:, :, :L]
    o = o.transpose(0, 2, 1, 3, 4).reshape(bsz, s_len, n_h, e)
    lse = lse.transpose(0, 1, 2, 4, 3).reshape(bsz, dilation, lp, n_h)[:, :, :L]
    lse = lse.transpose(0, 2, 1, 3).reshape(bsz, s_len, n_h)
    return o, lse


def dense_attention_blocks(q, k, v):
    bsz, s_len, n_h, e = q.shape
    ev = v.shape[-1]
    nq = s_len // Q_BLOCK
    qb = q.reshape(bsz, nq, Q_BLOCK, n_h, e).transpose(1, 0, 2, 3, 4)
    kf = k.astype(jnp.float32)
    vf = v.astype(jnp.float32)
    scale = e ** -0.5

    def one_block(q_blk):
        s = jnp.einsum('bqhe,bkhe->bhqk', q_blk.astype(jnp.float32), kf) * scale
        p = jax.nn.softmax(s, axis=-1)
        return jnp.einsum('bhqk,bkhv->bqhv', p, vf)

    o = lax.map(one_block, qb)
    return o.transpose(1, 0, 2, 3, 4).reshape(bsz, s_len, n_h, ev).astype(q.dtype)


def even_mixer(h, cos_a, sin_a, w_in, w_out, q_norm, k_norm, b_mix):
    bsz, s_len, _ = h.shape
    z = h @ w_in
    q, k, v, f = jnp.split(z, [A_WIDTH, 2 * A_WIDTH, 3 * A_WIDTH], axis=-1)
    q = rms_norm(q.reshape(bsz, s_len, N_HEADS_A, HEAD_DIM), q_norm)
    k = rms_norm(k.reshape(bsz, s_len, N_HEADS_A, HEAD_DIM), k_norm)
    v = v.reshape(bsz, s_len, N_HEADS_A, HEAD_DIM)
    q = jnp.concatenate([apply_rotary(q[..., :ROT_DIM_A], cos_a, sin_a), q[..., ROT_DIM_A:]], axis=-1)
    k = jnp.concatenate([apply_rotary(k[..., :ROT_DIM_A], cos_a, sin_a), k[..., ROT_DIM_A:]], axis=-1)
    outs, lses = [], []
    for window, dilation in DILATED_BRANCHES:
        o, lse = dilated_window_branch(q, k, v, dilation, window // (2 * dilation))
        outs.append(o)
        lses.append(lse)
    wts = jax.nn.softmax(jnp.stack(lses, axis=0), axis=0)
    a_out = jnp.sum(wts[..., None] * jnp.stack(outs, axis=0), axis=0).reshape(bsz, s_len, A_WIDTH)
    fg = f.astype(jnp.float32).reshape(bsz, s_len, FNET_GROUPS, FNET_GROUP_DIM).transpose(0, 2, 1, 3)
    spec = jnp.fft.fft2(fg, norm='ortho').real
    b_out = jnp.einsum('bgsc,gcd->bsgd', spec, b_mix.astype(jnp.float32)).reshape(bsz, s_len, B_WIDTH)
    merged = jnp.concatenate([a_out, b_out], axis=-1).astype(h.dtype)
    return merged @ w_out


def _linear_recurrence(left, right):
    a_l, b_l = left
    a_r, b_r = right
    return a_l * a_r, a_r * b_l + b_r


def s5_bidirectional(u, lam_re, lam_im, log_step, b_re, b_im, c_re, c_im, d_skip, w_glu, b_glu):
    bsz, s_len, _ = u.shape
    uf = u.astype(jnp.float32)
    ug = uf.reshape(bsz, s_len, S5_GROUPS, S5_GROUP_DIM)
    lam = lax.complex(lam_re.astype(jnp.float32), lam_im.astype(jnp.float32))
    step = jnp.exp(log_step.astype(jnp.float32))[..., None]
    lam_bar = jnp.exp(lam * step)
    b_bar = ((lam_bar - 1.0) / lam)[..., None] * lax.complex(b_re.astype(jnp.float32), b_im.astype(jnp.float32))
    c_mat = lax.complex(c_re.astype(jnp.float32), c_im.astype(jnp.float32))
    y = uf * d_skip.astype(jnp.float32)
    for direction, reverse in ((0, False), (1, True)):
        bu = jnp.einsum('bsgh,gph->bsgp', ug, b_bar[direction])
        a = jnp.broadcast_to(lam_bar[direction][None, None], (1, s_len, S5_GROUPS, S5_STATE))
        _, states = lax.associative_scan(_linear_recurrence, (a, bu), reverse=reverse, axis=1)
        y = y + jnp.einsum('bsgp,ghp->bsgh', states, c_mat[direction]).real.reshape(bsz, s_len, D_WIDTH)
    z = jax.nn.gelu(y)
    out = z * jax.nn.sigmoid(z @ w_glu.astype(jnp.float32) + b_glu.astype(jnp.float32))
    return out.astype(u.dtype)


def odd_mixer(h, cos_c, sin_c, w_in, w_out, q_lat_norm, w_q_up, kv_lat_norm, w_kv_up, q_norm, k_norm,
              lam_re, lam_im, log_step, b_re, b_im, c_re, c_im, d_skip, w_glu, b_glu):
    bsz, s_len, _ = h.shape
    z = h @ w_in
    q_lat, kv_lat, k_pe, u = jnp.split(z, [C_Q_RANK, C_Q_RANK + C_KV_RANK, C_Q_RANK + C_KV_RANK + C_ROPE_DIM], axis=-1)
    q = (rms_norm(q_lat, q_lat_norm) @ w_q_up).reshape(bsz, s_len, N_HEADS_C, C_QK_DIM)
    kv = (rms_norm(kv_lat, kv_lat_norm) @ w_kv_up).reshape(bsz, s_len, N_HEADS_C, C_NOPE_DIM + C_V_DIM)
    k_nope, v = kv[..., :C_NOPE_DIM], kv[..., C_NOPE_DIM:]
    k_pe = jnp.broadcast_to(k_pe[:, :, None, :], (bsz, s_len, N_HEADS_C, C_ROPE_DIM))
    k = jnp.concatenate([k_nope, k_pe], axis=-1)
    q = rms_norm(q, q_norm)
    k = rms_norm(k, k_norm)
    q = jnp.concatenate([q[..., :C_NOPE_DIM], apply_rotary(q[..., C_NOPE_DIM:], cos_c, sin_c)], axis=-1)
    k = jnp.concatenate([k[..., :C_NOPE_DIM], apply_rotary(k[..., C_NOPE_DIM:], cos_c, sin_c)], axis=-1)
    c_out = dense_attention_blocks(q, k, v).reshape(bsz, s_len, C_WIDTH)
    d_out = s5_bidirectional(u, lam_re, lam_im, log_step, b_re, b_im, c_re, c_im, d_skip, w_glu, b_glu)
    merged = jnp.concatenate([c_out, d_out.astype(c_out.dtype)], axis=-1).astype(h.dtype)
    return merged @ w_out


def setup_inputs(seed: int = 0) -> dict:
    key = jax.random.key(seed)
    ks = iter(jax.random.split(key, 40))
    f32 = jnp.float32

    def normal(shape, scale):
        return jax.random.normal(next(ks), shape, f32) * scale

    def gain(shape):
        return 1.0 + normal(shape, 0.02)

    ne, no = (DEPTH + 1) // 2, DEPTH // 2
    x = normal((BATCH, SEQ, D_MODEL), 1.0)
    start = jax.random.randint(next(ks), (BATCH, 1), 0, MAX_POS_OFFSET, dtype=jnp.int32)
    positions = start + jnp.arange(SEQ, dtype=jnp.int32)[None, :]
    d_in, f_in = D_MODEL ** -0.5, D_FF ** -0.5
    return {
        'x': x,
        'positions': positions,
        'ffn1_norm': gain((DEPTH, D_MODEL)),
        'ffn1_w_gate': normal((DEPTH, D_MODEL, D_FF), d_in),
        'ffn1_w_up': normal((DEPTH, D_MODEL, D_FF), d_in),
        'ffn1_w_down': normal((DEPTH, D_FF, D_MODEL), f_in),
        'mix_norm': gain((DEPTH, D_MODEL)),
        'ffn2_norm': gain((DEPTH, D_MODEL)),
        'ffn2_w_gate': normal((DEPTH, D_MODEL, D_FF), d_in),
        'ffn2_w_up': normal((DEPTH, D_MODEL, D_FF), d_in),
        'ffn2_w_down': normal((DEPTH, D_FF, D_MODEL), f_in),
        'ab_w_in': normal((ne, D_MODEL, EVEN_IN), d_in),
        'ab_w_out': normal((ne, EVEN_OUT, D_MODEL), EVEN_OUT ** -0.5),
        'a_q_norm': gain((ne, HEAD_DIM)),
        'a_k_norm': gain((ne, HEAD_DIM)),
        'b_w_mix': normal((ne, FNET_GROUPS, FNET_GROUP_DIM, FNET_GROUP_DIM), FNET_GROUP_DIM ** -0.5),
        'cd_w_in': normal((no, D_MODEL, ODD_IN), d_in),
        'cd_w_out': normal((no, ODD_OUT, D_MODEL), ODD_OUT ** -0.5),
        'c_q_lat_norm': gain((no, C_Q_RANK)),
        'c_w_q_up': normal((no, C_Q_RANK, N_HEADS_C * C_QK_DIM), C_Q_RANK ** -0.5),
        'c_kv_lat_norm': gain((no, C_KV_RANK)),
        'c_w_kv_up': normal((no, C_KV_RANK, N_HEADS_C * (C_NOPE_DIM + C_V_DIM)), C_KV_RANK ** -0.5),
        'c_q_norm': gain((no, C_QK_DIM)),
        'c_k_norm': gain((no, C_QK_DIM)),
        'd_lam_re': -0.5 + normal((no, 2, S5_GROUPS, S5_STATE), 0.01),
        'd_lam_im': jnp.pi * jnp.arange(S5_STATE, dtype=f32) + normal((no, 2, S5_GROUPS, S5_STATE), 0.01),
        'd_log_step': jax.random.uniform(next(ks), (no, 2, S5_GROUPS), f32, math.log(S5_MIN_STEP), math.log(S5_MAX_STEP)),
        'd_b_re': normal((no, 2, S5_GROUPS, S5_STATE, S5_GROUP_DIM), (2 * S5_GROUP_DIM) ** -0.5),
        'd_b_im': normal((no, 2, S5_GROUPS, S5_STATE, S5_GROUP_DIM), (2 * S5_GROUP_DIM) ** -0.5),
        'd_c_re': normal((no, 2, S5_GROUPS, S5_GROUP_DIM, S5_STATE), (2 * S5_STATE) ** -0.5),
        'd_c_im': normal((no, 2, S5_GROUPS, S5_GROUP_DIM, S5_STATE), (2 * S5_STATE) ** -0.5),
        'd_skip': normal((no, D_WIDTH), 1.0),
        'd_w_glu': normal((no, D_WIDTH, D_WIDTH), D_WIDTH ** -0.5),
        'd_b_glu': normal((no, D_WIDTH), 0.01),
    }


def reference(x, positions, ffn1_norm, ffn1_w_gate, ffn1_w_up, ffn1_w_down, mix_norm,
              ffn2_norm, ffn2_w_gate, ffn2_w_up, ffn2_w_down,
              ab_w_in, ab_w_out, a_q_norm, a_k_norm, b_w_mix,
              cd_w_in, cd_w_out, c_q_lat_norm, c_w_q_up, c_kv_lat_norm, c_w_kv_up, c_q_norm, c_k_norm,
              d_lam_re, d_lam_im, d_log_step, d_b_re, d_b_im, d_c_re, d_c_im, d_skip, d_w_glu, d_b_glu):
    cos_a, sin_a = rotary_tables(positions, ROT_DIM_A)
    cos_c, sin_c = rotary_tables(positions, C_ROPE_DIM)
    for layer in range(DEPTH):
        x = x + 0.5 * swiglu(rms_norm(x, ffn1_norm[layer]), ffn1_w_gate[layer], ffn1_w_up[layer], ffn1_w_down[layer])
        h = rms_norm(x, mix_norm[layer])
        i = layer // 2
        if layer % 2 == 0:
            x = x + even_mixer(h, cos_a, sin_a, ab_w_in[i], ab_w_out[i], a_q_norm[i], a_k_norm[i], b_w_mix[i])
        else:
            x = x + odd_mixer(h, cos_c, sin_c, cd_w_in[i], cd_w_out[i], c_q_lat_norm[i], c_w_q_up[i],
                              c_kv_lat_norm[i], c_w_kv_up[i], c_q_norm[i], c_k_norm[i],
                              d_lam_re[i], d_lam_im[i], d_log_step[i], d_b_re[i], d_b_im[i],
                              d_c_re[i], d_c_im[i], d_skip[i], d_w_glu[i], d_b_glu[i])
        x = x + 0.5 * swiglu(rms_norm(x, ffn2_norm[layer]), ffn2_w_gate[layer], ffn2_w_up[layer], ffn2_w_down[layer])
    return x
```

```python
import numpy as np
from contextlib import ExitStack
import concourse.bass as bass
import concourse.mybir as mybir

F32 = mybir.dt.float32
BF16 = mybir.dt.bfloat16
I32 = mybir.dt.int32
AF = mybir.ActivationFunctionType
ALU = mybir.AluOpType
AX = mybir.AxisListType

EPOCH = 30000
N_DMA_SEMS = 40


class V:
    __slots__ = ("ap", "key")

    def __init__(self, ap, key):
        self.ap = ap
        self.key = key

    def __getitem__(self, idx):
        return V(self.ap[idx], self.key)

    def k(self, sub):
        return V(self.ap, (self.key, sub))

    def re(self, s, **kw):
        return V(self.ap.rearrange(s, **kw), self.key)

    def bc(self, dt):
        return V(self.ap.bitcast(dt), self.key)

    def bcast(self, shape):
        return V(self.ap.to_broadcast(list(shape)), self.key)

    def ub(self, axis, shape):
        return V(self.ap.unsqueeze(axis).to_broadcast(list(shape)), self.key)


class Op:
    __slots__ = ("eng", "emit", "deps", "sig", "waits", "is_dma", "dsem", "dval", "needs_sig", "n")


class Prog:
    COMPUTE = ("pe", "act", "dve", "pool")

    def __init__(self, nc):
        self.nc = nc
        self.ops = []
        self.last_w = {}
        self.rd_eng = {}
        self.rd_dma = {}
        self.es = ExitStack()
        self.same_eng_sync = True
        self._n = 0
        self.gs = ExitStack()
        self.NEP = {"pe": 6, "act": 6, "dve": 8, "pool": 3}
        self.sems = {e: [self.gs.enter_context(nc.semaphore(f"s_{e}_{i}")) for i in range(self.NEP[e])]
                     for e in self.COMPUTE}
        self.dsems = [self.gs.enter_context(nc.semaphore(f"s_dma_{i}")) for i in range(N_DMA_SEMS)]
        self.cnt = {e: 0 for e in self.COMPUTE}
        self.dval = [0] * N_DMA_SEMS
        self.rot = 0
        self.know = {}
        self.bar = {}
        self.stats = {e: 0 for e in ("pe", "act", "dve", "pool", "sp")}

    def sb(self, name, shape, dt, glob=False):
        self._u = getattr(self, "_u", 0) + 1
        name = f"{name}_u{self._u}"
        t = (self.gs if glob else self.es).enter_context(self.nc.sbuf_tensor(name, list(shape), dt))
        return V(t[:], name)

    def ps(self, name, shape, dt):
        self._u = getattr(self, "_u", 0) + 1
        name = f"{name}_u{self._u}"
        t = self.es.enter_context(self.nc.psum_tensor(name, list(shape), dt))
        return V(t[:], name)

    def dram(self, name, shape, dt, kind="Internal"):
        t = self.nc.dram_tensor(name, list(shape), dt, kind=kind)
        return V(t.ap(), name)

    def add(self, eng, emit, reads=(), writes=(), is_dma=False):
        op = Op()
        op.eng = eng
        op.emit = emit
        op.is_dma = is_dma
        op.sig = None
        op.needs_sig = False
        op.waits = None
        op.dsem = None
        op.dval = None
        op.n = self._n
        self._n += 1
        deps = {}
        rk = [v.key if isinstance(v, V) else v for v in reads]
        wk = [v.key if isinstance(v, V) else v for v in writes]
        for k in rk:
            w = self.last_w.get(k)
            if w is not None:
                deps[w.n] = w
        for k in wk:
            w = self.last_w.get(k)
            if w is not None:
                deps[w.n] = w
            for r in self.rd_eng.get(k, {}).values():
                deps[r.n] = r
            for r in self.rd_dma.get(k, ()):
                deps[r.n] = r
        for k in wk:
            self.last_w[k] = op
            self.rd_eng[k] = {}
            self.rd_dma[k] = []
        for k in rk:
            if k in wk:
                continue
            if is_dma:
                self.rd_dma.setdefault(k, []).append(op)
            else:
                self.rd_eng.setdefault(k, {})[eng] = op
        deps.pop(op.n, None)
        op.deps = list(deps.values())
        self.ops.append(op)
        return op

    def dma(self, q, out, in_, extra_reads=(), extra_writes=()):
        o, i = out.ap, in_.ap
        return self.add(q, lambda e: e.dma_start(out=o, in_=i), [in_] + list(extra_reads),
                        [out] + list(extra_writes), is_dma=True)

    def mm(self, out, lhsT, rhs, start=True, stop=True, **kw):
        o, l, r = out.ap, lhsT.ap, rhs.ap
        rd = [lhsT, rhs] + ([] if start else [out])
        return self.add("pe", lambda e: e.matmul(o, l, r, start=start, stop=stop, **kw), rd, [out])

    def tr(self, out, in_, ident):
        o, i, d = out.ap, in_.ap, ident.ap
        return self.add("pe", lambda e: e.transpose(o, i, d), [in_, ident], [out])

    def act(self, out, in_, func, scale=1.0, bias=0.0, eng="act", accum=None):
        o, i = out.ap, in_.ap
        rd = [in_]
        wr = [out]
        kw = {}
        if isinstance(scale, V):
            rd.append(scale)
            kw["scale"] = scale.ap
        else:
            kw["scale"] = scale
        if isinstance(bias, V):
            rd.append(bias)
            kw["bias"] = bias.ap
        else:
            kw["bias"] = bias
        if accum is not None:
            kw["accum_out"] = accum.ap
            wr.append(accum)
        return self.add(eng, lambda e: e.activation(o, i, func, **kw), rd, wr)

    def tt(self, out, in0, in1, op, eng="dve"):
        o, a, b = out.ap, in0.ap, in1.ap
        return self.add(eng, lambda e: e.tensor_tensor(o, a, b, op), [in0, in1], [out])

    def ts(self, out, in0, s1, s2=None, op0=ALU.mult, op1=None, eng="dve"):
        o, a = out.ap, in0.ap
        rd = [in0]
        a1 = s1
        a2 = s2
        if isinstance(s1, V):
            rd.append(s1)
            a1 = s1.ap
        if isinstance(s2, V):
            rd.append(s2)
            a2 = s2.ap
        if op1 is None:
            return self.add(eng, lambda e: e.tensor_scalar(o, a, a1, None, op0), rd, [out])
        return self.add(eng, lambda e: e.tensor_scalar(o, a, a1, a2, op0, op1), rd, [out])

    def stt(self, out, in0, scalar, in1, op0, op1):
        o, a, b = out.ap, in0.ap, in1.ap
        rd = [in0, in1]
        s = scalar
        if isinstance(scalar, V):
            rd.append(scalar)
            s = scalar.ap
        return self.add("dve", lambda e: e.scalar_tensor_tensor(o, a, s, b, op0, op1), rd, [out])

    def copy(self, out, in_, eng="dve"):
        o, i = out.ap, in_.ap
        if eng == "act":
            return self.add(eng, lambda e: e.copy(o, i), [in_], [out])
        return self.add(eng, lambda e: e.tensor_copy(o, i), [in_], [out])

    def memset(self, out, val, eng="dve"):
        o = out.ap
        return self.add(eng, lambda e: e.memset(o, val), [], [out])

    def reduce(self, out, in_, op=ALU.add, axis=AX.X, eng="dve"):
        o, i = out.ap, in_.ap
        return self.add(eng, lambda e: e.tensor_reduce(o, i, axis, op), [in_], [out])

    def recip(self, out, in_):
        o, i = out.ap, in_.ap
        return self.add("dve", lambda e: e.reciprocal(o, i), [in_], [out])

    def _need_wait(self, op, d):
        if d.is_dma:
            return True
        if d.eng != op.eng:
            return True
        if op.is_dma:
            return True
        if op.eng == "pe":
            return False
        return self.same_eng_sync

    def _semval(self, key, val):
        if isinstance(key, tuple):
            return (self.dsems[key[1]], val)
        sig = val - 1
        assert sig // EPOCH < self.NEP[key], ("too many signals", key, sig)
        return (self.sems[key][sig // EPOCH], sig % EPOCH + 1)

    def flush(self):
        nc = self.nc
        ops = self.ops
        last = {}
        for op in ops:
            for d in op.deps:
                if self._need_wait(op, d):
                    d.needs_sig = True
            if not op.is_dma:
                last[op.eng] = op
        for op in last.values():
            op.needs_sig = True
        for op in ops:
            if not op.is_dma and op.needs_sig:
                op.sig = self.cnt[op.eng]
                self.cnt[op.eng] += 1
        seen_first = set()
        for op in ops:
            K = self.know.setdefault(op.eng, {})
            need = {}
            if op.eng not in seen_first:
                seen_first.add(op.eng)
                for k, v in self.bar.pop(op.eng, {}).items():
                    need[k] = max(need.get(k, 0), v)
            if op.is_dma:
                s = self.rot
                self.rot = (self.rot + 1) % N_DMA_SEMS
                if self.dval[s] > 0:
                    need[("d", s)] = max(need.get(("d", s), 0), self.dval[s])
                self.dval[s] += 16
                op.dsem = s
                op.dval = self.dval[s]
            for d in op.deps:
                if not self._need_wait(op, d):
                    continue
                if d.is_dma:
                    key, val = ("d", d.dsem), d.dval
                else:
                    key, val = d.eng, d.sig + 1
                if need.get(key, 0) < val:
                    need[key] = val
            waits = []
            for key, val in need.items():
                if K.get(key, 0) >= val:
                    continue
                K[key] = val
                waits.append(self._semval(key, val))
            op.waits = waits
            self.stats[op.eng] += 1
        by = {}
        for op in ops:
            by.setdefault(op.eng, []).append(op)
        sems, dsems = self.sems, self.dsems

        def run(engname, e):
            for op in by.get(engname, []):
                for (s, v) in op.waits:
                    e.wait_ge(s, v)
                ins = op.emit(e)
                if op.is_dma:
                    ins.then_inc(dsems[op.dsem], 16)
                elif op.sig is not None:
                    ins.then_inc(sems[op.eng][op.sig // EPOCH], 1)

        with nc.Block() as block:
            @block.sync
            def _(e):
                run("sp", e)

            @block.tensor
            def _(e):
                run("pe", e)

            @block.scalar
            def _(e):
                run("act", e)

            @block.vector
            def _(e):
                run("dve", e)

            @block.gpsimd
            def _(e):
                run("pool", e)
        front = {}
        for e in self.COMPUTE:
            if self.cnt[e] > 0:
                front[e] = self.cnt[e]
        for s in range(N_DMA_SEMS):
            if self.dval[s] > 0:
                front[("d", s)] = self.dval[s]
        self.bar = {e: dict(front) for e in ("pe", "act", "dve", "pool", "sp")}
        self.ops = []
        self.last_w = {}
        self.rd_eng = {}
        self.rd_dma = {}
        self.es.close()
        self.es = ExitStack()

    def finish(self):
        nc = self.nc
        self.flush()
        need = self.bar["sp"]
        K = self.know.setdefault("sp", {})
        dsems = self.dsems
        waits = [self._semval(k, v) for k, v in need.items() if K.get(k, 0) < v]
        with nc.Block() as block:
            @block.sync
            def _(e):
                for (s, v) in waits:
                    e.wait_ge(s, v)

from concourse.bass_utils import run_bass_kernel_spmd
import math

D = 1024
DFF = 2816
EPS = 1e-6
TWO_PI = 2.0 * math.pi
CW1 = 6.28125
CW2 = TWO_PI - 6.28125


class Ctx:
    pass


def make_psum(p):
    return [p.ps(f"psb{i}", [128, 512], F32) for i in range(8)]


def sincos(p, out_sin, out_cos, ang, t1, t2, ti):
    p.ts(t1, ang, 1.0 / TWO_PI, None, ALU.mult)
    p.copy(ti, t1)
    p.copy(t1, ti)
    p.stt(t2, t1, -CW1, ang, ALU.mult, ALU.add)
    p.stt(t2, t1, -CW2, t2, ALU.mult, ALU.add)
    p.ts(t1, t2, 3.1415925, -3.1415925, ALU.min, ALU.max)
    p.act(out_sin, t1, AF.Sin)
    p.ts(t1, t2, math.pi / 2, None, ALU.add)
    p.ts(ti.bc(F32), t1, math.pi, None, ALU.is_gt)
    p.stt(t1, ti.bc(F32), -TWO_PI, t1, ALU.mult, ALU.add)
    p.ts(t1, t1, 3.1415925, -3.1415925, ALU.min, ALU.max)
    p.act(out_cos, t1, AF.Sin)


def load_w_bf16(p, dst, src, rows, cols, stage, colchunk, tagi=[0]):
    nrc = rows // 128
    engs = ("dve", "pool", "act")
    for c in range(nrc):
        for c0 in range(0, cols, colchunk):
            w = min(colchunk, cols - c0)
            i = tagi[0]
            tagi[0] += 1
            st = stage.k(i % 2)[:, (i % 2), 0:w]
            p.dma("sp", st, src[c * 128:(c + 1) * 128, c0:c0 + w])
            p.copy(dst.k(c)[:, c, c0:c0 + w], st, eng=engs[i % 3])


def norm_T(p, C, xb, nt, gcol, hT, xn, ss, pst):
    for j in range(nt):
        p.act(xn[:, j, :], xb[:, j, :], AF.Square, accum=ss[:, j:j + 1])
    p.ts(ss[:, 4:4 + nt], ss[:, 0:nt], 1.0 / D, EPS, ALU.mult, ALU.add)
    p.act(ss[:, 4:4 + nt], ss[:, 4:4 + nt], AF.Sqrt)
    p.recip(ss[:, 8:8 + nt], ss[:, 4:4 + nt])
    for j in range(nt):
        p.ts(xn[:, j, :], xb[:, j, :], ss[:, 8 + j:9 + j], None, ALU.mult)
    for c in range(8):
        ps = pst[c % 2]
        pv = ps.bc(BF16)
        for j in range(nt):
            p.tr(pv[:, j * 128:(j + 1) * 128], xn[:, j, c * 128:(c + 1) * 128], C.ident_b)
        p.act(hT[:, c, 0:nt * 128], pv[:, 0:nt * 128], AF.Copy, scale=gcol[:, c:c + 1],
              eng=("act" if c % 2 == 0 else "act"))


def ffn_phase(p, C, x_src, x_dst, wg, wu, wd, gcol, S, TB=256):
    nt = TB // 128
    NF = DFF // 128
    Wg = p.sb("Wg", [128, 8, DFF], BF16)
    Wu = p.sb("Wu", [128, 8, DFF], BF16)
    Wd = p.sb("Wd", [128, NF, D], BF16)
    stage = p.sb("stage", [128, 2, 1408], F32)
    xbs = [p.sb(f"xb{i}", [128, nt, D], F32) for i in range(2)]
    xn = p.sb("xn", [128, nt, D], BF16)
    hT = p.sb("hT", [128, 8, TB], BF16)
    actT = p.sb("actT", [128, NF, TB], BF16)
    sg = [p.sb(f"sg{i}", [128, TB], F32) for i in range(2)]
    ss = p.sb("ss", [128, 12], F32)
    PS = make_psum(p)
    load_w_bf16(p, Wg, wg, D, DFF, stage, 1408)
    load_w_bf16(p, Wu, wu, D, DFF, stage, 1408)
    load_w_bf16(p, Wd, wd, DFF, D, stage, 1024)
    for b in range(S // TB):
        xb = xbs[b % 2]
        rows = x_src.k(b)[b * TB:(b + 1) * TB, :].re("(j q) d -> q j d", q=128)
        p.dma("sp", xb, rows)
        norm_T(p, C, xb, nt, gcol, hT, xn, ss, PS[0:2])
        for f in range(NF):
            pg = PS[2 + (f % 2)]
            pu = PS[4 + (f % 2)]
            for c in range(8):
                p.mm(pg[:, 0:TB], Wg.k(c)[:, c, f * 128:(f + 1) * 128], hT[:, c, :], start=(c == 0), stop=(c == 7))
            for c in range(8):
                p.mm(pu[:, 0:TB], Wu.k(c)[:, c, f * 128:(f + 1) * 128], hT[:, c, :], start=(c == 0), stop=(c == 7))
            s = sg[f % 2]
            p.act(s, pg[:, 0:TB], AF.Silu)
            p.tt(actT.k(f)[:, f, :], s, pu[:, 0:TB], ALU.mult)
        for j in range(nt):
            for dh in range(2):
                po = PS[6 + ((j * 2 + dh) % 2)]
                for f in range(NF):
                    p.mm(po, actT.k(f)[:, f, j * 128:(j + 1) * 128], Wd.k(f)[:, f, dh * 512:(dh + 1) * 512],
                         start=(f == 0), stop=(f == NF - 1))
                p.stt(xb[:, j, dh * 512:(dh + 1) * 512], po, 0.5, xb[:, j, dh * 512:(dh + 1) * 512], ALU.mult, ALU.add)
        p.dma("sp", x_dst.k(b)[b * TB:(b + 1) * TB, :].re("(j q) d -> q j d", q=128), xb)
    p.flush()


def load_block(p, C, x, b, TB, xb):
    p.dma("sp", xb, x.k(("r", b))[b * TB:(b + 1) * TB, :].re("(j q) d -> q j d", q=128))


def rot_pair(p, out3, in3, lo, half, cosb, sinb, ra, rb):
    t1 = in3[:, :, lo:lo + half]
    t2 = in3[:, :, lo + half:lo + 2 * half]
    p.tt(ra, t1, cosb, ALU.mult)
    p.tt(rb, t2, sinb, ALU.mult)
    p.tt(out3[:, :, lo:lo + half], ra, rb, ALU.subtract)
    p.tt(ra, t2, cosb, ALU.mult)
    p.tt(rb, t1, sinb, ALU.mult)
    p.tt(out3[:, :, lo + half:lo + 2 * half], ra, rb, ALU.add)


def rsqrt_mean(p, out, ssum, n, tmp):
    p.ts(tmp, ssum, 1.0 / n, EPS, ALU.mult, ALU.add)
    p.act(tmp, tmp, AF.Sqrt)
    p.recip(out, tmp)


def even_proj_phase(p, C, i, gcol, S):
    TB, nt = 512, 4
    Win = p.sb("Win", [128, 8, 1792], BF16)
    stage = p.sb("stage", [128, 2, 1792], F32)
    load_w_bf16(p, Win, C.ab_w_in[i], D, 1792, stage, 1792)
    gqk = p.sb("gqk", [128, 16, 64], F32)
    p.dma("sp", gqk, C.gqk[i])
    p.ts(gqk[:, 0:8, :], gqk[:, 0:8, :], 0.125, None, ALU.mult)
    xbs = [p.sb(f"xb{k}", [128, nt, D], F32) for k in range(2)]
    xn = p.sb("xn", [128, nt, D], BF16)
    hT = p.sb("hT", [128, 8, TB], BF16)
    ss = p.sb("ss", [128, 12], F32)
    cs = p.sb("cs", [128, 2, nt, 8], F32)
    sq = p.sb("sq", [128, 1024], F32)
    qn = p.sb("qn", [128, 16, 64], F32)
    qb = p.sb("qb", [128, 16, 64], BF16)
    ssq = p.sb("ssq", [128, 16], F32)
    rs = p.sb("rs", [128, 16], F32)
    ra = p.sb("ra", [128, 16, 8], F32)
    rb = p.sb("rb", [128, 16, 8], F32)
    qkT = p.sb("qkT", [128, 8, TB], BF16)
    vb = p.sb("vb", [128, nt, 512], BF16)
    fb = p.sb("fb", [128, nt, 256], BF16)
    PS = make_psum(p)
    for b in range(S // TB):
        xb = xbs[b % 2]
        load_block(p, C, C.xres, b, TB, xb)
        p.dma("sp", cs[:, 0], C.cosA[:, b * nt:(b + 1) * nt, :])
        p.dma("sp", cs[:, 1], C.sinA[:, b * nt:(b + 1) * nt, :])
        norm_T(p, C, xb, nt, gcol, hT, xn, ss, PS[0:2])
        for j in range(nt):
            pq, pk, pv, pf = PS[2], PS[3], PS[4], PS[5]
            for (ps, c0, w) in ((pq, 0, 512), (pk, 512, 512), (pv, 1024, 512), (pf, 1536, 256)):
                for c in range(8):
                    p.mm(ps[:, 0:w], hT[:, c, j * 128:(j + 1) * 128], Win.k(c)[:, c, c0:c0 + w],
                         start=(c == 0), stop=(c == 7))
            p.copy(vb[:, j, :], pv, eng="act")
            p.copy(fb[:, j, :], pf[:, 0:256], eng="act")
            p.act(sq[:, 0:512], pq, AF.Square)
            p.act(sq[:, 512:1024], pk, AF.Square)
            p.reduce(ssq, sq.re("p (h e) -> p h e", e=64))
            rsqrt_mean(p, rs, ssq, 64, ssq)
            p.tt(qn[:, 0:8, :], pq.re("p (h e) -> p h e", e=64), rs[:, 0:8].ub(2, [128, 8, 64]), ALU.mult)
            p.tt(qn[:, 8:16, :], pk.re("p (h e) -> p h e", e=64), rs[:, 8:16].ub(2, [128, 8, 64]), ALU.mult)
            p.tt(qn, qn, gqk, ALU.mult)
            cosb = cs[:, 0, j, :].ub(1, [128, 16, 8])
            sinb = cs[:, 1, j, :].ub(1, [128, 16, 8])
            rot_pair(p, qb, qn, 0, 8, cosb, sinb, ra, rb)
            p.copy(qb[:, :, 16:64], qn[:, :, 16:64], eng="pool")
            pt = PS[6 + (j % 2)].bc(BF16)
            qbf = qb.re("p h e -> p (h e)")
            for c in range(8):
                p.tr(pt[:, c * 128:(c + 1) * 128], qbf[:, c * 128:(c + 1) * 128], C.ident_b)
            p.copy(qkT[:, :, j * 128:(j + 1) * 128], pt.re("p (c t) -> p c t", t=128), eng="act")
        sl = slice(b * TB, (b + 1) * TB)
        p.dma("sp", C.QT.k(b)[:, sl].re("(c q) t -> q c t", q=128), qkT[:, 0:4, :])
        p.dma("sp", C.KT.k(b)[:, sl].re("(c q) t -> q c t", q=128), qkT[:, 4:8, :])
        p.dma("sp", C.Vs.k(b)[sl, :].re("(j q) e -> q j e", q=128), vb)
        p.dma("sp", C.Fs.k(b)[sl, :].re("(j q) e -> q j e", q=128), fb)
    p.flush()


def attention_phase(p, C, S, nheads, dk, qsrc, ksrc, Vs, OT, masked):
    NB, NQ = S // 128, S // 512
    qt = [p.sb(f"qt{k}", [128, S], BF16) for k in range(2)]
    kt = [p.sb(f"kt{k}", [128, S], BF16) for k in range(2)]
    vx = [p.sb(f"vx{k}", [128, NB, 128], BF16) for k in range(2)]
    pex = [p.sb(f"pex{k}", [128, 512], BF16) for k in range(3)]
    den = p.sb("den", [128, 512], F32)
    rden = p.sb("rden", [64, 512], F32)
    ot = [p.sb(f"ot{k}", [64, 512], BF16) for k in range(2)]
    onesf = p.sb("onesf", [128, 64], F32)
    p.memset(onesf, 1.0)
    for k in range(2):
        p.memset(vx[k][:, :, 64:128], 1.0)
    if masked:
        mk = p.sb("mk", [128, 20, 512], BF16)
        mst = p.sb("mst", [128, 2, 512], F32)
        for o in range(20):
            p.dma("sp", mst.k(o % 2)[:, o % 2, :], C.dmask[o])
            p.copy(mk.k(o)[:, o, :], mst.k(o % 2)[:, o % 2, :], eng=("dve" if o % 2 else "pool"))
    PS = make_psum(p)
    for h in range(nheads):
        q_, k_, v_ = qt[h % 2], kt[h % 2], vx[h % 2]
        p.dma("sp", q_[0:dk, :], qsrc(h))
        p.dma("sp", k_[0:dk, :], ksrc(h))
        p.dma("sp", v_[:, :, 0:64], Vs[:, h * 64:(h + 1) * 64].re("(n q) e -> q n e", q=128))
        for qb in range(NQ):
            q0 = qb * 512
            if masked:
                kbs = [kb for kb in range(NB) if -1024 <= kb * 128 - q0 <= 1408]
            else:
                kbs = list(range(NB))
            po = PS[4 + (qb % 2)]
            for idx, kb in enumerate(kbs):
                ps = PS[idx % 3]
                pe_ = pex[idx % 3]
                p.mm(ps, k_[0:dk, kb * 128:(kb + 1) * 128], q_[0:dk, q0:q0 + 512])
                p.act(pe_, ps, AF.Exp)
                if masked:
                    o = (kb * 128 - q0 + 1024) // 128
                    p.tt(pe_, pe_, mk.k(o)[:, o, :], ALU.mult)
                p.mm(po, v_[:, kb, :], pe_, start=(idx == 0), stop=(idx == len(kbs) - 1))
            p.copy(den[64:65, :], po[64:65, :], eng="act")
            pb = PS[6 + (qb % 2)]
            p.mm(pb[0:64, :], onesf[64:65, 0:64], den[64:65, :])
            p.recip(rden, pb[0:64, :])
            o_ = ot[qb % 2]
            p.tt(o_, po[0:64, :], rden, ALU.mult)
            p.dma("sp", OT.k((h, qb))[h * 64:(h + 1) * 64, q0:q0 + 512], o_)
    p.flush()


def fnet_phase(p, C, S):
    S2 = S // 128
    ya = p.sb("ya", [128, S2, 256], BF16)
    p.dma("sp", ya.re("p a c -> p (a c)"), C.Fs.re("(p a) c -> p (a c)", p=128))
    w1f = p.sb("w1f", [128, 256], F32)
    w1 = p.sb("w1", [128, 256], BF16)
    p.dma("sp", w1f, C.fW1)
    p.copy(w1, w1f)
    tw = p.sb("tw", [S2, 2, 128], F32)
    p.dma("sp", tw, C.fT)
    w3f = p.sb("w3f", [S2, 2, 2 * S2], F32)
    w3 = p.sb("w3", [S2, 2, 2 * S2], BF16)
    p.dma("sp", w3f, C.fW3)
    p.copy(w3, w3f)
    PS = make_psum(p)
    ta = [p.sb(f"ta{k}", [S2, 2, 128], F32) for k in range(4)]
    z2r = p.sb("z2r", [S2, 128, 128], BF16)
    z2i = p.sb("z2i", [S2, 128, 128], BF16)
    zt = p.sb("zt", [128, 2, S2, 128], BF16)
    for half in range(2):
        for cp in range(64):
            ps = PS[cp % 2]
            for c2 in range(2):
                ch = half * 128 + cp * 2 + c2
                p.mm(ps[0:S2, c2 * 256:(c2 + 1) * 256], ya[:, :, ch], w1)
            pv = ps[0:S2, :].re("p (c x k) -> p c x k", c=2, x=2)
            zr, zi = pv[:, :, 0, :], pv[:, :, 1, :]
            tc = tw[:, 0, :].ub(1, [S2, 2, 128])
            tsn = tw[:, 1, :].ub(1, [S2, 2, 128])
            p.tt(ta[0], zr, tc, ALU.mult)
            p.tt(ta[1], zi, tsn, ALU.mult)
            p.tt(ta[2], zi, tc, ALU.mult)
            p.tt(ta[3], zr, tsn, ALU.mult)
            cl = cp * 2
            p.tt(z2r[:, :, cl:cl + 2].re("p k c -> p c k"), ta[0], ta[1], ALU.add, eng="pool")
            p.tt(z2i[:, :, cl:cl + 2].re("p k c -> p c k"), ta[2], ta[3], ALU.subtract, eng="pool")
        for kg in range(32):
            ps = PS[2 + (kg % 2)]
            for kk in range(4):
                k1 = kg * 4 + kk
                o = ps[:, kk * 128:kk * 128 + 2 * S2]
                p.mm(o, z2r[:, k1, :], w3[:, 0, :], start=True, stop=False)
                p.mm(o, z2i[:, k1, :], w3[:, 1, :], start=False, stop=True)
            src = ps.re("p (q x) -> p q x", x=128)[:, :, 0:2 * S2]
            dst = zt[:, :, :, kg * 4:(kg + 1) * 4].re("p r k q -> p q (r k)")
            p.copy(dst, src, eng=("act" if kg % 2 else "dve"))
        p.dma("sp", C.ZrT[half * 128:(half + 1) * 128, :], zt[:, 0].re("p k q -> p (k q)"))
        p.dma("sp", C.ZiT[half * 128:(half + 1) * 128, :], zt[:, 1].re("p k q -> p (k q)"))
    p.flush()


def outproj_phase(p, C, S, wout, srcs, fold=None):
    TB, nt = 512, 4
    stage = p.sb("stage", [128, 2, 1024], F32)
    chunks = []
    for (t, n) in srcs:
        for c in range(n):
            chunks.append((t, c))
    NCH = len(chunks)
    W = p.sb("W", [128, NCH, 1024], BF16)
    PS = make_psum(p)
    if fold is None:
        load_w_bf16(p, W, wout, NCH * 128, 1024, stage, 1024)
    else:
        i = fold
        wtmp = p.sb("wtmp", [128, 6, 1024], BF16)
        load_w_bf16(p, wtmp, wout, 768, 1024, stage, 1024)
        for c in range(4):
            p.copy(W.k(c)[:, c, :], wtmp.k(c)[:, c, :])
        cf = p.sb("cf", [128, 4, 128], F32)
        cb = p.sb("cb", [128, 4, 128], BF16)
        p.dma("sp", cf, C.fBD[i])
        p.copy(cb, cf)
        m1 = p.sb("m1", [128, 2, 1024], BF16)
        for pr in range(2):
            for dh in range(2):
                ps = PS[dh]
                p.mm(ps, cb[:, pr, :], wtmp.k(4 + pr)[:, 4 + pr, dh * 512:(dh + 1) * 512])
                p.copy(m1[:, pr, dh * 512:(dh + 1) * 512], ps)
        for ri in range(2):
            for pr in range(2):
                for dh in range(2):
                    ps = PS[2 + dh]
                    p.mm(ps, cb[:, 2 + ri, :], m1[:, pr, dh * 512:(dh + 1) * 512])
                    c = 4 + ri * 2 + pr
                    p.copy(W.k(c)[:, c, dh * 512:(dh + 1) * 512], ps)
    xbs = [p.sb(f"xb{k}", [128, nt, D], F32) for k in range(2)]
    mT = [p.sb(f"mT{k}", [128, NCH, TB], BF16) for k in range(2)]
    for b in range(S // TB):
        xb, m = xbs[b % 2], mT[b % 2]
        load_block(p, C, C.xres, b, TB, xb)
        for ci, (t, c) in enumerate(chunks):
            p.dma("sp", m.k(ci)[:, ci, :], t[c * 128:(c + 1) * 128, b * TB:(b + 1) * TB])
        for j in range(nt):
            for dh in range(2):
                ps = PS[4 + ((j * 2 + dh) % 4)]
                for ci in range(NCH):
                    p.mm(ps, m.k(ci)[:, ci, j * 128:(j + 1) * 128], W.k(ci)[:, ci, dh * 512:(dh + 1) * 512],
                         start=(ci == 0), stop=(ci == NCH - 1))
                p.tt(xb[:, j, dh * 512:(dh + 1) * 512], ps, xb[:, j, dh * 512:(dh + 1) * 512], ALU.add)
        p.dma("sp", C.xres.k(("r", b))[b * TB:(b + 1) * TB, :].re("(j q) d -> q j d", q=128), xb)
    p.flush()


def odd_proj_phase(p, C, i, gcol, S):
    TB, nt = 512, 4
    Win = p.sb("Win", [128, 8, 672], BF16)
    Wq = p.sb("Wq", [128, 2, 768], BF16)
    Wkv = p.sb("Wkv", [128, 1, 1024], BF16)
    stage = p.sb("stage", [128, 2, 1024], F32)
    load_w_bf16(p, Win, C.cd_w_in[i], D, 672, stage, 672)
    load_w_bf16(p, Wq, C.c_w_q_up[i], 256, 768, stage, 768)
    load_w_bf16(p, Wkv, C.c_w_kv_up[i], 128, 1024, stage, 1024)
    glat = p.sb("glat", [128, 384], F32)
    p.dma("sp", glat, C.glat[i])
    gq = p.sb("gq", [128, 8, 96], F32)
    gk = p.sb("gk", [128, 8, 96], F32)
    p.dma("sp", gq, C.gq96[i])
    p.dma("sp", gk, C.gk96[i])
    p.ts(gq, gq, 96.0 ** -0.5, None, ALU.mult)
    xbs = [p.sb(f"xb{k}", [128, nt, D], F32) for k in range(2)]
    xn = p.sb("xn", [128, nt, D], BF16)
    hT = p.sb("hT", [128, 8, TB], BF16)
    ss = p.sb("ss", [128, 12], F32)
    cs = p.sb("cs", [128, 2, nt, 16], F32)
    junk = p.sb("junk", [128, 1024], F32)
    ssl = p.sb("ssl", [128, 4], F32)
    rl = p.sb("rl", [128, 2], F32)
    qln = p.sb("qln", [128, 384], F32)
    lnb = p.sb("lnb", [128, 384], BF16)
    lnT = p.sb("lnT", [128, 3, 128], BF16)
    kpe = p.sb("kpe", [128, 32], F32)
    kg = p.sb("kg", [128, 1, 32], F32)
    R = p.sb("R", [128, 1, 32], F32)
    qf = p.sb("qf", [128, 8, 96], F32)
    kvf = p.sb("kvf", [128, 8, 128], F32)
    s8 = p.sb("s8", [128, 8], F32)
    rq = p.sb("rq", [128, 8], F32)
    rk = p.sb("rk", [128, 8], F32)
    sp1 = p.sb("sp1", [128, 1], F32)
    ra = p.sb("ra", [128, 8, 16], F32)
    rb = p.sb("rb", [128, 8, 16], F32)
    tk = p.sb("tk", [128, 8, 64], F32)
    qb3 = p.sb("qb3", [128, 8, 96], BF16)
    kb3 = p.sb("kb3", [128, 8, 96], BF16)
    qTb = p.sb("qTb", [128, 8, TB], BF16)
    kTb = p.sb("kTb", [128, 8, TB], BF16)
    vb = p.sb("vb", [128, nt, 512], BF16)
    uTb = p.sb("uTb", [128, 2, TB], BF16)
    PS = make_psum(p)
    for b in range(S // TB):
        xb = xbs[b % 2]
        load_block(p, C, C.xres, b, TB, xb)
        p.dma("sp", cs[:, 0], C.cosC[:, b * nt:(b + 1) * nt, :])
        p.dma("sp", cs[:, 1], C.sinC[:, b * nt:(b + 1) * nt, :])
        norm_T(p, C, xb, nt, gcol, hT, xn, ss, PS[0:2])
        for m in range(2):
            pu = PS[3]
            for c in range(8):
                p.mm(pu, Win.k(c)[:, c, 416 + m * 128:416 + (m + 1) * 128], hT[:, c, :], start=(c == 0), stop=(c == 7))
            p.copy(uTb[:, m, :], pu, eng="act")
        for j in range(nt):
            pl = PS[2]
            for c in range(8):
                p.mm(pl[:, 0:416], hT[:, c, j * 128:(j + 1) * 128], Win.k(c)[:, c, 0:416], start=(c == 0), stop=(c == 7))
            p.act(junk[:, 0:256], pl[:, 0:256], AF.Square, accum=ssl[:, 0:1])
            p.act(junk[:, 256:384], pl[:, 256:384], AF.Square, accum=ssl[:, 1:2])
            p.ts(ssl[:, 2:3], ssl[:, 0:1], 1.0 / 256, EPS, ALU.mult, ALU.add)
            p.ts(ssl[:, 3:4], ssl[:, 1:2], 1.0 / 128, EPS, ALU.mult, ALU.add)
            p.act(ssl[:, 2:4], ssl[:, 2:4], AF.Sqrt)
            p.recip(rl, ssl[:, 2:4])
            p.ts(qln[:, 0:256], pl[:, 0:256], rl[:, 0:1], None, ALU.mult)
            p.ts(qln[:, 256:384], pl[:, 256:384], rl[:, 1:2], None, ALU.mult)
            p.copy(kpe, pl[:, 384:416], eng="act")
            p.tt(lnb, qln, glat, ALU.mult)
            pt = PS[0].bc(BF16)
            for c in range(3):
                p.tr(pt[:, c * 128:(c + 1) * 128], lnb[:, c * 128:(c + 1) * 128], C.ident_b)
            p.copy(lnT, pt[:, 0:384].re("p (c t) -> p c t", t=128), eng="act")
            pq0, pq1, pk0, pk1 = PS[4], PS[5], PS[6], PS[7]
            for c2 in range(2):
                p.mm(pq0, lnT[:, c2, :], Wq.k(c2)[:, c2, 0:512], start=(c2 == 0), stop=(c2 == 1))
            for c2 in range(2):
                p.mm(pq1[:, 0:256], lnT[:, c2, :], Wq.k(c2)[:, c2, 512:768], start=(c2 == 0), stop=(c2 == 1))
            p.mm(pk0, lnT[:, 2, :], Wkv.k(0)[:, 0, 0:512])
            p.mm(pk1, lnT[:, 2, :], Wkv.k(0)[:, 0, 512:1024])
            qff = qf.re("p h e -> p (h e)")
            kvff = kvf.re("p h e -> p (h e)")
            p.copy(qff[:, 0:512], pq0, eng="act")
            p.copy(qff[:, 512:768], pq1[:, 0:256], eng="act")
            p.copy(kvff[:, 0:512], pk0, eng="act")
            p.copy(kvff[:, 512:1024], pk1, eng="act")
            p.act(junk[:, 0:768], qff, AF.Square)
            p.reduce(s8, junk[:, 0:768].re("p (h e) -> p h e", e=96))
            rsqrt_mean(p, rq, s8, 96, s8)
            p.tt(qf, qf, rq.ub(2, [128, 8, 96]), ALU.mult)
            p.tt(qf, qf, gq, ALU.mult)
            cosb = cs[:, 0, j, :].ub(1, [128, 8, 16])
            sinb = cs[:, 1, j, :].ub(1, [128, 8, 16])
            rot_pair(p, qb3, qf, 64, 16, cosb, sinb, ra, rb)
            p.copy(qb3[:, :, 0:64], qf[:, :, 0:64], eng="pool")
            p.act(junk, kvff, AF.Square)
            p.reduce(s8, junk.re("p (h e) -> p h e", e=128)[:, :, 0:64])
            p.act(junk[:, 0:32], kpe, AF.Square, accum=sp1)
            p.ts(s8, s8, sp1[:, 0:1], None, ALU.add)
            rsqrt_mean(p, rk, s8, 96, s8)
            p.tt(tk, kvf[:, :, 0:64], rk.ub(2, [128, 8, 64]), ALU.mult)
            p.tt(kb3[:, :, 0:64], tk, gk[:, :, 0:64], ALU.mult)
            p.tt(kg[:, 0, :], kpe, gk[:, 0, 64:96], ALU.mult)
            rot_pair(p, R, kg, 0, 16, cs[:, 0, j, :].ub(1, [128, 1, 16]), cs[:, 1, j, :].ub(1, [128, 1, 16]),
                     ra[:, 0:1, :], rb[:, 0:1, :])
            p.tt(kb3[:, :, 64:96], R[:, 0, :].ub(1, [128, 8, 32]), rk.ub(2, [128, 8, 32]), ALU.mult)
            p.copy(vb[:, j, :].re("p (h e) -> p h e", e=64), kvf[:, :, 64:128], eng="pool")
            ptq = PS[0].bc(BF16)
            ptk = PS[1].bc(BF16)
            for h in range(8):
                p.tr(ptq[0:96, h * 128:(h + 1) * 128], qb3[:, h, :], C.ident_b)
            for h in range(8):
                p.tr(ptk[0:96, h * 128:(h + 1) * 128], kb3[:, h, :], C.ident_b)
            p.copy(qTb[0:96, :, j * 128:(j + 1) * 128], ptq[0:96, :].re("p (h t) -> p h t", t=128), eng="act")
            p.copy(kTb[0:96, :, j * 128:(j + 1) * 128], ptk[0:96, :].re("p (h t) -> p h t", t=128), eng="act")
        sl = slice(b * TB, (b + 1) * TB)
        p.dma("sp", C.QTc.k(b)[:, :, sl].re("h e t -> e h t"), qTb[0:96])
        p.dma("sp", C.KTc.k(b)[:, :, sl].re("h e t -> e h t"), kTb[0:96])
        p.dma("sp", C.Vs.k(b)[sl, :].re("(j q) e -> q j e", q=128), vb)
        p.dma("sp", C.UT.k(b)[:, sl].re("(m q) t -> q m t", q=128), uTb)
    p.flush()


def s5_phase(p, C, i, S):
    TB = 512
    NBk = S // TB
    names = ["lre", "lim", "lst", "step", "rho", "th", "cth", "sth", "lbr", "lbi", "nr", "den", "kr", "ki", "t1", "t2", "t3"]
    T = {n: p.sb("s5_" + n, [128, 16], F32) for n in names}
    tiI = p.sb("s5_ti", [128, 16], I32)
    p.dma("sp", T["lre"], C.s5lre[i])
    p.dma("sp", T["lim"], C.s5lim[i])
    p.dma("sp", T["lst"], C.s5lst[i])
    p.act(T["step"], T["lst"], AF.Exp)
    p.tt(T["t1"], T["lre"], T["step"], ALU.mult)
    p.act(T["rho"], T["t1"], AF.Exp)
    p.tt(T["th"], T["lim"], T["step"], ALU.mult)
    thr = p.sb("s5_thr", [128, 16], F32)
    p.ts(T["t1"], T["th"], 1.0 / TWO_PI, None, ALU.mult)
    p.copy(tiI, T["t1"])
    p.copy(T["t1"], tiI)
    p.stt(thr, T["t1"], -CW1, T["th"], ALU.mult, ALU.add)
    p.stt(thr, T["t1"], -CW2, thr, ALU.mult, ALU.add)
    sincos(p, T["sth"], T["cth"], thr, T["t1"], T["t2"], tiI)
    p.tt(T["lbr"], T["rho"], T["cth"], ALU.mult)
    p.tt(T["lbi"], T["rho"], T["sth"], ALU.mult)
    p.ts(T["nr"], T["lbr"], -1.0, None, ALU.add)
    p.tt(T["t1"], T["lre"], T["lre"], ALU.mult)
    p.tt(T["t2"], T["lim"], T["lim"], ALU.mult)
    p.tt(T["den"], T["t1"], T["t2"], ALU.add)
    p.recip(T["den"], T["den"])
    p.tt(T["t1"], T["nr"], T["lre"], ALU.mult)
    p.tt(T["t2"], T["lbi"], T["lim"], ALU.mult)
    p.tt(T["t1"], T["t1"], T["t2"], ALU.add)
    p.tt(T["kr"], T["t1"], T["den"], ALU.mult)
    p.tt(T["t1"], T["lbi"], T["lre"], ALU.mult)
    p.tt(T["t2"], T["nr"], T["lim"], ALU.mult)
    p.tt(T["t1"], T["t1"], T["t2"], ALU.subtract)
    p.tt(T["ki"], T["t1"], T["den"], ALU.mult)
    bre = p.sb("s5_bre", [128, 16, 16], F32)
    bim = p.sb("s5_bim", [128, 16, 16], F32)
    bbr = p.sb("s5_bbr", [128, 16, 16], F32)
    bbi = p.sb("s5_bbi", [128, 16, 16], F32)
    bt = p.sb("s5_bt", [128, 16, 16], F32)
    p.dma("sp", bre, C.s5bre[i])
    p.dma("sp", bim, C.s5bim[i])
    krb, kib = T["kr"].ub(2, [128, 16, 16]), T["ki"].ub(2, [128, 16, 16])
    p.tt(bbr, bre, krb, ALU.mult)
    p.tt(bt, bim, kib, ALU.mult)
    p.tt(bbr, bbr, bt, ALU.subtract)
    p.tt(bbi, bim, krb, ALU.mult)
    p.tt(bt, bre, kib, ALU.mult)
    p.tt(bbi, bbi, bt, ALU.add)
    PS = make_psum(p)
    LB = p.sb("s5_LB", [128, 16, 2, 128], BF16)
    blk = p.sb("s5_blk", [128, 2, 128], F32)
    for dg in range(16):
        gq_ = dg % 4
        p.memset(blk, 0.0)
        for ri, src in enumerate((bbr, bbi)):
            for g2 in range(2):
                c0 = (gq_ * 2 + g2) * 16
                p.copy(blk[g2 * 64:(g2 + 1) * 64, ri, c0:c0 + 16], src[g2 * 64:(g2 + 1) * 64, dg, :])
        for ri in range(2):
            ps = PS[ri]
            p.tr(ps[:, 0:128], blk[:, ri, :], C.ident_f)
            p.copy(LB[:, dg, ri, :], ps[:, 0:128])
    CT = p.sb("s5_CT", [128, 16, 2, 128], BF16)
    cst = p.sb("s5_cst", [128, 16, 128], F32)
    p.dma("sp", cst, C.s5ctr[i])
    p.copy(CT[:, :, 0, :], cst)
    p.dma("sp", cst, C.s5cti[i])
    p.ts(CT[:, :, 1, :], cst, -1.0, None, ALU.mult)
    NJ = 513
    iot = p.sb("s5_iot", [128, NJ], F32)
    p.dma("sp", iot, C.iota)
    RC = p.sb("s5_RC", [128, 16, NJ], F32)
    RS = p.sb("s5_RS", [128, 16, NJ], F32)
    a1 = p.sb("s5_a1", [128, NJ], F32)
    a2 = p.sb("s5_a2", [128, NJ], F32)
    a3 = p.sb("s5_a3", [128, NJ], F32)
    ai = p.sb("s5_ai", [128, NJ], I32)
    for dg in range(16):
        p.ts(a1, iot, thr[:, dg:dg + 1], None, ALU.mult)
        sincos(p, RS[:, dg, :], RC[:, dg, :], a1, a2, a3, ai)
    dsk = p.sb("s5_dsk", [128, 2], F32)
    bgl = p.sb("s5_bgl", [128, 2], F32)
    p.dma("sp", dsk, C.s5dsk[i])
    p.dma("sp", bgl, C.s5bgl[i])
    Wgl = p.sb("s5_Wgl", [128, 2, 256], BF16)
    stage = p.sb("stage", [128, 2, 256], F32)
    load_w_bf16(p, Wgl, C.d_w_glu[i], 256, 256, stage, 256)
    uT = p.sb("s5_uT", [128, 2, S], BF16)
    p.dma("sp", uT, C.UT.re("(m q) t -> q m t", q=128))
    carry = p.sb("s5_carry", [128, 16, 2], F32)
    cnew = p.sb("s5_cnew", [128, 4], F32)
    er = p.sb("s5_er", [128, TB], F32)
    ei = p.sb("s5_ei", [128, TB], F32)
    wr = p.sb("s5_wr", [128, TB], F32)
    wi = p.sb("s5_wi", [128, TB], F32)
    m1 = p.sb("s5_m1", [128, TB], F32)
    m2 = p.sb("s5_m2", [128, TB], F32)
    n1 = p.sb("s5_n1", [128, TB], F32)
    n2 = p.sb("s5_n2", [128, TB], F32)
    xr = [p.sb(f"s5_xr{k}", [128, TB], BF16) for k in range(2)]
    xi = [p.sb(f"s5_xi{k}", [128, TB], BF16) for k in range(2)]
    yf = p.sb("s5_yf", [128, 2, TB], F32)
    yv = p.sb("s5_yv", [128, 2, TB], F32)
    zb = p.sb("s5_zb", [128, 2, TB], BF16)
    zf = p.sb("s5_zf", [128, 2, TB], F32)
    g1 = p.sb("s5_g1", [128, TB], F32)
    g2t = p.sb("s5_g2", [128, TB], F32)
    dob = p.sb("s5_dob", [128, 2, TB], BF16)
    for d in range(2):
        order = range(NBk) if d == 0 else range(NBk - 1, -1, -1)
        for bi, b in enumerate(order):
            t0 = b * TB
            usl = (lambda m: uT[:, m, t0:t0 + TB]) if d == 0 else (lambda m: uT[:, m, t0:t0 + TB][:, ::-1])
            py = [PS[4], PS[5]]
            cnt = [0, 0]
            for gp in range(8):
                dg = d * 8 + gp
                m = gp // 4
                pr_, pi_ = PS[0 + (gp % 2) * 2], PS[1 + (gp % 2) * 2]
                p.mm(pr_, LB[:, dg, 0, :], usl(m))
                p.mm(pi_, LB[:, dg, 1, :], usl(m))
                cb_, sb_ = RC[:, dg, 0:TB], RS[:, dg, 0:TB]
                p.tt(m1, pr_, cb_, ALU.mult)
                p.tt(m2, pi_, sb_, ALU.mult)
                p.tt(er, m1, m2, ALU.add)
                p.tt(m1, pi_, cb_, ALU.mult)
                p.tt(m2, pr_, sb_, ALU.mult)
                p.tt(ei, m1, m2, ALU.subtract)
                rho_b = T["rho"][:, dg:dg + 1].bcast([128, TB])
                for (w_, e_, ci) in ((wr, er, 0), (wi, ei, 1)):
                    init = 0.0 if bi == 0 else carry[:, dg, ci:ci + 1]
                    oo, a_, b_ = w_.ap, rho_b.ap, e_.ap
                    ini = init if bi == 0 else init.ap
                    rd = [T["rho"], e_] + ([] if bi == 0 else [carry])
                    p.add("dve", (lambda e, oo=oo, a_=a_, b_=b_, ini=ini: e.tensor_tensor_scan(oo, a_, b_, ini, ALU.mult, ALU.add)),
                          rd, [w_])
                c5, s5 = RC[:, dg, TB:TB + 1], RS[:, dg, TB:TB + 1]
                p.ts(cnew[:, 0:1], wr[:, TB - 1:TB], c5, None, ALU.mult)
                p.ts(cnew[:, 1:2], wi[:, TB - 1:TB], s5, None, ALU.mult)
                p.ts(cnew[:, 2:3], wr[:, TB - 1:TB], s5, None, ALU.mult)
                p.ts(cnew[:, 3:4], wi[:, TB - 1:TB], c5, None, ALU.mult)
                p.tt(carry[:, dg, 0:1], cnew[:, 0:1], cnew[:, 1:2], ALU.subtract)
                p.tt(carry[:, dg, 1:2], cnew[:, 2:3], cnew[:, 3:4], ALU.add)
                xr_, xi_ = xr[gp % 2], xi[gp % 2]
                p.tt(n1, wr, cb_, ALU.mult, eng="pool")
                p.tt(n2, wi, sb_, ALU.mult, eng="pool")
                p.tt(xr_, n1, n2, ALU.subtract, eng="pool")
                p.tt(n1, wi, cb_, ALU.mult, eng="pool")
                p.tt(n2, wr, sb_, ALU.mult, eng="pool")
                p.tt(xi_, n1, n2, ALU.add, eng="pool")
                xr_o = xr_ if d == 0 else xr_[:, ::-1]
                xi_o = xi_ if d == 0 else xi_[:, ::-1]
                p.mm(py[m], CT[:, dg, 0, :], xr_o, start=(cnt[m] == 0), stop=False)
                p.mm(py[m], CT[:, dg, 1, :], xi_o, start=False, stop=(cnt[m] == 3))
                cnt[m] += 1
            if d == 0:
                for m in range(2):
                    p.copy(yf[:, m, :], py[m], eng="act")
                p.dma("sp", C.YF.k(b)[:, t0:t0 + TB].re("(m q) t -> q m t", q=128), yf)
            else:
                p.dma("sp", yf, C.YF.k(b)[:, t0:t0 + TB].re("(m q) t -> q m t", q=128))
                for m in range(2):
                    p.tt(yv[:, m, :], py[m], yf[:, m, :], ALU.add)
                    p.stt(yv[:, m, :], uT[:, m, t0:t0 + TB], dsk[:, m:m + 1], yv[:, m, :], ALU.mult, ALU.add)
                    p.tt(g1, yv[:, m, :], yv[:, m, :], ALU.mult)
                    p.ts(g1, g1, 0.044715, 1.0, ALU.mult, ALU.add)
                    p.tt(g1, g1, yv[:, m, :], ALU.mult)
                    p.act(g2t, g1, AF.Sigmoid, scale=1.5957691216057308)
                    p.tt(zf[:, m, :], yv[:, m, :], g2t, ALU.mult)
                    p.copy(zb[:, m, :], zf[:, m, :], eng="pool")
                for oc in range(2):
                    pg = PS[6 + oc]
                    for c in range(2):
                        p.mm(pg, Wgl.k(c)[:, c, oc * 128:(oc + 1) * 128], zb[:, c, :], start=(c == 0), stop=(c == 1))
                    p.act(g2t, pg, AF.Sigmoid, bias=bgl[:, oc:oc + 1])
                    p.tt(dob[:, oc, :], zf[:, oc, :], g2t, ALU.mult)
                p.dma("sp", C.DoT.k(b)[:, t0:t0 + TB].re("(m q) t -> q m t", q=128), dob)
    p.flush()


NE, NO = 2, 2
W_SHAPES = {
    "ffn1_w_gate": (4, 1024, 2816), "ffn1_w_up": (4, 1024, 2816), "ffn1_w_down": (4, 2816, 1024),
    "ffn2_w_gate": (4, 1024, 2816), "ffn2_w_up": (4, 1024, 2816), "ffn2_w_down": (4, 2816, 1024),
    "ab_w_in": (2, 1024, 1792), "ab_w_out": (2, 768, 1024),
    "cd_w_in": (2, 1024, 672), "cd_w_out": (2, 768, 1024),
    "c_w_q_up": (2, 256, 768), "c_w_kv_up": (2, 128, 1024), "d_w_glu": (2, 256, 256),
}


def aux_shapes(S):
    NT, S2 = S // 128, S // 128
    return {
        "ident": ((128, 128), "f"), "gcols": ((128, 12, 8), "f"), "pos": ((128, NT), "i"),
        "gqk": ((2, 128, 16, 64), "f"), "fBD": ((2, 128, 4, 128), "f"),
        "glat": ((2, 128, 384), "f"), "gq96": ((2, 128, 8, 96), "f"), "gk96": ((2, 128, 8, 96), "f"),
        "s5lre": ((2, 128, 16), "f"), "s5lim": ((2, 128, 16), "f"), "s5lst": ((2, 128, 16), "f"),
        "s5bre": ((2, 128, 16, 16), "f"), "s5bim": ((2, 128, 16, 16), "f"),
        "s5ctr": ((2, 128, 16, 128), "f"), "s5cti": ((2, 128, 16, 128), "f"),
        "s5dsk": ((2, 128, 2), "f"), "s5bgl": ((2, 128, 2), "f"),
        "iota": ((128, 513), "f"), "dmask": ((20, 128, 512), "f"),
        "fW1": ((128, 256), "f"), "fT": ((S2, 2, 128), "f"), "fW3": ((S2, 2, 2 * S2), "f"),
    }


def build_program(S, layers=(0, 1, 2, 3), stop_after=None):
    nc = bass.Bass("TRN2", target_bir_lowering=False)
    p = Prog(nc)
    C = Ctx()
    NT = S // 128
    xin = p.dram("x", [S, D], F32, kind="ExternalInput")
    for n, shp in W_SHAPES.items():
        setattr(C, n, p.dram(n, list(shp), F32, kind="ExternalInput"))
    for n, (shp, ty) in aux_shapes(S).items():
        setattr(C, n, p.dram(n, list(shp), F32 if ty == "f" else I32, kind="ExternalInput"))
    out = p.dram("out", [S, D], F32, kind="ExternalOutput")
    C.xres = out
    C.cosA = p.dram("cosA", [128, NT, 8], F32)
    C.sinA = p.dram("sinA", [128, NT, 8], F32)
    C.cosC = p.dram("cosC", [128, NT, 16], F32)
    C.sinC = p.dram("sinC", [128, NT, 16], F32)
    C.QT = p.dram("QT", [512, S], BF16)
    C.KT = p.dram("KT", [512, S], BF16)
    C.Vs = p.dram("Vs", [S, 512], BF16)
    C.Fs = p.dram("Fs", [S, 256], BF16)
    C.AoT = p.dram("AoT", [512, S], BF16)
    C.ZrT = p.dram("ZrT", [256, S], BF16)
    C.ZiT = p.dram("ZiT", [256, S], BF16)
    C.QTc = p.dram("QTc", [8, 96, S], BF16)
    C.KTc = p.dram("KTc", [8, 96, S], BF16)
    C.UT = p.dram("UT", [256, S], BF16)
    C.YF = p.dram("YF", [256, S], F32)
    C.DoT = p.dram("DoT", [256, S], BF16)
    idf = p.sb("idf", [128, 128], F32, glob=True)
    C.ident_f = idf
    C.ident_b = p.sb("idb", [128, 128], BF16, glob=True)
    gcols = p.sb("gcols", [128, 12, 8], F32, glob=True)
    p.dma("sp", idf, C.ident)
    p.copy(C.ident_b, idf)
    p.dma("sp", gcols, C.gcols)
    posi = p.sb("posi", [128, NT], I32)
    posf = p.sb("posf", [128, NT], F32)
    p.dma("sp", posi, C.pos)
    p.copy(posf, posi)
    for (nf, rot, dc, ds) in ((8, 16, C.cosA, C.sinA), (16, 32, C.cosC, C.sinC)):
        ang = p.sb(f"ang{nf}", [128, NT, nf], F32)
        t1 = p.sb(f"rt1{nf}", [128, NT, nf], F32)
        t2 = p.sb(f"rt2{nf}", [128, NT, nf], F32)
        ti = p.sb(f"rti{nf}", [128, NT, nf], I32)
        so = p.sb(f"rso{nf}", [128, NT, nf], F32)
        co = p.sb(f"rco{nf}", [128, NT, nf], F32)
        invf = (500000.0 ** (-np.arange(0, rot, 2, dtype=np.float32) / np.float32(rot))).astype(np.float32)
        for f in range(nf):
            p.ts(ang[:, :, f], posf, float(invf[f]), None, ALU.mult)
        sincos(p, so, co, ang, t1, t2, ti)
        p.dma("sp", dc, co)
        p.dma("sp", ds, so)
    p.flush()
    first = True
    for L in layers:
        i = L // 2
        src = xin if first else out
        first = False
        ffn_phase(p, C, src, out, C.ffn1_w_gate[L], C.ffn1_w_up[L], C.ffn1_w_down[L], gcols[:, L * 3 + 0, :], S)
        if stop_after == (L, 0):
            break
        gmix = gcols[:, L * 3 + 1, :]
        if L % 2 == 0:
            even_proj_phase(p, C, i, gmix, S)
            attention_phase(p, C, S, 8, 64, lambda h: C.QT[h * 64:(h + 1) * 64, :], lambda h: C.KT[h * 64:(h + 1) * 64, :],
                            C.Vs, C.AoT, True)
            fnet_phase(p, C, S)
            outproj_phase(p, C, S, C.ab_w_out[i], [(C.AoT, 4), (C.ZrT, 2), (C.ZiT, 2)], fold=i)
        else:
            odd_proj_phase(p, C, i, gmix, S)
            attention_phase(p, C, S, 8, 96, lambda h: C.QTc[h], lambda h: C.KTc[h], C.Vs, C.AoT, False)
            s5_phase(p, C, i, S)
            outproj_phase(p, C, S, C.cd_w_out[i], [(C.AoT, 4), (C.DoT, 2)])
        if stop_after == (L, 1):
            break
        ffn_phase(p, C, out, out, C.ffn2_w_gate[L], C.ffn2_w_up[L], C.ffn2_w_down[L], gcols[:, L * 3 + 2, :], S)
    p.finish()
    return nc, p


def host_aux(inp, S, positions_row):
    f32 = np.float32
    NT = S // 128
    S2 = S // 128
    a = {}
    a["ident"] = np.eye(128, dtype=f32)
    g = np.zeros((128, 12, 8), f32)
    for L in range(4):
        for n, nm in enumerate(("ffn1_norm", "mix_norm", "ffn2_norm")):
            g[:, L * 3 + n, :] = np.asarray(inp[nm][L], f32).reshape(8, 128).T
    a["gcols"] = g
    a["pos"] = np.ascontiguousarray(np.asarray(positions_row, np.int32).reshape(NT, 128).T)
    gqk = np.zeros((2, 128, 16, 64), f32)
    fBD = np.zeros((2, 128, 4, 128), f32)
    cidx = np.arange(64)
    Cc = np.cos(2 * np.pi * np.outer(cidx, cidx) / 64).astype(f32)
    Sc = np.sin(2 * np.pi * np.outer(cidx, cidx) / 64).astype(f32)
    for i in range(2):
        gqk[i, :, 0:8, :] = np.asarray(inp["a_q_norm"][i], f32)[None, None, :]
        gqk[i, :, 8:16, :] = np.asarray(inp["a_k_norm"][i], f32)[None, None, :]
        bm = np.asarray(inp["b_w_mix"][i], f32)
        for pr in range(2):
            for g2 in range(2):
                fBD[i, g2 * 64:(g2 + 1) * 64, pr, g2 * 64:(g2 + 1) * 64] = bm[pr * 2 + g2].T
        for g2 in range(2):
            fBD[i, g2 * 64:(g2 + 1) * 64, 2, g2 * 64:(g2 + 1) * 64] = Cc
            fBD[i, g2 * 64:(g2 + 1) * 64, 3, g2 * 64:(g2 + 1) * 64] = Sc
    a["gqk"], a["fBD"] = gqk, fBD
    glat = np.zeros((2, 128, 384), f32)
    gq96 = np.zeros((2, 128, 8, 96), f32)
    gk96 = np.zeros((2, 128, 8, 96), f32)
    for i in range(2):
        glat[i, :, 0:256] = np.asarray(inp["c_q_lat_norm"][i], f32)[None, :]
        glat[i, :, 256:384] = np.asarray(inp["c_kv_lat_norm"][i], f32)[None, :]
        gq96[i] = np.asarray(inp["c_q_norm"][i], f32)[None, None, :]
        gk96[i] = np.asarray(inp["c_k_norm"][i], f32)[None, None, :]
    a["glat"], a["gq96"], a["gk96"] = glat, gq96, gk96

    def sl(arr):
        return np.ascontiguousarray(np.asarray(arr, f32).reshape(2, 8, 2, 64).transpose(2, 3, 0, 1).reshape(128, 16))
    a["s5lre"] = np.stack([sl(inp["d_lam_re"][i]) for i in range(2)])
    a["s5lim"] = np.stack([sl(inp["d_lam_im"][i]) for i in range(2)])
    a["s5lst"] = np.stack([sl(np.broadcast_to(np.asarray(inp["d_log_step"][i], f32)[:, :, None], (2, 16, 64))) for i in range(2)])

    def slb(arr):
        return np.ascontiguousarray(np.asarray(arr, f32).reshape(2, 8, 2, 64, 16).transpose(2, 3, 0, 1, 4).reshape(128, 16, 16))
    a["s5bre"] = np.stack([slb(inp["d_b_re"][i]) for i in range(2)])
    a["s5bim"] = np.stack([slb(inp["d_b_im"][i]) for i in range(2)])

    def slc(arr):
        arr = np.asarray(arr, f32)
        o = np.zeros((128, 16, 128), f32)
        for d in range(2):
            for gp in range(8):
                for g2 in range(2):
                    gg = gp * 2 + g2
                    c0 = ((gp % 4) * 2 + g2) * 16
                    o[g2 * 64:(g2 + 1) * 64, d * 8 + gp, c0:c0 + 16] = arr[d, gg].T
        return o
    a["s5ctr"] = np.stack([slc(inp["d_c_re"][i]) for i in range(2)])
    a["s5cti"] = np.stack([slc(inp["d_c_im"][i]) for i in range(2)])
    a["s5dsk"] = np.stack([np.asarray(inp["d_skip"][i], f32).reshape(2, 128).T for i in range(2)])
    a["s5bgl"] = np.stack([np.asarray(inp["d_b_glu"][i], f32).reshape(2, 128).T for i in range(2)])
    a["iota"] = np.broadcast_to(np.arange(513, dtype=f32)[None, :], (128, 513)).copy()
    ki = np.arange(128)[:, None]
    qi = np.arange(512)[None, :]
    dm = np.zeros((20, 128, 512), f32)
    for o in range(20):
        dl = (o * 128 - 1024) + ki - qi
        m = np.zeros_like(dl, dtype=f32)
        for dil in (1, 4, 16):
            m += ((dl % dil == 0) & (np.abs(dl) <= 64 * dil)).astype(f32)
        dm[o] = m
    a["dmask"] = dm
    s1 = np.arange(128)
    k1 = np.arange(128)
    ang = 2 * np.pi * np.outer(s1, k1) / 128
    nrm = 1.0 / np.sqrt(S * 64.0)
    a["fW1"] = (np.concatenate([np.cos(ang), -np.sin(ang)], axis=1) * nrm).astype(f32)
    s2 = np.arange(S2)
    phi = 2 * np.pi * np.outer(s2, k1) / S
    a["fT"] = np.stack([np.cos(phi), np.sin(phi)], axis=1).astype(f32)
    th = 2 * np.pi * np.outer(s2, np.arange(S2)) / S2
    a["fW3"] = np.stack([np.concatenate([np.cos(th), -np.sin(th)], 1), np.concatenate([np.sin(th), np.cos(th)], 1)], axis=1).astype(f32)
    for k in a:
        a[k] = np.ascontiguousarray(a[k])
    return a


def run_module(inp, S, layers=(0, 1, 2, 3), stop_after=None, trace=False):
    x = np.asarray(inp["x"], np.float32)
    B = x.shape[0]
    nc, p = build_program(S, layers, stop_after)
    base = {n: np.ascontiguousarray(np.asarray(inp[n], np.float32)) for n in W_SHAPES}
    in_maps = []
    for b in range(B):
        m = dict(base)
        m.update(host_aux(inp, S, np.asarray(inp["positions"])[b]))
        m["x"] = np.ascontiguousarray(x[b])
        in_maps.append(m)
    res = run_bass_kernel_spmd(nc, in_maps, core_ids=list(range(B)), trace=trace)
    return np.stack([r["out"] for r in res.results], axis=0), res, p


def kernel(**inputs):
    out, _, _ = run_module(inputs, 8192)
    return out.astype(np.float32)
```

```python
import numpy as np
from contextlib import ExitStack
import concourse.bass as bass
import concourse.mybir as mybir

F32 = mybir.dt.float32
BF16 = mybir.dt.bfloat16
I32 = mybir.dt.int32
AF = mybir.ActivationFunctionType
ALU = mybir.AluOpType
AX = mybir.AxisListType

EPOCH = 30000
N_DMA_SEMS = 40


class V:
    __slots__ = ("ap", "key")

    def __init__(self, ap, key):
        self.ap = ap
        self.key = key

    def __getitem__(self, idx):
        return V(self.ap[idx], self.key)

    def k(self, sub):
        return V(self.ap, (self.key, sub))

    def re(self, s, **kw):
        return V(self.ap.rearrange(s, **kw), self.key)

    def bc(self, dt):
        return V(self.ap.bitcast(dt), self.key)

    def bcast(self, shape):
        return V(self.ap.to_broadcast(list(shape)), self.key)

    def ub(self, axis, shape):
        return V(self.ap.unsqueeze(axis).to_broadcast(list(shape)), self.key)


class Op:
    __slots__ = ("eng", "emit", "deps", "sig", "waits", "is_dma", "dsem", "dval", "needs_sig", "n")


class Prog:
    COMPUTE = ("pe", "act", "dve", "pool")

    def __init__(self, nc):
        self.nc = nc
        self.ops = []
        self.last_w = {}
        self.rd_eng = {}
        self.rd_dma = {}
        self.es = ExitStack()
        self.same_eng_sync = True
        self._n = 0
        self.gs = ExitStack()
        self.NEP = {"pe": 6, "act": 6, "dve": 8, "pool": 3}
        self.sems = {e: [self.gs.enter_context(nc.semaphore(f"s_{e}_{i}")) for i in range(self.NEP[e])]
                     for e in self.COMPUTE}
        self.dsems = [self.gs.enter_context(nc.semaphore(f"s_dma_{i}")) for i in range(N_DMA_SEMS)]
        self.cnt = {e: 0 for e in self.COMPUTE}
        self.dval = [0] * N_DMA_SEMS
        self.rot = 0
        self.know = {}
        self.bar = {}
        self.stats = {e: 0 for e in ("pe", "act", "dve", "pool", "sp")}

    def sb(self, name, shape, dt, glob=False):
        self._u = getattr(self, "_u", 0) + 1
        name = f"{name}_u{self._u}"
        t = (self.gs if glob else self.es).enter_context(self.nc.sbuf_tensor(name, list(shape), dt))
        return V(t[:], name)

    def ps(self, name, shape, dt):
        self._u = getattr(self, "_u", 0) + 1
        name = f"{name}_u{self._u}"
        t = self.es.enter_context(self.nc.psum_tensor(name, list(shape), dt))
        return V(t[:], name)

    def dram(self, name, shape, dt, kind="Internal"):
        t = self.nc.dram_tensor(name, list(shape), dt, kind=kind)
        return V(t.ap(), name)

    def add(self, eng, emit, reads=(), writes=(), is_dma=False):
        op = Op()
        op.eng = eng
        op.emit = emit
        op.is_dma = is_dma
        op.sig = None
        op.needs_sig = False
        op.waits = None
        op.dsem = None
        op.dval = None
        op.n = self._n
        self._n += 1
        deps = {}
        rk = [v.key if isinstance(v, V) else v for v in reads]
        wk = [v.key if isinstance(v, V) else v for v in writes]
        for k in rk:
            w = self.last_w.get(k)
            if w is not None:
                deps[w.n] = w
        for k in wk:
            w = self.last_w.get(k)
            if w is not None:
                deps[w.n] = w
            for r in self.rd_eng.get(k, {}).values():
                deps[r.n] = r
            for r in self.rd_dma.get(k, ()):
                deps[r.n] = r
        for k in wk:
            self.last_w[k] = op
            self.rd_eng[k] = {}
            self.rd_dma[k] = []
        for k in rk:
            if k in wk:
                continue
            if is_dma:
                self.rd_dma.setdefault(k, []).append(op)
            else:
                self.rd_eng.setdefault(k, {})[eng] = op
        deps.pop(op.n, None)
        op.deps = list(deps.values())
        self.ops.append(op)
        return op

    def dma(self, q, out, in_, extra_reads=(), extra_writes=()):
        o, i = out.ap, in_.ap
        return self.add(q, lambda e: e.dma_start(out=o, in_=i), [in_] + list(extra_reads),
                        [out] + list(extra_writes), is_dma=True)

    def mm(self, out, lhsT, rhs, start=True, stop=True, **kw):
        o, l, r = out.ap, lhsT.ap, rhs.ap
        rd = [lhsT, rhs] + ([] if start else [out])
        return self.add("pe", lambda e: e.matmul(o, l, r, start=start, stop=stop, **kw), rd, [out])

    def tr(self, out, in_, ident):
        o, i, d = out.ap, in_.ap, ident.ap
        return self.add("pe", lambda e: e.transpose(o, i, d), [in_, ident], [out])

    def act(self, out, in_, func, scale=1.0, bias=0.0, eng="act", accum=None):
        o, i = out.ap, in_.ap
        rd = [in_]
        wr = [out]
        kw = {}
        if isinstance(scale, V):
            rd.append(scale)
            kw["scale"] = scale.ap
        else:
            kw["scale"] = scale
        if isinstance(bias, V):
            rd.append(bias)
            kw["bias"] = bias.ap
        else:
            kw["bias"] = bias
        if accum is not None:
            kw["accum_out"] = accum.ap
            wr.append(accum)
        return self.add(eng, lambda e: e.activation(o, i, func, **kw), rd, wr)

    def tt(self, out, in0, in1, op, eng="dve"):
        o, a, b = out.ap, in0.ap, in1.ap
        return self.add(eng, lambda e: e.tensor_tensor(o, a, b, op), [in0, in1], [out])

    def ts(self, out, in0, s1, s2=None, op0=ALU.mult, op1=None, eng="dve"):
        o, a = out.ap, in0.ap
        rd = [in0]
        a1 = s1
        a2 = s2
        if isinstance(s1, V):
            rd.append(s1)
            a1 = s1.ap
        if isinstance(s2, V):
            rd.append(s2)
            a2 = s2.ap
        if op1 is None:
            return self.add(eng, lambda e: e.tensor_scalar(o, a, a1, None, op0), rd, [out])
        return self.add(eng, lambda e: e.tensor_scalar(o, a, a1, a2, op0, op1), rd, [out])

    def stt(self, out, in0, scalar, in1, op0, op1):
        o, a, b = out.ap, in0.ap, in1.ap
        rd = [in0, in1]
        s = scalar
        if isinstance(scalar, V):
            rd.append(scalar)
            s = scalar.ap
        return self.add("dve", lambda e: e.scalar_tensor_tensor(o, a, s, b, op0, op1), rd, [out])

    def copy(self, out, in_, eng="dve"):
        o, i = out.ap, in_.ap
        if eng == "act":
            return self.add(eng, lambda e: e.copy(o, i), [in_], [out])
        return self.add(eng, lambda e: e.tensor_copy(o, i), [in_], [out])

    def memset(self, out, val, eng="dve"):
        o = out.ap
        return self.add(eng, lambda e: e.memset(o, val), [], [out])

    def reduce(self, out, in_, op=ALU.add, axis=AX.X, eng="dve"):
        o, i = out.ap, in_.ap
        return self.add(eng, lambda e: e.tensor_reduce(o, i, axis, op), [in_], [out])

    def recip(self, out, in_):
        o, i = out.ap, in_.ap
        return self.add("dve", lambda e: e.reciprocal(o, i), [in_], [out])

    def _need_wait(self, op, d):
        if d.is_dma:
            return True
        if d.eng != op.eng:
            return True
        if op.is_dma:
            return True
        if op.eng == "pe":
            return False
        if op.eng == "pool":
            return True
        return self.same_eng_sync

    def _semval(self, key, val):
        if isinstance(key, tuple):
            return (self.dsems[key[1]], val)
        sig = val - 1
        assert sig // EPOCH < self.NEP[key], ("too many signals", key, sig)
        return (self.sems[key][sig // EPOCH], sig % EPOCH + 1)

    def flush(self):
        nc = self.nc
        ops = self.ops
        last = {}
        for op in ops:
            for d in op.deps:
                if self._need_wait(op, d):
                    d.needs_sig = True
            if not op.is_dma:
                last[op.eng] = op
        for op in last.values():
            op.needs_sig = True
        for op in ops:
            if not op.is_dma and op.needs_sig:
                op.sig = self.cnt[op.eng]
                self.cnt[op.eng] += 1
        seen_first = set()
        for op in ops:
            K = self.know.setdefault(op.eng, {})
            need = {}
            if op.eng not in seen_first:
                seen_first.add(op.eng)
                for k, v in self.bar.pop(op.eng, {}).items():
                    need[k] = max(need.get(k, 0), v)
            if op.is_dma:
                s = self.rot
                self.rot = (self.rot + 1) % N_DMA_SEMS
                if self.dval[s] > 0:
                    need[("d", s)] = max(need.get(("d", s), 0), self.dval[s])
                self.dval[s] += 16
                op.dsem = s
                op.dval = self.dval[s]
            for d in op.deps:
                if not self._need_wait(op, d):
                    continue
                if d.is_dma:
                    key, val = ("d", d.dsem), d.dval
                else:
                    key, val = d.eng, d.sig + 1
                if need.get(key, 0) < val:
                    need[key] = val
            waits = []
            for key, val in need.items():
                if K.get(key, 0) >= val:
                    continue
                K[key] = val
                waits.append(self._semval(key, val))
            op.waits = waits
            self.stats[op.eng] += 1
        by = {}
        for op in ops:
            by.setdefault(op.eng, []).append(op)
        sems, dsems = self.sems, self.dsems

        def run(engname, e):
            for op in by.get(engname, []):
                for (s, v) in op.waits:
                    e.wait_ge(s, v)
                ins = op.emit(e)
                if op.is_dma:
                    ins.then_inc(dsems[op.dsem], 16)
                elif op.sig is not None:
                    ins.then_inc(sems[op.eng][op.sig // EPOCH], 1)

        with nc.Block() as block:
            @block.sync
            def _(e):
                run("sp", e)

            @block.tensor
            def _(e):
                run("pe", e)

            @block.scalar
            def _(e):
                run("act", e)

            @block.vector
            def _(e):
                run("dve", e)

            @block.gpsimd
            def _(e):
                run("pool", e)
        front = {}
        for e in self.COMPUTE:
            if self.cnt[e] > 0:
                front[e] = self.cnt[e]
        for s in range(N_DMA_SEMS):
            if self.dval[s] > 0:
                front[("d", s)] = self.dval[s]
        self.bar = {e: dict(front) for e in ("pe", "act", "dve", "pool", "sp")}
        self.ops = []
        self.last_w = {}
        self.rd_eng = {}
        self.rd_dma = {}
        self.es.close()
        self.es = ExitStack()

    def finish(self):
        nc = self.nc
        self.flush()
        need = self.bar["sp"]
        K = self.know.setdefault("sp", {})
        dsems = self.dsems
        waits = [self._semval(k, v) for k, v in need.items() if K.get(k, 0) < v]
        with nc.Block() as block:
            @block.sync
            def _(e):
                for (s, v) in waits:
                    e.wait_ge(s, v)

from concourse.bass_utils import run_bass_kernel_spmd
import math

D = 1024
DFF = 2816
EPS = 1e-6
TWO_PI = 2.0 * math.pi
CW1 = 6.28125
CW2 = TWO_PI - 6.28125


class Ctx:
    pass


def make_psum(p):
    return [p.ps(f"psb{i}", [128, 512], F32) for i in range(8)]


def sincos(p, out_sin, out_cos, ang, t1, t2, ti):
    p.ts(t1, ang, 1.0 / TWO_PI, None, ALU.mult)
    p.copy(ti, t1)
    p.copy(t1, ti)
    p.stt(t2, t1, -CW1, ang, ALU.mult, ALU.add)
    p.stt(t2, t1, -CW2, t2, ALU.mult, ALU.add)
    p.ts(t1, t2, 3.1415925, -3.1415925, ALU.min, ALU.max)
    p.act(out_sin, t1, AF.Sin)
    p.ts(t1, t2, math.pi / 2, None, ALU.add)
    p.ts(ti.bc(F32), t1, math.pi, None, ALU.is_gt)
    p.stt(t1, ti.bc(F32), -TWO_PI, t1, ALU.mult, ALU.add)
    p.ts(t1, t1, 3.1415925, -3.1415925, ALU.min, ALU.max)
    p.act(out_cos, t1, AF.Sin)


def load_w_bf16(p, dst, src, rows, cols, stage, colchunk, tagi=[0]):
    nrc = rows // 128
    engs = ("dve", "pool", "act")
    for c in range(nrc):
        for c0 in range(0, cols, colchunk):
            w = min(colchunk, cols - c0)
            i = tagi[0]
            tagi[0] += 1
            st = stage.k(i % 2)[:, (i % 2), 0:w]
            p.dma("sp", st, src[c * 128:(c + 1) * 128, c0:c0 + w])
            p.copy(dst.k(c)[:, c, c0:c0 + w], st, eng=engs[i % 3])


def norm_T(p, C, xb, nt, gcol, hT, xn, ss, pst):
    for j in range(nt):
        p.act(xn[:, j, :], xb[:, j, :], AF.Square, accum=ss[:, j:j + 1])
    p.ts(ss[:, 4:4 + nt], ss[:, 0:nt], 1.0 / D, EPS, ALU.mult, ALU.add)
    p.act(ss[:, 4:4 + nt], ss[:, 4:4 + nt], AF.Sqrt)
    p.recip(ss[:, 8:8 + nt], ss[:, 4:4 + nt])
    for j in range(nt):
        p.ts(xn[:, j, :], xb[:, j, :], ss[:, 8 + j:9 + j], None, ALU.mult)
    for c in range(8):
        ps = pst[c % 2]
        pv = ps.bc(BF16)
        for j in range(nt):
            p.tr(pv[:, j * 128:(j + 1) * 128], xn[:, j, c * 128:(c + 1) * 128], C.ident_b)
        p.act(hT[:, c, 0:nt * 128], pv[:, 0:nt * 128], AF.Copy, scale=gcol[:, c:c + 1],
              eng=("act" if c % 2 == 0 else "act"))


def ffn_phase(p, C, x_src, x_dst, wg, wu, wd, gcol, S, TB=512):
    nt = TB // 128
    NF = DFF // 128
    Wg = p.sb("Wg", [128, 8, DFF], BF16)
    Wu = p.sb("Wu", [128, 8, DFF], BF16)
    Wd = p.sb("Wd", [128, NF, D], BF16)
    stage = p.sb("stage", [128, 2, 704], F32)
    xbs = [p.sb(f"xb{i}", [128, nt, D], F32) for i in range(2)]
    xn = p.sb("xn", [128, 2, D], BF16)
    hT = p.sb("hT", [128, 8, TB], BF16)
    actT = p.sb("actT", [128, NF, TB], BF16)
    sg = [p.sb(f"sg{i}", [128, TB], BF16) for i in range(2)]
    ss = p.sb("ss", [128, 12], F32)
    PS = make_psum(p)
    load_w_bf16(p, Wg, wg, D, DFF, stage, 704)
    load_w_bf16(p, Wu, wu, D, DFF, stage, 704)
    load_w_bf16(p, Wd, wd, DFF, D, stage, 512)
    for b in range(S // TB):
        xb = xbs[b % 2]
        rows = x_src.k(b)[b * TB:(b + 1) * TB, :].re("(j q) d -> q j d", q=128)
        p.dma("sp", xb, rows)
        for hh in range(nt // 2):
            norm_T(p, C, xb[:, 2 * hh:2 * hh + 2], 2, gcol, hT[:, :, hh * 256:(hh + 1) * 256], xn, ss, PS[0:2])
        for f in range(NF):
            pg = PS[2 + (f % 2)]
            pu = PS[4 + (f % 2)]
            for c in range(8):
                p.mm(pg[:, 0:TB], Wg.k(c)[:, c, f * 128:(f + 1) * 128], hT[:, c, :], start=(c == 0), stop=(c == 7))
            for c in range(8):
                p.mm(pu[:, 0:TB], Wu.k(c)[:, c, f * 128:(f + 1) * 128], hT[:, c, :], start=(c == 0), stop=(c == 7))
            s = sg[f % 2]
            p.act(s, pg[:, 0:TB], AF.Silu)
            p.tt(actT.k(f)[:, f, :], s, pu[:, 0:TB], ALU.mult)
        for j in range(nt):
            for dh in range(2):
                po = PS[6 + ((j * 2 + dh) % 2)]
                for f in range(NF):
                    p.mm(po, actT.k(f)[:, f, j * 128:(j + 1) * 128], Wd.k(f)[:, f, dh * 512:(dh + 1) * 512],
                         start=(f == 0), stop=(f == NF - 1))
                p.stt(xb[:, j, dh * 512:(dh + 1) * 512], po, 0.5, xb[:, j, dh * 512:(dh + 1) * 512], ALU.mult, ALU.add)
        p.dma("sp", x_dst.k(b)[b * TB:(b + 1) * TB, :].re("(j q) d -> q j d", q=128), xb)
    p.flush()


def load_block(p, C, x, b, TB, xb):
    p.dma("sp", xb, x.k(("r", b))[b * TB:(b + 1) * TB, :].re("(j q) d -> q j d", q=128))


def rot_pair(p, out3, in3, lo, half, cosb, sinb, ra, rb):
    t1 = in3[:, :, lo:lo + half]
    t2 = in3[:, :, lo + half:lo + 2 * half]
    p.tt(ra, t1, cosb, ALU.mult)
    p.tt(rb, t2, sinb, ALU.mult)
    p.tt(out3[:, :, lo:lo + half], ra, rb, ALU.subtract)
    p.tt(ra, t2, cosb, ALU.mult)
    p.tt(rb, t1, sinb, ALU.mult)
    p.tt(out3[:, :, lo + half:lo + 2 * half], ra, rb, ALU.add)


def rsqrt_mean(p, out, ssum, n, tmp):
    p.ts(tmp, ssum, 1.0 / n, EPS, ALU.mult, ALU.add)
    p.act(tmp, tmp, AF.Sqrt)
    p.recip(out, tmp)


def even_proj_phase(p, C, i, gcol, S):
    TB, nt = 512, 4
    Win = p.sb("Win", [128, 8, 1792], BF16)
    stage = p.sb("stage", [128, 2, 1792], F32)
    load_w_bf16(p, Win, C.ab_w_in[i], D, 1792, stage, 1792)
    gqk = p.sb("gqk", [128, 16, 64], F32)
    p.dma("sp", gqk, C.gqk[i])
    p.ts(gqk[:, 0:8, :], gqk[:, 0:8, :], 0.125, None, ALU.mult)
    xbs = [p.sb(f"xb{k}", [128, nt, D], F32) for k in range(2)]
    xn = p.sb("xn", [128, nt, D], BF16)
    hT = p.sb("hT", [128, 8, TB], BF16)
    ss = p.sb("ss", [128, 12], F32)
    cs = p.sb("cs", [128, 2, nt, 8], F32)
    sq = p.sb("sq", [128, 1024], F32)
    qn = p.sb("qn", [128, 16, 64], F32)
    qb = p.sb("qb", [128, 16, 64], BF16)
    ssq = p.sb("ssq", [128, 16], F32)
    rs = p.sb("rs", [128, 16], F32)
    ra = p.sb("ra", [128, 16, 8], F32)
    rb = p.sb("rb", [128, 16, 8], F32)
    qkT = p.sb("qkT", [128, 8, TB], BF16)
    vb = p.sb("vb", [128, nt, 512], BF16)
    fb = p.sb("fb", [128, nt, 256], BF16)
    PS = make_psum(p)
    for b in range(S // TB):
        xb = xbs[b % 2]
        load_block(p, C, C.xres, b, TB, xb)
        p.dma("sp", cs[:, 0], C.cosA[:, b * nt:(b + 1) * nt, :])
        p.dma("sp", cs[:, 1], C.sinA[:, b * nt:(b + 1) * nt, :])
        norm_T(p, C, xb, nt, gcol, hT, xn, ss, PS[0:2])
        for j in range(nt):
            pq, pk, pv, pf = PS[2], PS[3], PS[4], PS[5]
            for (ps, c0, w) in ((pq, 0, 512), (pk, 512, 512), (pv, 1024, 512), (pf, 1536, 256)):
                for c in range(8):
                    p.mm(ps[:, 0:w], hT[:, c, j * 128:(j + 1) * 128], Win.k(c)[:, c, c0:c0 + w],
                         start=(c == 0), stop=(c == 7))
            p.copy(vb[:, j, :], pv, eng="act")
            p.copy(fb[:, j, :], pf[:, 0:256], eng="act")
            p.act(sq[:, 0:512], pq, AF.Square)
            p.act(sq[:, 512:1024], pk, AF.Square)
            p.reduce(ssq, sq.re("p (h e) -> p h e", e=64))
            rsqrt_mean(p, rs, ssq, 64, ssq)
            p.tt(qn[:, 0:8, :], pq.re("p (h e) -> p h e", e=64), rs[:, 0:8].ub(2, [128, 8, 64]), ALU.mult)
            p.tt(qn[:, 8:16, :], pk.re("p (h e) -> p h e", e=64), rs[:, 8:16].ub(2, [128, 8, 64]), ALU.mult)
            p.tt(qn, qn, gqk, ALU.mult)
            cosb = cs[:, 0, j, :].ub(1, [128, 16, 8])
            sinb = cs[:, 1, j, :].ub(1, [128, 16, 8])
            rot_pair(p, qb, qn, 0, 8, cosb, sinb, ra, rb)
            p.copy(qb[:, :, 16:64], qn[:, :, 16:64], eng="pool")
            pt = PS[6 + (j % 2)].bc(BF16)
            qbf = qb.re("p h e -> p (h e)")
            for c in range(8):
                p.tr(pt[:, c * 128:(c + 1) * 128], qbf[:, c * 128:(c + 1) * 128], C.ident_b)
            p.copy(qkT[:, :, j * 128:(j + 1) * 128], pt.re("p (c t) -> p c t", t=128), eng="act")
        sl = slice(b * TB, (b + 1) * TB)
        p.dma("sp", C.QT.k(b)[:, sl].re("(c q) t -> q c t", q=128), qkT[:, 0:4, :])
        p.dma("sp", C.KT.k(b)[:, sl].re("(c q) t -> q c t", q=128), qkT[:, 4:8, :])
        p.dma("sp", C.Vs.k(b)[sl, :].re("(j q) e -> q j e", q=128), vb)
        p.dma("sp", C.Fs.k(b)[sl, :].re("(j q) e -> q j e", q=128), fb)
    p.flush()


def attention_phase(p, C, S, nheads, dk, qsrc, ksrc, Vs, OT, masked):
    NB, NQ = S // 128, S // 512
    qt = [p.sb(f"qt{k}", [128, S], BF16) for k in range(2)]
    kt = [p.sb(f"kt{k}", [128, S], BF16) for k in range(2)]
    vx = [p.sb(f"vx{k}", [128, NB, 128], BF16) for k in range(2)]
    pex = [p.sb(f"pex{k}", [128, 512], BF16) for k in range(4)]
    den = p.sb("den", [128, 512], F32)
    rden = p.sb("rden", [64, 512], F32)
    ot = [p.sb(f"ot{k}", [64, 512], BF16) for k in range(2)]
    onesf = p.sb("onesf", [128, 64], F32)
    p.memset(onesf, 1.0)
    for k in range(2):
        p.memset(vx[k][:, :, 64:128], 1.0)
    if masked:
        mk = p.sb("mk", [128, 20, 512], BF16)
        mst = p.sb("mst", [128, 2, 512], F32)
        for o in range(20):
            p.dma("sp", mst.k(o % 2)[:, o % 2, :], C.dmask[o])
            p.copy(mk.k(o)[:, o, :], mst.k(o % 2)[:, o % 2, :], eng=("dve" if o % 2 else "pool"))
    PS = make_psum(p)
    NBUF, LA = 4, 2
    tiles = []
    for h in range(nheads):
        for qb in range(NQ):
            q0 = qb * 512
            if masked:
                kbs = [kb for kb in range(NB) if -1024 <= kb * 128 - q0 <= 1408]
            else:
                kbs = list(range(NB))
            for idx, kb in enumerate(kbs):
                tiles.append((h, qb, kb, idx == 0, idx == len(kbs) - 1))
    n = len(tiles)
    loaded = set()
    for t in range(n + LA):
        if t < n:
            h, qb, kb, first, last = tiles[t]
            q_, k_, v_ = qt[h % 2], kt[h % 2], vx[h % 2]
            if h not in loaded:
                loaded.add(h)
                p.dma("sp", q_[0:dk, :], qsrc(h))
                p.dma("sp", k_[0:dk, :], ksrc(h))
                p.dma("sp", v_[:, :, 0:64], Vs[:, h * 64:(h + 1) * 64].re("(n q) e -> q n e", q=128))
            q0 = qb * 512
            ps = PS[t % NBUF]
            pe_ = pex[t % NBUF]
            p.mm(ps, k_[0:dk, kb * 128:(kb + 1) * 128], q_[0:dk, q0:q0 + 512])
            p.act(pe_, ps, AF.Exp)
            if masked:
                o = (kb * 128 - q0 + 1024) // 128
                p.tt(pe_, pe_, mk.k(o)[:, o, :], ALU.mult)
        u = t - LA
        if u >= 0:
            h, qb, kb, first, last = tiles[u]
            v_ = vx[h % 2]
            q0 = qb * 512
            po = PS[4 + (qb % 2)]
            p.mm(po, v_[:, kb, :], pex[u % NBUF], start=first, stop=last)
            if last:
                p.copy(den[64:65, :], po[64:65, :], eng="act")
                pb = PS[6 + (qb % 2)]
                p.mm(pb[0:64, :], onesf[64:65, 0:64], den[64:65, :])
                p.recip(rden, pb[0:64, :])
                o_ = ot[qb % 2]
                p.tt(o_, po[0:64, :], rden, ALU.mult)
                p.dma("sp", OT.k((h, qb))[h * 64:(h + 1) * 64, q0:q0 + 512], o_)
    p.flush()


def fnet_phase(p, C, S):
    S2 = S // 128
    ya = p.sb("ya", [128, S2, 256], BF16)
    p.dma("sp", ya.re("p a c -> p (a c)"), C.Fs.re("(p a) c -> p (a c)", p=128))
    w1f = p.sb("w1f", [128, 256], F32)
    w1 = p.sb("w1", [128, 256], BF16)
    p.dma("sp", w1f, C.fW1)
    p.copy(w1, w1f)
    tw = p.sb("tw", [S2, 2, 128], F32)
    p.dma("sp", tw, C.fT)
    w3f = p.sb("w3f", [S2, 2, 2 * S2], F32)
    w3 = p.sb("w3", [S2, 2, 2 * S2], BF16)
    p.dma("sp", w3f, C.fW3)
    p.copy(w3, w3f)
    PS = make_psum(p)
    ta = [p.sb(f"ta{k}", [S2, 2, 128], F32) for k in range(4)]
    z2r = p.sb("z2r", [S2, 128, 128], BF16)
    z2i = p.sb("z2i", [S2, 128, 128], BF16)
    zt = p.sb("zt", [128, 2, S2, 128], BF16)
    for half in range(2):
        for cp in range(64):
            ps = PS[cp % 2]
            for c2 in range(2):
                ch = half * 128 + cp * 2 + c2
                p.mm(ps[0:S2, c2 * 256:(c2 + 1) * 256], ya[:, :, ch], w1)
            pv = ps[0:S2, :].re("p (c x k) -> p c x k", c=2, x=2)
            zr, zi = pv[:, :, 0, :], pv[:, :, 1, :]
            tc = tw[:, 0, :].ub(1, [S2, 2, 128])
            tsn = tw[:, 1, :].ub(1, [S2, 2, 128])
            p.tt(ta[0], zr, tc, ALU.mult)
            p.tt(ta[1], zi, tsn, ALU.mult)
            p.tt(ta[2], zi, tc, ALU.mult)
            p.tt(ta[3], zr, tsn, ALU.mult)
            cl = cp * 2
            p.tt(z2r[:, :, cl:cl + 2].re("p k c -> p c k"), ta[0], ta[1], ALU.add, eng="pool")
            p.tt(z2i[:, :, cl:cl + 2].re("p k c -> p c k"), ta[2], ta[3], ALU.subtract, eng="pool")
        for kg in range(32):
            ps = PS[2 + (kg % 2)]
            for kk in range(4):
                k1 = kg * 4 + kk
                o = ps[:, kk * 128:kk * 128 + 2 * S2]
                p.mm(o, z2r[:, k1, :], w3[:, 0, :], start=True, stop=False)
                p.mm(o, z2i[:, k1, :], w3[:, 1, :], start=False, stop=True)
            src = ps.re("p (q x) -> p q x", x=128)[:, :, 0:2 * S2]
            dst = zt[:, :, :, kg * 4:(kg + 1) * 4].re("p r k q -> p q (r k)")
            p.copy(dst, src, eng=("act" if kg % 2 else "dve"))
        p.dma("sp", C.ZrT[half * 128:(half + 1) * 128, :], zt[:, 0].re("p k q -> p (k q)"))
        p.dma("sp", C.ZiT[half * 128:(half + 1) * 128, :], zt[:, 1].re("p k q -> p (k q)"))
    p.flush()


def outproj_phase(p, C, S, wout, srcs, fold=None):
    TB, nt = 512, 4
    stage = p.sb("stage", [128, 2, 1024], F32)
    chunks = []
    for (t, n) in srcs:
        for c in range(n):
            chunks.append((t, c))
    NCH = len(chunks)
    W = p.sb("W", [128, NCH, 1024], BF16)
    PS = make_psum(p)
    if fold is None:
        load_w_bf16(p, W, wout, NCH * 128, 1024, stage, 1024)
    else:
        i = fold
        wtmp = p.sb("wtmp", [128, 6, 1024], BF16)
        load_w_bf16(p, wtmp, wout, 768, 1024, stage, 1024)
        for c in range(4):
            p.copy(W.k(c)[:, c, :], wtmp.k(c)[:, c, :])
        cf = p.sb("cf", [128, 4, 128], F32)
        cb = p.sb("cb", [128, 4, 128], BF16)
        p.dma("sp", cf, C.fBD[i])
        p.copy(cb, cf)
        m1 = p.sb("m1", [128, 2, 1024], BF16)
        for pr in range(2):
            for dh in range(2):
                ps = PS[dh]
                p.mm(ps, cb[:, pr, :], wtmp.k(4 + pr)[:, 4 + pr, dh * 512:(dh + 1) * 512])
                p.copy(m1[:, pr, dh * 512:(dh + 1) * 512], ps)
        for ri in range(2):
            for pr in range(2):
                for dh in range(2):
                    ps = PS[2 + dh]
                    p.mm(ps, cb[:, 2 + ri, :], m1[:, pr, dh * 512:(dh + 1) * 512])
                    c = 4 + ri * 2 + pr
                    p.copy(W.k(c)[:, c, dh * 512:(dh + 1) * 512], ps)
    xbs = [p.sb(f"xb{k}", [128, nt, D], F32) for k in range(2)]
    mT = [p.sb(f"mT{k}", [128, NCH, TB], BF16) for k in range(2)]
    for b in range(S // TB):
        xb, m = xbs[b % 2], mT[b % 2]
        load_block(p, C, C.xres, b, TB, xb)
        for ci, (t, c) in enumerate(chunks):
            p.dma("sp", m.k(ci)[:, ci, :], t[c * 128:(c + 1) * 128, b * TB:(b + 1) * TB])
        for j in range(nt):
            for dh in range(2):
                ps = PS[4 + ((j * 2 + dh) % 4)]
                for ci in range(NCH):
                    p.mm(ps, m.k(ci)[:, ci, j * 128:(j + 1) * 128], W.k(ci)[:, ci, dh * 512:(dh + 1) * 512],
                         start=(ci == 0), stop=(ci == NCH - 1))
                p.tt(xb[:, j, dh * 512:(dh + 1) * 512], ps, xb[:, j, dh * 512:(dh + 1) * 512], ALU.add)
        p.dma("sp", C.xres.k(("r", b))[b * TB:(b + 1) * TB, :].re("(j q) d -> q j d", q=128), xb)
    p.flush()


def odd_proj_phase(p, C, i, gcol, S):
    TB, nt = 512, 4
    Win = p.sb("Win", [128, 8, 672], BF16)
    Wq = p.sb("Wq", [128, 2, 768], BF16)
    Wkv = p.sb("Wkv", [128, 1, 1024], BF16)
    stage = p.sb("stage", [128, 2, 1024], F32)
    load_w_bf16(p, Win, C.cd_w_in[i], D, 672, stage, 672)
    load_w_bf16(p, Wq, C.c_w_q_up[i], 256, 768, stage, 768)
    load_w_bf16(p, Wkv, C.c_w_kv_up[i], 128, 1024, stage, 1024)
    glat = p.sb("glat", [128, 384], F32)
    p.dma("sp", glat, C.glat[i])
    gq = p.sb("gq", [128, 8, 96], F32)
    gk = p.sb("gk", [128, 8, 96], F32)
    p.dma("sp", gq, C.gq96[i])
    p.dma("sp", gk, C.gk96[i])
    p.ts(gq, gq, 96.0 ** -0.5, None, ALU.mult)
    xbs = [p.sb(f"xb{k}", [128, nt, D], F32) for k in range(2)]
    xn = p.sb("xn", [128, nt, D], BF16)
    hT = p.sb("hT", [128, 8, TB], BF16)
    ss = p.sb("ss", [128, 12], F32)
    cs = p.sb("cs", [128, 2, nt, 16], F32)
    junk = p.sb("junk", [128, 1024], F32)
    ssl = p.sb("ssl", [128, 4], F32)
    rl = p.sb("rl", [128, 2], F32)
    qln = p.sb("qln", [128, 384], F32)
    lnb = p.sb("lnb", [128, 384], BF16)
    lnT = p.sb("lnT", [128, 3, 128], BF16)
    kpe = p.sb("kpe", [128, 32], F32)
    kg = p.sb("kg", [128, 1, 32], F32)
    R = p.sb("R", [128, 1, 32], F32)
    qf = p.sb("qf", [128, 8, 96], F32)
    kvf = p.sb("kvf", [128, 8, 128], F32)
    s8 = p.sb("s8", [128, 8], F32)
    rq = p.sb("rq", [128, 8], F32)
    rk = p.sb("rk", [128, 8], F32)
    sp1 = p.sb("sp1", [128, 1], F32)
    ra = p.sb("ra", [128, 8, 16], F32)
    rb = p.sb("rb", [128, 8, 16], F32)
    tk = p.sb("tk", [128, 8, 64], F32)
    qb3 = p.sb("qb3", [128, 8, 96], BF16)
    kb3 = p.sb("kb3", [128, 8, 96], BF16)
    qTb = p.sb("qTb", [128, 8, TB], BF16)
    kTb = p.sb("kTb", [128, 8, TB], BF16)
    vb = p.sb("vb", [128, nt, 512], BF16)
    uTb = p.sb("uTb", [128, 2, TB], BF16)
    PS = make_psum(p)
    for b in range(S // TB):
        xb = xbs[b % 2]
        load_block(p, C, C.xres, b, TB, xb)
        p.dma("sp", cs[:, 0], C.cosC[:, b * nt:(b + 1) * nt, :])
        p.dma("sp", cs[:, 1], C.sinC[:, b * nt:(b + 1) * nt, :])
        norm_T(p, C, xb, nt, gcol, hT, xn, ss, PS[0:2])
        for m in range(2):
            pu = PS[3]
            for c in range(8):
                p.mm(pu, Win.k(c)[:, c, 416 + m * 128:416 + (m + 1) * 128], hT[:, c, :], start=(c == 0), stop=(c == 7))
            p.copy(uTb[:, m, :], pu, eng="act")
        for j in range(nt):
            pl = PS[2]
            for c in range(8):
                p.mm(pl[:, 0:416], hT[:, c, j * 128:(j + 1) * 128], Win.k(c)[:, c, 0:416], start=(c == 0), stop=(c == 7))
            p.act(junk[:, 0:256], pl[:, 0:256], AF.Square, accum=ssl[:, 0:1])
            p.act(junk[:, 256:384], pl[:, 256:384], AF.Square, accum=ssl[:, 1:2])
            p.ts(ssl[:, 2:3], ssl[:, 0:1], 1.0 / 256, EPS, ALU.mult, ALU.add)
            p.ts(ssl[:, 3:4], ssl[:, 1:2], 1.0 / 128, EPS, ALU.mult, ALU.add)
            p.act(ssl[:, 2:4], ssl[:, 2:4], AF.Sqrt)
            p.recip(rl, ssl[:, 2:4])
            p.ts(qln[:, 0:256], pl[:, 0:256], rl[:, 0:1], None, ALU.mult)
            p.ts(qln[:, 256:384], pl[:, 256:384], rl[:, 1:2], None, ALU.mult)
            p.copy(kpe, pl[:, 384:416], eng="act")
            p.tt(lnb, qln, glat, ALU.mult)
            pt = PS[0].bc(BF16)
            for c in range(3):
                p.tr(pt[:, c * 128:(c + 1) * 128], lnb[:, c * 128:(c + 1) * 128], C.ident_b)
            p.copy(lnT, pt[:, 0:384].re("p (c t) -> p c t", t=128), eng="act")
            pq0, pq1, pk0, pk1 = PS[4], PS[5], PS[6], PS[7]
            for c2 in range(2):
                p.mm(pq0, lnT[:, c2, :], Wq.k(c2)[:, c2, 0:512], start=(c2 == 0), stop=(c2 == 1))
            for c2 in range(2):
                p.mm(pq1[:, 0:256], lnT[:, c2, :], Wq.k(c2)[:, c2, 512:768], start=(c2 == 0), stop=(c2 == 1))
            p.mm(pk0, lnT[:, 2, :], Wkv.k(0)[:, 0, 0:512])
            p.mm(pk1, lnT[:, 2, :], Wkv.k(0)[:, 0, 512:1024])
            qff = qf.re("p h e -> p (h e)")
            kvff = kvf.re("p h e -> p (h e)")
            p.copy(qff[:, 0:512], pq0, eng="act")
            p.copy(qff[:, 512:768], pq1[:, 0:256], eng="act")
            p.copy(kvff[:, 0:512], pk0, eng="act")
            p.copy(kvff[:, 512:1024], pk1, eng="act")
            p.act(junk[:, 0:768], qff, AF.Square)
            p.reduce(s8, junk[:, 0:768].re("p (h e) -> p h e", e=96))
            rsqrt_mean(p, rq, s8, 96, s8)
            p.tt(qf, qf, rq.ub(2, [128, 8, 96]), ALU.mult)
            p.tt(qf, qf, gq, ALU.mult)
            cosb = cs[:, 0, j, :].ub(1, [128, 8, 16])
            sinb = cs[:, 1, j, :].ub(1, [128, 8, 16])
            rot_pair(p, qb3, qf, 64, 16, cosb, sinb, ra, rb)
            p.copy(qb3[:, :, 0:64], qf[:, :, 0:64], eng="pool")
            p.act(junk, kvff, AF.Square)
            p.reduce(s8, junk.re("p (h e) -> p h e", e=128)[:, :, 0:64])
            p.act(junk[:, 0:32], kpe, AF.Square, accum=sp1)
            p.ts(s8, s8, sp1[:, 0:1], None, ALU.add)
            rsqrt_mean(p, rk, s8, 96, s8)
            p.tt(tk, kvf[:, :, 0:64], rk.ub(2, [128, 8, 64]), ALU.mult)
            p.tt(kb3[:, :, 0:64], tk, gk[:, :, 0:64], ALU.mult)
            p.tt(kg[:, 0, :], kpe, gk[:, 0, 64:96], ALU.mult)
            rot_pair(p, R, kg, 0, 16, cs[:, 0, j, :].ub(1, [128, 1, 16]), cs[:, 1, j, :].ub(1, [128, 1, 16]),
                     ra[:, 0:1, :], rb[:, 0:1, :])
            p.tt(kb3[:, :, 64:96], R[:, 0, :].ub(1, [128, 8, 32]), rk.ub(2, [128, 8, 32]), ALU.mult)
            p.copy(vb[:, j, :].re("p (h e) -> p h e", e=64), kvf[:, :, 64:128], eng="pool")
            ptq = PS[0].bc(BF16)
            ptk = PS[1].bc(BF16)
            for h in range(8):
                p.tr(ptq[0:96, h * 128:(h + 1) * 128], qb3[:, h, :], C.ident_b)
            for h in range(8):
                p.tr(ptk[0:96, h * 128:(h + 1) * 128], kb3[:, h, :], C.ident_b)
            p.copy(qTb[0:96, :, j * 128:(j + 1) * 128], ptq[0:96, :].re("p (h t) -> p h t", t=128), eng="act")
            p.copy(kTb[0:96, :, j * 128:(j + 1) * 128], ptk[0:96, :].re("p (h t) -> p h t", t=128), eng="act")
        sl = slice(b * TB, (b + 1) * TB)
        p.dma("sp", C.QTc.k(b)[:, :, sl].re("h e t -> e h t"), qTb[0:96])
        p.dma("sp", C.KTc.k(b)[:, :, sl].re("h e t -> e h t"), kTb[0:96])
        p.dma("sp", C.Vs.k(b)[sl, :].re("(j q) e -> q j e", q=128), vb)
        p.dma("sp", C.UT.k(b)[:, sl].re("(m q) t -> q m t", q=128), uTb)
    p.flush()


def s5_phase(p, C, i, S):
    TB = 512
    NBk = S // TB
    names = ["lre", "lim", "lst", "step", "rho", "th", "cth", "sth", "lbr", "lbi", "nr", "den", "kr", "ki", "t1", "t2", "t3"]
    T = {n: p.sb("s5_" + n, [128, 16], F32) for n in names}
    tiI = p.sb("s5_ti", [128, 16], I32)
    p.dma("sp", T["lre"], C.s5lre[i])
    p.dma("sp", T["lim"], C.s5lim[i])
    p.dma("sp", T["lst"], C.s5lst[i])
    p.act(T["step"], T["lst"], AF.Exp)
    p.tt(T["t1"], T["lre"], T["step"], ALU.mult)
    p.act(T["rho"], T["t1"], AF.Exp)
    p.tt(T["th"], T["lim"], T["step"], ALU.mult)
    thr = p.sb("s5_thr", [128, 16], F32)
    p.ts(T["t1"], T["th"], 1.0 / TWO_PI, None, ALU.mult)
    p.copy(tiI, T["t1"])
    p.copy(T["t1"], tiI)
    p.stt(thr, T["t1"], -CW1, T["th"], ALU.mult, ALU.add)
    p.stt(thr, T["t1"], -CW2, thr, ALU.mult, ALU.add)
    sincos(p, T["sth"], T["cth"], thr, T["t1"], T["t2"], tiI)
    p.tt(T["lbr"], T["rho"], T["cth"], ALU.mult)
    p.tt(T["lbi"], T["rho"], T["sth"], ALU.mult)
    p.ts(T["nr"], T["lbr"], -1.0, None, ALU.add)
    p.tt(T["t1"], T["lre"], T["lre"], ALU.mult)
    p.tt(T["t2"], T["lim"], T["lim"], ALU.mult)
    p.tt(T["den"], T["t1"], T["t2"], ALU.add)
    p.recip(T["den"], T["den"])
    p.tt(T["t1"], T["nr"], T["lre"], ALU.mult)
    p.tt(T["t2"], T["lbi"], T["lim"], ALU.mult)
    p.tt(T["t1"], T["t1"], T["t2"], ALU.add)
    p.tt(T["kr"], T["t1"], T["den"], ALU.mult)
    p.tt(T["t1"], T["lbi"], T["lre"], ALU.mult)
    p.tt(T["t2"], T["nr"], T["lim"], ALU.mult)
    p.tt(T["t1"], T["t1"], T["t2"], ALU.subtract)
    p.tt(T["ki"], T["t1"], T["den"], ALU.mult)
    bre = p.sb("s5_bre", [128, 16, 16], F32)
    bim = p.sb("s5_bim", [128, 16, 16], F32)
    bbr = p.sb("s5_bbr", [128, 16, 16], F32)
    bbi = p.sb("s5_bbi", [128, 16, 16], F32)
    bt = p.sb("s5_bt", [128, 16, 16], F32)
    p.dma("sp", bre, C.s5bre[i])
    p.dma("sp", bim, C.s5bim[i])
    krb, kib = T["kr"].ub(2, [128, 16, 16]), T["ki"].ub(2, [128, 16, 16])
    p.tt(bbr, bre, krb, ALU.mult)
    p.tt(bt, bim, kib, ALU.mult)
    p.tt(bbr, bbr, bt, ALU.subtract)
    p.tt(bbi, bim, krb, ALU.mult)
    p.tt(bt, bre, kib, ALU.mult)
    p.tt(bbi, bbi, bt, ALU.add)
    PS = make_psum(p)
    LB = p.sb("s5_LB", [128, 16, 2, 128], BF16)
    blk = p.sb("s5_blk", [128, 2, 128], F32)
    for dg in range(16):
        gq_ = dg % 4
        p.memset(blk, 0.0)
        for ri, src in enumerate((bbr, bbi)):
            for g2 in range(2):
                c0 = (gq_ * 2 + g2) * 16
                p.copy(blk[g2 * 64:(g2 + 1) * 64, ri, c0:c0 + 16], src[g2 * 64:(g2 + 1) * 64, dg, :])
        for ri in range(2):
            ps = PS[ri]
            p.tr(ps[:, 0:128], blk[:, ri, :], C.ident_f)
            p.copy(LB[:, dg, ri, :], ps[:, 0:128])
    CT = p.sb("s5_CT", [128, 16, 2, 128], BF16)
    cst = p.sb("s5_cst", [128, 16, 128], F32)
    p.dma("sp", cst, C.s5ctr[i])
    p.copy(CT[:, :, 0, :], cst)
    p.dma("sp", cst, C.s5cti[i])
    p.ts(CT[:, :, 1, :], cst, -1.0, None, ALU.mult)
    NJ = 513
    iot = p.sb("s5_iot", [128, NJ], F32)
    p.dma("sp", iot, C.iota)
    RC = p.sb("s5_RC", [128, 16, NJ], F32)
    RS = p.sb("s5_RS", [128, 16, NJ], F32)
    a1 = p.sb("s5_a1", [128, NJ], F32)
    a2 = p.sb("s5_a2", [128, NJ], F32)
    a3 = p.sb("s5_a3", [128, NJ], F32)
    ai = p.sb("s5_ai", [128, NJ], I32)
    for dg in range(16):
        p.ts(a1, iot, thr[:, dg:dg + 1], None, ALU.mult)
        sincos(p, RS[:, dg, :], RC[:, dg, :], a1, a2, a3, ai)
    dsk = p.sb("s5_dsk", [128, 2], F32)
    bgl = p.sb("s5_bgl", [128, 2], F32)
    p.dma("sp", dsk, C.s5dsk[i])
    p.dma("sp", bgl, C.s5bgl[i])
    Wgl = p.sb("s5_Wgl", [128, 2, 256], BF16)
    stage = p.sb("stage", [128, 2, 256], F32)
    load_w_bf16(p, Wgl, C.d_w_glu[i], 256, 256, stage, 256)
    uT = p.sb("s5_uT", [128, 2, S], BF16)
    p.dma("sp", uT, C.UT.re("(m q) t -> q m t", q=128))
    carry = p.sb("s5_carry", [128, 16, 2], F32)
    cnew_ = [p.sb(f"s5_cnew{k}", [128, 4], F32) for k in range(2)]
    er_ = [p.sb(f"s5_er{k}", [128, TB], F32) for k in range(2)]
    ei_ = [p.sb(f"s5_ei{k}", [128, TB], F32) for k in range(2)]
    wr_ = [p.sb(f"s5_wr{k}", [128, TB], F32) for k in range(2)]
    wi_ = [p.sb(f"s5_wi{k}", [128, TB], F32) for k in range(2)]
    m1_ = [p.sb(f"s5_m1{k}", [128, TB], F32) for k in range(2)]
    m2_ = [p.sb(f"s5_m2{k}", [128, TB], F32) for k in range(2)]
    n1_ = [p.sb(f"s5_n1{k}", [128, TB], F32) for k in range(2)]
    n2_ = [p.sb(f"s5_n2{k}", [128, TB], F32) for k in range(2)]
    xr = [p.sb(f"s5_xr{k}", [128, TB], BF16) for k in range(2)]
    xi = [p.sb(f"s5_xi{k}", [128, TB], BF16) for k in range(2)]
    yf = p.sb("s5_yf", [128, 2, TB], F32)
    yv = p.sb("s5_yv", [128, 2, TB], F32)
    zb = p.sb("s5_zb", [128, 2, TB], BF16)
    zf = p.sb("s5_zf", [128, 2, TB], F32)
    g1 = p.sb("s5_g1", [128, TB], F32)
    g2t = p.sb("s5_g2", [128, TB], F32)
    dob = p.sb("s5_dob", [128, 2, TB], BF16)
    for d in range(2):
        order = range(NBk) if d == 0 else range(NBk - 1, -1, -1)
        for bi, b in enumerate(order):
            t0 = b * TB
            usl = (lambda m: uT[:, m, t0:t0 + TB]) if d == 0 else (lambda m: uT[:, m, t0:t0 + TB][:, ::-1])
            py = [PS[4], PS[5]]
            cnt = [0, 0]
            for gp in range(8):
                dg = d * 8 + gp
                m = gp // 4
                er, ei, wr, wi = er_[gp % 2], ei_[gp % 2], wr_[gp % 2], wi_[gp % 2]
                m1, m2, n1, n2 = m1_[gp % 2], m2_[gp % 2], n1_[gp % 2], n2_[gp % 2]
                pr_, pi_ = PS[0 + (gp % 2) * 2], PS[1 + (gp % 2) * 2]
                p.mm(pr_, LB[:, dg, 0, :], usl(m))
                p.mm(pi_, LB[:, dg, 1, :], usl(m))
                cb_, sb_ = RC[:, dg, 0:TB], RS[:, dg, 0:TB]
                p.tt(m1, pr_, cb_, ALU.mult)
                p.tt(m2, pi_, sb_, ALU.mult)
                p.tt(er, m1, m2, ALU.add)
                p.tt(m1, pi_, cb_, ALU.mult)
                p.tt(m2, pr_, sb_, ALU.mult)
                p.tt(ei, m1, m2, ALU.subtract)
                rho_b = T["rho"][:, dg:dg + 1].bcast([128, TB])
                for (w_, e_, ci) in ((wr, er, 0), (wi, ei, 1)):
                    init = 0.0 if bi == 0 else carry.k(dg)[:, dg, ci:ci + 1]
                    oo, a_, b_ = w_.ap, rho_b.ap, e_.ap
                    ini = init if bi == 0 else init.ap
                    rd = [T["rho"], e_] + ([] if bi == 0 else [carry.k(dg)])
                    p.add("dve", (lambda e, oo=oo, a_=a_, b_=b_, ini=ini: e.tensor_tensor_scan(oo, a_, b_, ini, ALU.mult, ALU.add)),
                          rd, [w_])
                c5, s5 = RC[:, dg, TB:TB + 1], RS[:, dg, TB:TB + 1]
                cnew = cnew_[gp % 2]
                p.ts(cnew[:, 0:1], wr[:, TB - 1:TB], c5, None, ALU.mult)
                p.ts(cnew[:, 1:2], wi[:, TB - 1:TB], s5, None, ALU.mult)
                p.ts(cnew[:, 2:3], wr[:, TB - 1:TB], s5, None, ALU.mult)
                p.ts(cnew[:, 3:4], wi[:, TB - 1:TB], c5, None, ALU.mult)
                p.tt(carry.k(dg)[:, dg, 0:1], cnew[:, 0:1], cnew[:, 1:2], ALU.subtract)
                p.tt(carry.k(dg)[:, dg, 1:2], cnew[:, 2:3], cnew[:, 3:4], ALU.add)
                xr_, xi_ = xr[gp % 2], xi[gp % 2]
                p.tt(n1, wr, cb_, ALU.mult, eng="pool")
                p.tt(n2, wi, sb_, ALU.mult, eng="pool")
                p.tt(xr_, n1, n2, ALU.subtract, eng="pool")
                p.tt(n1, wi, cb_, ALU.mult, eng="pool")
                p.tt(n2, wr, sb_, ALU.mult, eng="pool")
                p.tt(xi_, n1, n2, ALU.add, eng="pool")
                xr_o = xr_ if d == 0 else xr_[:, ::-1]
                xi_o = xi_ if d == 0 else xi_[:, ::-1]
                p.mm(py[m], CT[:, dg, 0, :], xr_o, start=(cnt[m] == 0), stop=False)
                p.mm(py[m], CT[:, dg, 1, :], xi_o, start=False, stop=(cnt[m] == 3))
                cnt[m] += 1
            if d == 0:
                for m in range(2):
                    p.copy(yf[:, m, :], py[m], eng="act")
                p.dma("sp", C.YF.k(b)[:, t0:t0 + TB].re("(m q) t -> q m t", q=128), yf)
            else:
                p.dma("sp", yf, C.YF.k(b)[:, t0:t0 + TB].re("(m q) t -> q m t", q=128))
                for m in range(2):
                    p.tt(yv[:, m, :], py[m], yf[:, m, :], ALU.add)
                    p.stt(yv[:, m, :], uT[:, m, t0:t0 + TB], dsk[:, m:m + 1], yv[:, m, :], ALU.mult, ALU.add)
                    p.tt(g1, yv[:, m, :], yv[:, m, :], ALU.mult)
                    p.ts(g1, g1, 0.044715, 1.0, ALU.mult, ALU.add)
                    p.tt(g1, g1, yv[:, m, :], ALU.mult)
                    p.act(g2t, g1, AF.Sigmoid, scale=1.5957691216057308)
                    p.tt(zf[:, m, :], yv[:, m, :], g2t, ALU.mult)
                    p.copy(zb[:, m, :], zf[:, m, :], eng="pool")
                for oc in range(2):
                    pg = PS[6 + oc]
                    for c in range(2):
                        p.mm(pg, Wgl.k(c)[:, c, oc * 128:(oc + 1) * 128], zb[:, c, :], start=(c == 0), stop=(c == 1))
                    p.act(g2t, pg, AF.Sigmoid, bias=bgl[:, oc:oc + 1])
                    p.tt(dob[:, oc, :], zf[:, oc, :], g2t, ALU.mult)
                p.dma("sp", C.DoT.k(b)[:, t0:t0 + TB].re("(m q) t -> q m t", q=128), dob)
    p.flush()


NE, NO = 2, 2
SAME_ENG_SYNC = True
W_SHAPES = {
    "ffn1_w_gate": (4, 1024, 2816), "ffn1_w_up": (4, 1024, 2816), "ffn1_w_down": (4, 2816, 1024),
    "ffn2_w_gate": (4, 1024, 2816), "ffn2_w_up": (4, 1024, 2816), "ffn2_w_down": (4, 2816, 1024),
    "ab_w_in": (2, 1024, 1792), "ab_w_out": (2, 768, 1024),
    "cd_w_in": (2, 1024, 672), "cd_w_out": (2, 768, 1024),
    "c_w_q_up": (2, 256, 768), "c_w_kv_up": (2, 128, 1024), "d_w_glu": (2, 256, 256),
}


def aux_shapes(S):
    NT, S2 = S // 128, S // 128
    return {
        "ident": ((128, 128), "f"), "gcols": ((128, 12, 8), "f"), "pos": ((128, NT), "i"),
        "gqk": ((2, 128, 16, 64), "f"), "fBD": ((2, 128, 4, 128), "f"),
        "glat": ((2, 128, 384), "f"), "gq96": ((2, 128, 8, 96), "f"), "gk96": ((2, 128, 8, 96), "f"),
        "s5lre": ((2, 128, 16), "f"), "s5lim": ((2, 128, 16), "f"), "s5lst": ((2, 128, 16), "f"),
        "s5bre": ((2, 128, 16, 16), "f"), "s5bim": ((2, 128, 16, 16), "f"),
        "s5ctr": ((2, 128, 16, 128), "f"), "s5cti": ((2, 128, 16, 128), "f"),
        "s5dsk": ((2, 128, 2), "f"), "s5bgl": ((2, 128, 2), "f"),
        "iota": ((128, 513), "f"), "dmask": ((20, 128, 512), "f"),
        "fW1": ((128, 256), "f"), "fT": ((S2, 2, 128), "f"), "fW3": ((S2, 2, 2 * S2), "f"),
    }


def build_program(S, layers=(0, 1, 2, 3), stop_after=None):
    nc = bass.Bass("TRN2", target_bir_lowering=False)
    p = Prog(nc)
    p.same_eng_sync = SAME_ENG_SYNC
    C = Ctx()
    NT = S // 128
    xin = p.dram("x", [S, D], F32, kind="ExternalInput")
    for n, shp in W_SHAPES.items():
        setattr(C, n, p.dram(n, list(shp), F32, kind="ExternalInput"))
    for n, (shp, ty) in aux_shapes(S).items():
        setattr(C, n, p.dram(n, list(shp), F32 if ty == "f" else I32, kind="ExternalInput"))
    out = p.dram("out", [S, D], F32, kind="ExternalOutput")
    C.xres = out
    C.cosA = p.dram("cosA", [128, NT, 8], F32)
    C.sinA = p.dram("sinA", [128, NT, 8], F32)
    C.cosC = p.dram("cosC", [128, NT, 16], F32)
    C.sinC = p.dram("sinC", [128, NT, 16], F32)
    C.QT = p.dram("QT", [512, S], BF16)
    C.KT = p.dram("KT", [512, S], BF16)
    C.Vs = p.dram("Vs", [S, 512], BF16)
    C.Fs = p.dram("Fs", [S, 256], BF16)
    C.AoT = p.dram("AoT", [512, S], BF16)
    C.ZrT = p.dram("ZrT", [256, S], BF16)
    C.ZiT = p.dram("ZiT", [256, S], BF16)
    C.QTc = p.dram("QTc", [8, 96, S], BF16)
    C.KTc = p.dram("KTc", [8, 96, S], BF16)
    C.UT = p.dram("UT", [256, S], BF16)
    C.YF = p.dram("YF", [256, S], F32)
    C.DoT = p.dram("DoT", [256, S], BF16)
    idf = p.sb("idf", [128, 128], F32, glob=True)
    C.ident_f = idf
    C.ident_b = p.sb("idb", [128, 128], BF16, glob=True)
    gcols = p.sb("gcols", [128, 12, 8], F32, glob=True)
    p.dma("sp", idf, C.ident)
    p.copy(C.ident_b, idf)
    p.dma("sp", gcols, C.gcols)
    posi = p.sb("posi", [128, NT], I32)
    posf = p.sb("posf", [128, NT], F32)
    p.dma("sp", posi, C.pos)
    p.copy(posf, posi)
    for (nf, rot, dc, ds) in ((8, 16, C.cosA, C.sinA), (16, 32, C.cosC, C.sinC)):
        ang = p.sb(f"ang{nf}", [128, NT, nf], F32)
        t1 = p.sb(f"rt1{nf}", [128, NT, nf], F32)
        t2 = p.sb(f"rt2{nf}", [128, NT, nf], F32)
        ti = p.sb(f"rti{nf}", [128, NT, nf], I32)
        so = p.sb(f"rso{nf}", [128, NT, nf], F32)
        co = p.sb(f"rco{nf}", [128, NT, nf], F32)
        invf = (500000.0 ** (-np.arange(0, rot, 2, dtype=np.float32) / np.float32(rot))).astype(np.float32)
        for f in range(nf):
            p.ts(ang[:, :, f], posf, float(invf[f]), None, ALU.mult)
        sincos(p, so, co, ang, t1, t2, ti)
        p.dma("sp", dc, co)
        p.dma("sp", ds, so)
    p.flush()
    first = True
    for L in layers:
        i = L // 2
        src = xin if first else out
        first = False
        ffn_phase(p, C, src, out, C.ffn1_w_gate[L], C.ffn1_w_up[L], C.ffn1_w_down[L], gcols[:, L * 3 + 0, :], S)
        if stop_after == (L, 0):
            break
        gmix = gcols[:, L * 3 + 1, :]
        if L % 2 == 0:
            even_proj_phase(p, C, i, gmix, S)
            attention_phase(p, C, S, 8, 64, lambda h: C.QT[h * 64:(h + 1) * 64, :], lambda h: C.KT[h * 64:(h + 1) * 64, :],
                            C.Vs, C.AoT, True)
            fnet_phase(p, C, S)
            outproj_phase(p, C, S, C.ab_w_out[i], [(C.AoT, 4), (C.ZrT, 2), (C.ZiT, 2)], fold=i)
        else:
            odd_proj_phase(p, C, i, gmix, S)
            attention_phase(p, C, S, 8, 96, lambda h: C.QTc[h], lambda h: C.KTc[h], C.Vs, C.AoT, False)
            s5_phase(p, C, i, S)
            outproj_phase(p, C, S, C.cd_w_out[i], [(C.AoT, 4), (C.DoT, 2)])
        if stop_after == (L, 1):
            break
        ffn_phase(p, C, out, out, C.ffn2_w_gate[L], C.ffn2_w_up[L], C.ffn2_w_down[L], gcols[:, L * 3 + 2, :], S)
    p.finish()
    return nc, p


def host_aux(inp, S, positions_row):
    f32 = np.float32
    NT = S // 128
    S2 = S // 128
    a = {}
    a["ident"] = np.eye(128, dtype=f32)
    g = np.zeros((128, 12, 8), f32)
    for L in range(4):
        for n, nm in enumerate(("ffn1_norm", "mix_norm", "ffn2_norm")):
            g[:, L * 3 + n, :] = np.asarray(inp[nm][L], f32).reshape(8, 128).T
    a["gcols"] = g
    a["pos"] = np.ascontiguousarray(np.asarray(positions_row, np.int32).reshape(NT, 128).T)
    gqk = np.zeros((2, 128, 16, 64), f32)
    fBD = np.zeros((2, 128, 4, 128), f32)
    cidx = np.arange(64)
    Cc = np.cos(2 * np.pi * np.outer(cidx, cidx) / 64).astype(f32)
    Sc = np.sin(2 * np.pi * np.outer(cidx, cidx) / 64).astype(f32)
    for i in range(2):
        gqk[i, :, 0:8, :] = np.asarray(inp["a_q_norm"][i], f32)[None, None, :]
        gqk[i, :, 8:16, :] = np.asarray(inp["a_k_norm"][i], f32)[None, None, :]
        bm = np.asarray(inp["b_w_mix"][i], f32)
        for pr in range(2):
            for g2 in range(2):
                fBD[i, g2 * 64:(g2 + 1) * 64, pr, g2 * 64:(g2 + 1) * 64] = bm[pr * 2 + g2].T
        for g2 in range(2):
            fBD[i, g2 * 64:(g2 + 1) * 64, 2, g2 * 64:(g2 + 1) * 64] = Cc
            fBD[i, g2 * 64:(g2 + 1) * 64, 3, g2 * 64:(g2 + 1) * 64] = Sc
    a["gqk"], a["fBD"] = gqk, fBD
    glat = np.zeros((2, 128, 384), f32)
    gq96 = np.zeros((2, 128, 8, 96), f32)
    gk96 = np.zeros((2, 128, 8, 96), f32)
    for i in range(2):
        glat[i, :, 0:256] = np.asarray(inp["c_q_lat_norm"][i], f32)[None, :]
        glat[i, :, 256:384] = np.asarray(inp["c_kv_lat_norm"][i], f32)[None, :]
        gq96[i] = np.asarray(inp["c_q_norm"][i], f32)[None, None, :]
        gk96[i] = np.asarray(inp["c_k_norm"][i], f32)[None, None, :]
    a["glat"], a["gq96"], a["gk96"] = glat, gq96, gk96

    def sl(arr):
        return np.ascontiguousarray(np.asarray(arr, f32).reshape(2, 8, 2, 64).transpose(2, 3, 0, 1).reshape(128, 16))
    a["s5lre"] = np.stack([sl(inp["d_lam_re"][i]) for i in range(2)])
    a["s5lim"] = np.stack([sl(inp["d_lam_im"][i]) for i in range(2)])
    a["s5lst"] = np.stack([sl(np.broadcast_to(np.asarray(inp["d_log_step"][i], f32)[:, :, None], (2, 16, 64))) for i in range(2)])

    def slb(arr):
        return np.ascontiguousarray(np.asarray(arr, f32).reshape(2, 8, 2, 64, 16).transpose(2, 3, 0, 1, 4).reshape(128, 16, 16))
    a["s5bre"] = np.stack([slb(inp["d_b_re"][i]) for i in range(2)])
    a["s5bim"] = np.stack([slb(inp["d_b_im"][i]) for i in range(2)])

    def slc(arr):
        arr = np.asarray(arr, f32)
        o = np.zeros((128, 16, 128), f32)
        for d in range(2):
            for gp in range(8):
                for g2 in range(2):
                    gg = gp * 2 + g2
                    c0 = ((gp % 4) * 2 + g2) * 16
                    o[g2 * 64:(g2 + 1) * 64, d * 8 + gp, c0:c0 + 16] = arr[d, gg].T
        return o
    a["s5ctr"] = np.stack([slc(inp["d_c_re"][i]) for i in range(2)])
    a["s5cti"] = np.stack([slc(inp["d_c_im"][i]) for i in range(2)])
    a["s5dsk"] = np.stack([np.asarray(inp["d_skip"][i], f32).reshape(2, 128).T for i in range(2)])
    a["s5bgl"] = np.stack([np.asarray(inp["d_b_glu"][i], f32).reshape(2, 128).T for i in range(2)])
    a["iota"] = np.broadcast_to(np.arange(513, dtype=f32)[None, :], (128, 513)).copy()
    ki = np.arange(128)[:, None]
    qi = np.arange(512)[None, :]
    dm = np.zeros((20, 128, 512), f32)
    for o in range(20):
        dl = (o * 128 - 1024) + ki - qi
        m = np.zeros_like(dl, dtype=f32)
        for dil in (1, 4, 16):
            m += ((dl % dil == 0) & (np.abs(dl) <= 64 * dil)).astype(f32)
        dm[o] = m
    a["dmask"] = dm
    s1 = np.arange(128)
    k1 = np.arange(128)
    ang = 2 * np.pi * np.outer(s1, k1) / 128
    nrm = 1.0 / np.sqrt(S * 64.0)
    a["fW1"] = (np.concatenate([np.cos(ang), -np.sin(ang)], axis=1) * nrm).astype(f32)
    s2 = np.arange(S2)
    phi = 2 * np.pi * np.outer(s2, k1) / S
    a["fT"] = np.stack([np.cos(phi), np.sin(phi)], axis=1).astype(f32)
    th = 2 * np.pi * np.outer(s2, np.arange(S2)) / S2
    a["fW3"] = np.stack([np.concatenate([np.cos(th), -np.sin(th)], 1), np.concatenate([np.sin(th), np.cos(th)], 1)], axis=1).astype(f32)
    for k in a:
        a[k] = np.ascontiguousarray(a[k])
    return a


def run_module(inp, S, layers=(0, 1, 2, 3), stop_after=None, trace=False):
    x = np.asarray(inp["x"], np.float32)
    B = x.shape[0]
    nc, p = build_program(S, layers, stop_after)
    base = {n: np.ascontiguousarray(np.asarray(inp[n], np.float32)) for n in W_SHAPES}
    in_maps = []
    for b in range(B):
        m = dict(base)
        m.update(host_aux(inp, S, np.asarray(inp["positions"])[b]))
        m["x"] = np.ascontiguousarray(x[b])
        in_maps.append(m)
    res = run_bass_kernel_spmd(nc, in_maps, core_ids=list(range(B)), trace=trace)
    return np.stack([r["out"] for r in res.results], axis=0), res, p


def kernel(**inputs):
    out, _, _ = run_module(inputs, 8192)
    return out.astype(np.float32)
```

```python
import numpy as np
from contextlib import ExitStack
import concourse.bass as bass
import concourse.mybir as mybir

F32 = mybir.dt.float32
BF16 = mybir.dt.bfloat16
I32 = mybir.dt.int32
AF = mybir.ActivationFunctionType
ALU = mybir.AluOpType
AX = mybir.AxisListType

EPOCH = 30000
N_DMA_SEMS = 40


class V:
    __slots__ = ("ap", "key")

    def __init__(self, ap, key):
        self.ap = ap
        self.key = key

    def __getitem__(self, idx):
        return V(self.ap[idx], self.key)

    def k(self, sub):
        return V(self.ap, (self.key, sub))

    def re(self, s, **kw):
        return V(self.ap.rearrange(s, **kw), self.key)

    def bc(self, dt):
        return V(self.ap.bitcast(dt), self.key)

    def bcast(self, shape):
        return V(self.ap.to_broadcast(list(shape)), self.key)

    def ub(self, axis, shape):
        return V(self.ap.unsqueeze(axis).to_broadcast(list(shape)), self.key)


class Op:
    __slots__ = ("eng", "emit", "deps", "sig", "waits", "is_dma", "dsem", "dval", "needs_sig", "n")


class Prog:
    COMPUTE = ("pe", "act", "dve", "pool")

    def __init__(self, nc):
        self.nc = nc
        self.ops = []
        self.last_w = {}
        self.rd_eng = {}
        self.rd_dma = {}
        self.es = ExitStack()
        self.same_eng_sync = True
        self._n = 0
        self.gs = ExitStack()
        self.NEP = {"pe": 6, "act": 6, "dve": 8, "pool": 3}
        self.sems = {e: [self.gs.enter_context(nc.semaphore(f"s_{e}_{i}")) for i in range(self.NEP[e])]
                     for e in self.COMPUTE}
        self.dsems = [self.gs.enter_context(nc.semaphore(f"s_dma_{i}")) for i in range(N_DMA_SEMS)]
        self.cnt = {e: 0 for e in self.COMPUTE}
        self.dval = [0] * N_DMA_SEMS
        self.rot = 0
        self.know = {}
        self.bar = {}
        self.stats = {e: 0 for e in ("pe", "act", "dve", "pool", "sp")}

    def sb(self, name, shape, dt, glob=False):
        self._u = getattr(self, "_u", 0) + 1
        name = f"{name}_u{self._u}"
        t = (self.gs if glob else self.es).enter_context(self.nc.sbuf_tensor(name, list(shape), dt))
        return V(t[:], name)

    def ps(self, name, shape, dt):
        self._u = getattr(self, "_u", 0) + 1
        name = f"{name}_u{self._u}"
        t = self.es.enter_context(self.nc.psum_tensor(name, list(shape), dt))
        return V(t[:], name)

    def dram(self, name, shape, dt, kind="Internal"):
        t = self.nc.dram_tensor(name, list(shape), dt, kind=kind)
        return V(t.ap(), name)

    def add(self, eng, emit, reads=(), writes=(), is_dma=False):
        op = Op()
        op.eng = eng
        op.emit = emit
        op.is_dma = is_dma
        op.sig = None
        op.needs_sig = False
        op.waits = None
        op.dsem = None
        op.dval = None
        op.n = self._n
        self._n += 1
        deps = {}
        rk = [v.key if isinstance(v, V) else v for v in reads]
        wk = [v.key if isinstance(v, V) else v for v in writes]
        for k in rk:
            w = self.last_w.get(k)
            if w is not None:
                deps[w.n] = w
        for k in wk:
            w = self.last_w.get(k)
            if w is not None:
                deps[w.n] = w
            for r in self.rd_eng.get(k, {}).values():
                deps[r.n] = r
            for r in self.rd_dma.get(k, ()):
                deps[r.n] = r
        for k in wk:
            self.last_w[k] = op
            self.rd_eng[k] = {}
            self.rd_dma[k] = []
        for k in rk:
            if k in wk:
                continue
            if is_dma:
                self.rd_dma.setdefault(k, []).append(op)
            else:
                self.rd_eng.setdefault(k, {})[eng] = op
        deps.pop(op.n, None)
        op.deps = list(deps.values())
        self.ops.append(op)
        return op

    def dma(self, q, out, in_, extra_reads=(), extra_writes=()):
        o, i = out.ap, in_.ap
        return self.add(q, lambda e: e.dma_start(out=o, in_=i), [in_] + list(extra_reads),
                        [out] + list(extra_writes), is_dma=True)

    def mm(self, out, lhsT, rhs, start=True, stop=True, **kw):
        o, l, r = out.ap, lhsT.ap, rhs.ap
        rd = [lhsT, rhs] + ([] if start else [out])
        return self.add("pe", lambda e: e.matmul(o, l, r, start=start, stop=stop, **kw), rd, [out])

    def tr(self, out, in_, ident):
        o, i, d = out.ap, in_.ap, ident.ap
        return self.add("pe", lambda e: e.transpose(o, i, d), [in_, ident], [out])

    def act(self, out, in_, func, scale=1.0, bias=0.0, eng="act", accum=None):
        o, i = out.ap, in_.ap
        rd = [in_]
        wr = [out]
        kw = {}
        if isinstance(scale, V):
            rd.append(scale)
            kw["scale"] = scale.ap
        else:
            kw["scale"] = scale
        if isinstance(bias, V):
            rd.append(bias)
            kw["bias"] = bias.ap
        else:
            kw["bias"] = bias
        if accum is not None:
            kw["accum_out"] = accum.ap
            wr.append(accum)
        return self.add(eng, lambda e: e.activation(o, i, func, **kw), rd, wr)

    def tt(self, out, in0, in1, op, eng="dve"):
        o, a, b = out.ap, in0.ap, in1.ap
        return self.add(eng, lambda e: e.tensor_tensor(o, a, b, op), [in0, in1], [out])

    def ts(self, out, in0, s1, s2=None, op0=ALU.mult, op1=None, eng="dve"):
        o, a = out.ap, in0.ap
        rd = [in0]
        a1 = s1
        a2 = s2
        if isinstance(s1, V):
            rd.append(s1)
            a1 = s1.ap
        if isinstance(s2, V):
            rd.append(s2)
            a2 = s2.ap
        if op1 is None:
            return self.add(eng, lambda e: e.tensor_scalar(o, a, a1, None, op0), rd, [out])
        return self.add(eng, lambda e: e.tensor_scalar(o, a, a1, a2, op0, op1), rd, [out])

    def stt(self, out, in0, scalar, in1, op0, op1):
        o, a, b = out.ap, in0.ap, in1.ap
        rd = [in0, in1]
        s = scalar
        if isinstance(scalar, V):
            rd.append(scalar)
            s = scalar.ap
        return self.add("dve", lambda e: e.scalar_tensor_tensor(o, a, s, b, op0, op1), rd, [out])

    def copy(self, out, in_, eng="dve"):
        o, i = out.ap, in_.ap
        if eng == "act":
            return self.add(eng, lambda e: e.copy(o, i), [in_], [out])
        return self.add(eng, lambda e: e.tensor_copy(o, i), [in_], [out])

    def memset(self, out, val, eng="dve"):
        o = out.ap
        return self.add(eng, lambda e: e.memset(o, val), [], [out])

    def reduce(self, out, in_, op=ALU.add, axis=AX.X, eng="dve"):
        o, i = out.ap, in_.ap
        return self.add(eng, lambda e: e.tensor_reduce(o, i, axis, op), [in_], [out])

    def recip(self, out, in_):
        o, i = out.ap, in_.ap
        return self.add("dve", lambda e: e.reciprocal(o, i), [in_], [out])

    def _need_wait(self, op, d):
        if d.is_dma:
            return True
        if d.eng != op.eng:
            return True
        if op.is_dma:
            return True
        if op.eng == "pe":
            return False
        if op.eng == "pool":
            return True
        return self.same_eng_sync

    def _semval(self, key, val):
        if isinstance(key, tuple):
            return (self.dsems[key[1]], val)
        sig = val - 1
        assert sig // EPOCH < self.NEP[key], ("too many signals", key, sig)
        return (self.sems[key][sig // EPOCH], sig % EPOCH + 1)

    def flush(self):
        nc = self.nc
        ops = self.ops
        last = {}
        for op in ops:
            for d in op.deps:
                if self._need_wait(op, d):
                    d.needs_sig = True
            if not op.is_dma:
                last[op.eng] = op
        for op in last.values():
            op.needs_sig = True
        for op in ops:
            if not op.is_dma and op.needs_sig:
                op.sig = self.cnt[op.eng]
                self.cnt[op.eng] += 1
        seen_first = set()
        for op in ops:
            K = self.know.setdefault(op.eng, {})
            need = {}
            if op.eng not in seen_first:
                seen_first.add(op.eng)
                for k, v in self.bar.pop(op.eng, {}).items():
                    need[k] = max(need.get(k, 0), v)
            if op.is_dma:
                s = self.rot
                self.rot = (self.rot + 1) % N_DMA_SEMS
                if self.dval[s] > 0:
                    need[("d", s)] = max(need.get(("d", s), 0), self.dval[s])
                self.dval[s] += 16
                op.dsem = s
                op.dval = self.dval[s]
            for d in op.deps:
                if not self._need_wait(op, d):
                    continue
                if d.is_dma:
                    key, val = ("d", d.dsem), d.dval
                else:
                    key, val = d.eng, d.sig + 1
                if need.get(key, 0) < val:
                    need[key] = val
            waits = []
            for key, val in need.items():
                if K.get(key, 0) >= val:
                    continue
                K[key] = val
                waits.append(self._semval(key, val))
            op.waits = waits
            self.stats[op.eng] += 1
        by = {}
        for op in ops:
            by.setdefault(op.eng, []).append(op)
        sems, dsems = self.sems, self.dsems

        def run(engname, e):
            for op in by.get(engname, []):
                for (s, v) in op.waits:
                    e.wait_ge(s, v)
                ins = op.emit(e)
                if op.is_dma:
                    ins.then_inc(dsems[op.dsem], 16)
                elif op.sig is not None:
                    ins.then_inc(sems[op.eng][op.sig // EPOCH], 1)

        with nc.Block() as block:
            @block.sync
            def _(e):
                run("sp", e)

            @block.tensor
            def _(e):
                run("pe", e)

            @block.scalar
            def _(e):
                run("act", e)

            @block.vector
            def _(e):
                run("dve", e)

            @block.gpsimd
            def _(e):
                run("pool", e)
        front = {}
        for e in self.COMPUTE:
            if self.cnt[e] > 0:
                front[e] = self.cnt[e]
        for s in range(N_DMA_SEMS):
            if self.dval[s] > 0:
                front[("d", s)] = self.dval[s]
        self.bar = {e: dict(front) for e in ("pe", "act", "dve", "pool", "sp")}
        self.ops = []
        self.last_w = {}
        self.rd_eng = {}
        self.rd_dma = {}
        self.es.close()
        self.es = ExitStack()

    def finish(self):
        nc = self.nc
        self.flush()
        need = self.bar["sp"]
        K = self.know.setdefault("sp", {})
        dsems = self.dsems
        waits = [self._semval(k, v) for k, v in need.items() if K.get(k, 0) < v]
        with nc.Block() as block:
            @block.sync
            def _(e):
                for (s, v) in waits:
                    e.wait_ge(s, v)

from concourse.bass_utils import run_bass_kernel_spmd
import math

D = 1024
DFF = 2816
EPS = 1e-6
TWO_PI = 2.0 * math.pi
CW1 = 6.28125
CW2 = TWO_PI - 6.28125


class Ctx:
    pass


def make_psum(p):
    return [p.ps(f"psb{i}", [128, 512], F32) for i in range(8)]


def sincos(p, out_sin, out_cos, ang, t1, t2, ti):
    p.ts(t1, ang, 1.0 / TWO_PI, None, ALU.mult)
    p.copy(ti, t1)
    p.copy(t1, ti)
    p.stt(t2, t1, -CW1, ang, ALU.mult, ALU.add)
    p.stt(t2, t1, -CW2, t2, ALU.mult, ALU.add)
    p.ts(t1, t2, 3.1415925, -3.1415925, ALU.min, ALU.max)
    p.act(out_sin, t1, AF.Sin)
    p.ts(t1, t2, math.pi / 2, None, ALU.add)
    p.ts(ti.bc(F32), t1, math.pi, None, ALU.is_gt)
    p.stt(t1, ti.bc(F32), -TWO_PI, t1, ALU.mult, ALU.add)
    p.ts(t1, t1, 3.1415925, -3.1415925, ALU.min, ALU.max)
    p.act(out_cos, t1, AF.Sin)


def load_w_bf16(p, dst, src, rows, cols, stage, colchunk, tagi=[0]):
    nrc = rows // 128
    engs = ("dve", "pool", "act")
    for c in range(nrc):
        for c0 in range(0, cols, colchunk):
            w = min(colchunk, cols - c0)
            i = tagi[0]
            tagi[0] += 1
            st = stage.k(i % 2)[:, (i % 2), 0:w]
            p.dma("sp", st, src[c * 128:(c + 1) * 128, c0:c0 + w])
            p.copy(dst.k(c)[:, c, c0:c0 + w], st, eng=engs[i % 3])


def norm_T(p, C, xb, nt, gcol, hT, xn, ss, pst):
    for j in range(nt):
        p.act(xn[:, j, :], xb[:, j, :], AF.Square, accum=ss[:, j:j + 1])
    p.ts(ss[:, 4:4 + nt], ss[:, 0:nt], 1.0 / D, EPS, ALU.mult, ALU.add)
    p.act(ss[:, 4:4 + nt], ss[:, 4:4 + nt], AF.Sqrt)
    p.recip(ss[:, 8:8 + nt], ss[:, 4:4 + nt])
    for j in range(nt):
        p.ts(xn[:, j, :], xb[:, j, :], ss[:, 8 + j:9 + j], None, ALU.mult)
    for c in range(8):
        ps = pst[c % 2]
        pv = ps.bc(BF16)
        for j in range(nt):
            p.tr(pv[:, j * 128:(j + 1) * 128], xn[:, j, c * 128:(c + 1) * 128], C.ident_b)
        p.act(hT[:, c, 0:nt * 128], pv[:, 0:nt * 128], AF.Copy, scale=gcol[:, c:c + 1],
              eng=("act" if c % 2 == 0 else "act"))


def ffn_phase(p, C, x_src, x_dst, wg, wu, wd, gcol, S, TB=512):
    nt = TB // 128
    NF = DFF // 128
    Wg = p.sb("Wg", [128, 8, DFF], BF16)
    Wu = p.sb("Wu", [128, 8, DFF], BF16)
    Wd = p.sb("Wd", [128, NF, D], BF16)
    stage = p.sb("stage", [128, 2, 704], F32)
    xbs = [p.sb(f"xb{i}", [128, nt, D], F32) for i in range(2)]
    xn = p.sb("xn", [128, 2, D], BF16)
    hT = p.sb("hT", [128, 8, TB], BF16)
    actT = p.sb("actT", [128, NF, TB], BF16)
    sg = [p.sb(f"sg{i}", [128, TB], BF16) for i in range(2)]
    ss = p.sb("ss", [128, 12], F32)
    PS = make_psum(p)
    load_w_bf16(p, Wg, wg, D, DFF, stage, 704)
    load_w_bf16(p, Wu, wu, D, DFF, stage, 704)
    load_w_bf16(p, Wd, wd, DFF, D, stage, 512)
    for b in range(S // TB):
        xb = xbs[b % 2]
        rows = x_src.k(b)[b * TB:(b + 1) * TB, :].re("(j q) d -> q j d", q=128)
        p.dma("sp", xb, rows)
        for hh in range(nt // 2):
            norm_T(p, C, xb[:, 2 * hh:2 * hh + 2], 2, gcol, hT[:, :, hh * 256:(hh + 1) * 256], xn, ss, PS[0:2])
        for f in range(NF):
            pg = PS[2 + (f % 2)]
            pu = PS[4 + (f % 2)]
            for c in range(8):
                p.mm(pg[:, 0:TB], Wg.k(c)[:, c, f * 128:(f + 1) * 128], hT[:, c, :], start=(c == 0), stop=(c == 7))
            for c in range(8):
                p.mm(pu[:, 0:TB], Wu.k(c)[:, c, f * 128:(f + 1) * 128], hT[:, c, :], start=(c == 0), stop=(c == 7))
            s = sg[f % 2]
            p.act(s, pg[:, 0:TB], AF.Silu)
            p.tt(actT.k(f)[:, f, :], s, pu[:, 0:TB], ALU.mult)
        for j in range(nt):
            for dh in range(2):
                po = PS[6 + ((j * 2 + dh) % 2)]
                for f in range(NF):
                    p.mm(po, actT.k(f)[:, f, j * 128:(j + 1) * 128], Wd.k(f)[:, f, dh * 512:(dh + 1) * 512],
                         start=(f == 0), stop=(f == NF - 1))
                p.stt(xb[:, j, dh * 512:(dh + 1) * 512], po, 0.5, xb[:, j, dh * 512:(dh + 1) * 512], ALU.mult, ALU.add)
        p.dma("sp", x_dst.k(b)[b * TB:(b + 1) * TB, :].re("(j q) d -> q j d", q=128), xb)
    p.flush()


def load_block(p, C, x, b, TB, xb):
    p.dma("sp", xb, x.k(("r", b))[b * TB:(b + 1) * TB, :].re("(j q) d -> q j d", q=128))


def rot_pair(p, out3, in3, lo, half, cosb, sinb, ra, rb):
    t1 = in3[:, :, lo:lo + half]
    t2 = in3[:, :, lo + half:lo + 2 * half]
    p.tt(ra, t1, cosb, ALU.mult)
    p.tt(rb, t2, sinb, ALU.mult)
    p.tt(out3[:, :, lo:lo + half], ra, rb, ALU.subtract)
    p.tt(ra, t2, cosb, ALU.mult)
    p.tt(rb, t1, sinb, ALU.mult)
    p.tt(out3[:, :, lo + half:lo + 2 * half], ra, rb, ALU.add)


def rsqrt_mean(p, out, ssum, n, tmp):
    p.ts(tmp, ssum, 1.0 / n, EPS, ALU.mult, ALU.add)
    p.act(tmp, tmp, AF.Sqrt)
    p.recip(out, tmp)


def even_proj_phase(p, C, i, gcol, S):
    TB, nt = 512, 4
    Win = p.sb("Win", [128, 8, 1792], BF16)
    stage = p.sb("stage", [128, 2, 1792], F32)
    load_w_bf16(p, Win, C.ab_w_in[i], D, 1792, stage, 1792)
    gqk = p.sb("gqk", [128, 16, 64], F32)
    p.dma("sp", gqk, C.gqk[i])
    p.ts(gqk[:, 0:8, :], gqk[:, 0:8, :], 0.125, None, ALU.mult)
    xbs = [p.sb(f"xb{k}", [128, nt, D], F32) for k in range(2)]
    xn = p.sb("xn", [128, nt, D], BF16)
    hT = p.sb("hT", [128, 8, TB], BF16)
    ss = p.sb("ss", [128, 12], F32)
    cs = p.sb("cs", [128, 2, nt, 8], F32)
    sq = p.sb("sq", [128, 1024], F32)
    qn = p.sb("qn", [128, 16, 64], F32)
    qb = p.sb("qb", [128, 16, 64], BF16)
    ssq = p.sb("ssq", [128, 16], F32)
    rs = p.sb("rs", [128, 16], F32)
    ra = p.sb("ra", [128, 16, 8], F32)
    rb = p.sb("rb", [128, 16, 8], F32)
    qkT = p.sb("qkT", [128, 8, TB], BF16)
    vb = p.sb("vb", [128, nt, 512], BF16)
    fb = p.sb("fb", [128, nt, 256], BF16)
    PS = make_psum(p)
    for b in range(S // TB):
        xb = xbs[b % 2]
        load_block(p, C, C.xres, b, TB, xb)
        p.dma("sp", cs[:, 0], C.cosA[:, b * nt:(b + 1) * nt, :])
        p.dma("sp", cs[:, 1], C.sinA[:, b * nt:(b + 1) * nt, :])
        norm_T(p, C, xb, nt, gcol, hT, xn, ss, PS[0:2])
        for j in range(nt):
            pq, pk, pv, pf = PS[2], PS[3], PS[4], PS[5]
            for (ps, c0, w) in ((pq, 0, 512), (pk, 512, 512), (pv, 1024, 512), (pf, 1536, 256)):
                for c in range(8):
                    p.mm(ps[:, 0:w], hT[:, c, j * 128:(j + 1) * 128], Win.k(c)[:, c, c0:c0 + w],
                         start=(c == 0), stop=(c == 7))
            p.copy(vb[:, j, :], pv, eng="act")
            p.copy(fb[:, j, :], pf[:, 0:256], eng="act")
            p.act(sq[:, 0:512], pq, AF.Square)
            p.act(sq[:, 512:1024], pk, AF.Square)
            p.reduce(ssq, sq.re("p (h e) -> p h e", e=64))
            rsqrt_mean(p, rs, ssq, 64, ssq)
            p.tt(qn[:, 0:8, :], pq.re("p (h e) -> p h e", e=64), rs[:, 0:8].ub(2, [128, 8, 64]), ALU.mult)
            p.tt(qn[:, 8:16, :], pk.re("p (h e) -> p h e", e=64), rs[:, 8:16].ub(2, [128, 8, 64]), ALU.mult)
            p.tt(qn, qn, gqk, ALU.mult)
            cosb = cs[:, 0, j, :].ub(1, [128, 16, 8])
            sinb = cs[:, 1, j, :].ub(1, [128, 16, 8])
            rot_pair(p, qb, qn, 0, 8, cosb, sinb, ra, rb)
            p.copy(qb[:, :, 16:64], qn[:, :, 16:64], eng="pool")
            pt = PS[6 + (j % 2)].bc(BF16)
            qbf = qb.re("p h e -> p (h e)")
            for c in range(8):
                p.tr(pt[:, c * 128:(c + 1) * 128], qbf[:, c * 128:(c + 1) * 128], C.ident_b)
            p.copy(qkT[:, :, j * 128:(j + 1) * 128], pt.re("p (c t) -> p c t", t=128), eng="act")
        sl = slice(b * TB, (b + 1) * TB)
        p.dma("sp", C.QT.k(b)[:, sl].re("(c q) t -> q c t", q=128), qkT[:, 0:4, :])
        p.dma("sp", C.KT.k(b)[:, sl].re("(c q) t -> q c t", q=128), qkT[:, 4:8, :])
        p.dma("sp", C.Vs.k(b)[sl, :].re("(j q) e -> q j e", q=128), vb)
        p.dma("sp", C.Fs.k(b)[sl, :].re("(j q) e -> q j e", q=128), fb)
    p.flush()


def attention_phase(p, C, S, nheads, dk, qsrc, ksrc, Vs, OT, masked):
    NB, NQ = S // 128, S // 512
    qt = [p.sb(f"qt{k}", [128, S], BF16) for k in range(2)]
    kt = [p.sb(f"kt{k}", [128, S], BF16) for k in range(2)]
    vx = [p.sb(f"vx{k}", [128, NB, 128], BF16) for k in range(2)]
    pex = [p.sb(f"pex{k}", [128, 512], BF16) for k in range(4)]
    den = p.sb("den", [128, 512], F32)
    rden = p.sb("rden", [64, 512], F32)
    ot = [p.sb(f"ot{k}", [64, 512], BF16) for k in range(2)]
    onesf = p.sb("onesf", [128, 64], F32)
    p.memset(onesf, 1.0)
    for k in range(2):
        p.memset(vx[k][:, :, 64:128], 1.0)
    KD = dk
    if dk == 64:
        KD = 128
        for k in range(2):
            p.memset(qt[k][64:128, :], 0.0, eng="pool")
            p.memset(kt[k][64:128, :], 0.0, eng="pool")
    if masked:
        mk = p.sb("mk", [128, 20, 512], BF16)
        mst = p.sb("mst", [128, 2, 512], F32)
        for o in range(20):
            p.dma("sp", mst.k(o % 2)[:, o % 2, :], C.dmask[o])
            p.copy(mk.k(o)[:, o, :], mst.k(o % 2)[:, o % 2, :], eng=("dve" if o % 2 else "pool"))
    PS = make_psum(p)
    NBUF, LA = 4, 2
    tiles = []
    for h in range(nheads):
        for qb in range(NQ):
            q0 = qb * 512
            if masked:
                kbs = [kb for kb in range(NB) if -1024 <= kb * 128 - q0 <= 1408]
            else:
                kbs = list(range(NB))
            for idx, kb in enumerate(kbs):
                tiles.append((h, qb, kb, idx == 0, idx == len(kbs) - 1))
    n = len(tiles)
    loaded = set()
    for t in range(n + LA):
        if t < n:
            h, qb, kb, first, last = tiles[t]
            q_, k_, v_ = qt[h % 2], kt[h % 2], vx[h % 2]
            if h not in loaded:
                loaded.add(h)
                p.dma("sp", q_[0:dk, :], qsrc(h))
                p.dma("sp", k_[0:dk, :], ksrc(h))
                p.dma("sp", v_[:, :, 0:64], Vs[:, h * 64:(h + 1) * 64].re("(n q) e -> q n e", q=128))
            q0 = qb * 512
            ps = PS[t % NBUF]
            pe_ = pex[t % NBUF]
            p.mm(ps, k_[0:KD, kb * 128:(kb + 1) * 128], q_[0:KD, q0:q0 + 512])
            p.act(pe_, ps, AF.Exp)
            if masked:
                o = (kb * 128 - q0 + 1024) // 128
                p.tt(pe_, pe_, mk.k(o)[:, o, :], ALU.mult)
        u = t - LA
        if u >= 0:
            h, qb, kb, first, last = tiles[u]
            v_ = vx[h % 2]
            q0 = qb * 512
            po = PS[4 + (qb % 2)]
            p.mm(po, v_[:, kb, :], pex[u % NBUF], start=first, stop=last)
            if last:
                p.copy(den[64:65, :], po[64:65, :], eng="act")
                pb = PS[6 + (qb % 2)]
                p.mm(pb[0:64, :], onesf[64:65, 0:64], den[64:65, :])
                p.recip(rden, pb[0:64, :])
                o_ = ot[qb % 2]
                p.tt(o_, po[0:64, :], rden, ALU.mult)
                p.dma("sp", OT.k((h, qb))[h * 64:(h + 1) * 64, q0:q0 + 512], o_)
    p.flush()


def fnet_phase(p, C, S):
    S2 = S // 128
    ya = p.sb("ya", [128, S2, 256], BF16)
    p.dma("sp", ya.re("p a c -> p (a c)"), C.Fs.re("(p a) c -> p (a c)", p=128))
    w1f = p.sb("w1f", [128, 256], F32)
    w1 = p.sb("w1", [128, 256], BF16)
    p.dma("sp", w1f, C.fW1)
    p.copy(w1, w1f)
    tw = p.sb("tw", [S2, 2, 128], F32)
    p.dma("sp", tw, C.fT)
    w3f = p.sb("w3f", [S2, 2, 2 * S2], F32)
    w3 = p.sb("w3", [S2, 2, 2 * S2], BF16)
    p.dma("sp", w3f, C.fW3)
    p.copy(w3, w3f)
    PS = make_psum(p)
    ta = [p.sb(f"ta{k}", [S2, 2, 128], F32) for k in range(4)]
    z2r = p.sb("z2r", [S2, 128, 128], BF16)
    z2i = p.sb("z2i", [S2, 128, 128], BF16)
    zt = p.sb("zt", [128, 2, S2, 128], BF16)
    for half in range(2):
        for cp in range(64):
            ps = PS[cp % 2]
            for c2 in range(2):
                ch = half * 128 + cp * 2 + c2
                p.mm(ps[0:S2, c2 * 256:(c2 + 1) * 256], ya[:, :, ch], w1)
            pv = ps[0:S2, :].re("p (c x k) -> p c x k", c=2, x=2)
            zr, zi = pv[:, :, 0, :], pv[:, :, 1, :]
            tc = tw[:, 0, :].ub(1, [S2, 2, 128])
            tsn = tw[:, 1, :].ub(1, [S2, 2, 128])
            p.tt(ta[0], zr, tc, ALU.mult)
            p.tt(ta[1], zi, tsn, ALU.mult)
            p.tt(ta[2], zi, tc, ALU.mult)
            p.tt(ta[3], zr, tsn, ALU.mult)
            cl = cp * 2
            p.tt(z2r[:, :, cl:cl + 2].re("p k c -> p c k"), ta[0], ta[1], ALU.add, eng="pool")
            p.tt(z2i[:, :, cl:cl + 2].re("p k c -> p c k"), ta[2], ta[3], ALU.subtract, eng="pool")
        for kg in range(32):
            ps = PS[2 + (kg % 2)]
            for kk in range(4):
                k1 = kg * 4 + kk
                o = ps[:, kk * 128:kk * 128 + 2 * S2]
                p.mm(o, z2r[:, k1, :], w3[:, 0, :], start=True, stop=False)
                p.mm(o, z2i[:, k1, :], w3[:, 1, :], start=False, stop=True)
            src = ps.re("p (q x) -> p q x", x=128)[:, :, 0:2 * S2]
            dst = zt[:, :, :, kg * 4:(kg + 1) * 4].re("p r k q -> p q (r k)")
            p.copy(dst, src, eng=("act" if kg % 2 else "dve"))
        p.dma("sp", C.ZrT[half * 128:(half + 1) * 128, :], zt[:, 0].re("p k q -> p (k q)"))
        p.dma("sp", C.ZiT[half * 128:(half + 1) * 128, :], zt[:, 1].re("p k q -> p (k q)"))
    p.flush()


def outproj_phase(p, C, S, wout, srcs, fold=None):
    TB, nt = 512, 4
    stage = p.sb("stage", [128, 2, 1024], F32)
    chunks = []
    for (t, n) in srcs:
        for c in range(n):
            chunks.append((t, c))
    NCH = len(chunks)
    W = p.sb("W", [128, NCH, 1024], BF16)
    PS = make_psum(p)
    if fold is None:
        load_w_bf16(p, W, wout, NCH * 128, 1024, stage, 1024)
    else:
        i = fold
        wtmp = p.sb("wtmp", [128, 6, 1024], BF16)
        load_w_bf16(p, wtmp, wout, 768, 1024, stage, 1024)
        for c in range(4):
            p.copy(W.k(c)[:, c, :], wtmp.k(c)[:, c, :])
        cf = p.sb("cf", [128, 4, 128], F32)
        cb = p.sb("cb", [128, 4, 128], BF16)
        p.dma("sp", cf, C.fBD[i])
        p.copy(cb, cf)
        m1 = p.sb("m1", [128, 2, 1024], BF16)
        for pr in range(2):
            for dh in range(2):
                ps = PS[dh]
                p.mm(ps, cb[:, pr, :], wtmp.k(4 + pr)[:, 4 + pr, dh * 512:(dh + 1) * 512])
                p.copy(m1[:, pr, dh * 512:(dh + 1) * 512], ps)
        for ri in range(2):
            for pr in range(2):
                for dh in range(2):
                    ps = PS[2 + dh]
                    p.mm(ps, cb[:, 2 + ri, :], m1[:, pr, dh * 512:(dh + 1) * 512])
                    c = 4 + ri * 2 + pr
                    p.copy(W.k(c)[:, c, dh * 512:(dh + 1) * 512], ps)
    xbs = [p.sb(f"xb{k}", [128, nt, D], F32) for k in range(2)]
    mT = [p.sb(f"mT{k}", [128, NCH, TB], BF16) for k in range(2)]
    for b in range(S // TB):
        xb, m = xbs[b % 2], mT[b % 2]
        load_block(p, C, C.xres, b, TB, xb)
        for ci, (t, c) in enumerate(chunks):
            p.dma("sp", m.k(ci)[:, ci, :], t[c * 128:(c + 1) * 128, b * TB:(b + 1) * TB])
        for j in range(nt):
            for dh in range(2):
                ps = PS[4 + ((j * 2 + dh) % 4)]
                for ci in range(NCH):
                    p.mm(ps, m.k(ci)[:, ci, j * 128:(j + 1) * 128], W.k(ci)[:, ci, dh * 512:(dh + 1) * 512],
                         start=(ci == 0), stop=(ci == NCH - 1))
                p.tt(xb[:, j, dh * 512:(dh + 1) * 512], ps, xb[:, j, dh * 512:(dh + 1) * 512], ALU.add)
        p.dma("sp", C.xres.k(("r", b))[b * TB:(b + 1) * TB, :].re("(j q) d -> q j d", q=128), xb)
    p.flush()


def odd_proj_phase(p, C, i, gcol, S):
    TB, nt = 512, 4
    Win = p.sb("Win", [128, 8, 672], BF16)
    Wq = p.sb("Wq", [128, 2, 768], BF16)
    Wkv = p.sb("Wkv", [128, 1, 1024], BF16)
    stage = p.sb("stage", [128, 2, 1024], F32)
    load_w_bf16(p, Win, C.cd_w_in[i], D, 672, stage, 672)
    load_w_bf16(p, Wq, C.c_w_q_up[i], 256, 768, stage, 768)
    load_w_bf16(p, Wkv, C.c_w_kv_up[i], 128, 1024, stage, 1024)
    glat = p.sb("glat", [128, 384], F32)
    p.dma("sp", glat, C.glat[i])
    gq = p.sb("gq", [128, 8, 96], F32)
    gk = p.sb("gk", [128, 8, 96], F32)
    p.dma("sp", gq, C.gq96[i])
    p.dma("sp", gk, C.gk96[i])
    p.ts(gq, gq, 96.0 ** -0.5, None, ALU.mult)
    xbs = [p.sb(f"xb{k}", [128, nt, D], F32) for k in range(2)]
    xn = p.sb("xn", [128, nt, D], BF16)
    hT = p.sb("hT", [128, 8, TB], BF16)
    ss = p.sb("ss", [128, 12], F32)
    cs = p.sb("cs", [128, 2, nt, 16], F32)
    junk = p.sb("junk", [128, 1024], F32)
    ssl = p.sb("ssl", [128, 4], F32)
    rl = p.sb("rl", [128, 2], F32)
    qln = p.sb("qln", [128, 384], F32)
    lnb = p.sb("lnb", [128, 384], BF16)
    lnT = p.sb("lnT", [128, 3, 128], BF16)
    kpe = p.sb("kpe", [128, 32], F32)
    kg = p.sb("kg", [128, 1, 32], F32)
    R = p.sb("R", [128, 1, 32], F32)
    qf = p.sb("qf", [128, 8, 96], F32)
    kvf = p.sb("kvf", [128, 8, 128], F32)
    s8 = p.sb("s8", [128, 8], F32)
    rq = p.sb("rq", [128, 8], F32)
    rk = p.sb("rk", [128, 8], F32)
    sp1 = p.sb("sp1", [128, 1], F32)
    ra = p.sb("ra", [128, 8, 16], F32)
    rb = p.sb("rb", [128, 8, 16], F32)
    tk = p.sb("tk", [128, 8, 64], F32)
    qb3 = p.sb("qb3", [128, 8, 96], BF16)
    kb3 = p.sb("kb3", [128, 8, 96], BF16)
    qTb = p.sb("qTb", [128, 8, TB], BF16)
    kTb = p.sb("kTb", [128, 8, TB], BF16)
    vb = p.sb("vb", [128, nt, 512], BF16)
    uTb = p.sb("uTb", [128, 2, TB], BF16)
    PS = make_psum(p)
    for b in range(S // TB):
        xb = xbs[b % 2]
        load_block(p, C, C.xres, b, TB, xb)
        p.dma("sp", cs[:, 0], C.cosC[:, b * nt:(b + 1) * nt, :])
        p.dma("sp", cs[:, 1], C.sinC[:, b * nt:(b + 1) * nt, :])
        norm_T(p, C, xb, nt, gcol, hT, xn, ss, PS[0:2])
        for m in range(2):
            pu = PS[3]
            for c in range(8):
                p.mm(pu, Win.k(c)[:, c, 416 + m * 128:416 + (m + 1) * 128], hT[:, c, :], start=(c == 0), stop=(c == 7))
            p.copy(uTb[:, m, :], pu, eng="act")
        for j in range(nt):
            pl = PS[2]
            for c in range(8):
                p.mm(pl[:, 0:416], hT[:, c, j * 128:(j + 1) * 128], Win.k(c)[:, c, 0:416], start=(c == 0), stop=(c == 7))
            p.act(junk[:, 0:256], pl[:, 0:256], AF.Square, accum=ssl[:, 0:1])
            p.act(junk[:, 256:384], pl[:, 256:384], AF.Square, accum=ssl[:, 1:2])
            p.ts(ssl[:, 2:3], ssl[:, 0:1], 1.0 / 256, EPS, ALU.mult, ALU.add)
            p.ts(ssl[:, 3:4], ssl[:, 1:2], 1.0 / 128, EPS, ALU.mult, ALU.add)
            p.act(ssl[:, 2:4], ssl[:, 2:4], AF.Sqrt)
            p.recip(rl, ssl[:, 2:4])
            p.ts(qln[:, 0:256], pl[:, 0:256], rl[:, 0:1], None, ALU.mult)
            p.ts(qln[:, 256:384], pl[:, 256:384], rl[:, 1:2], None, ALU.mult)
            p.copy(kpe, pl[:, 384:416], eng="act")
            p.tt(lnb, qln, glat, ALU.mult)
            pt = PS[0].bc(BF16)
            for c in range(3):
                p.tr(pt[:, c * 128:(c + 1) * 128], lnb[:, c * 128:(c + 1) * 128], C.ident_b)
            p.copy(lnT, pt[:, 0:384].re("p (c t) -> p c t", t=128), eng="act")
            pq0, pq1, pk0, pk1 = PS[4], PS[5], PS[6], PS[7]
            for c2 in range(2):
                p.mm(pq0, lnT[:, c2, :], Wq.k(c2)[:, c2, 0:512], start=(c2 == 0), stop=(c2 == 1))
            for c2 in range(2):
                p.mm(pq1[:, 0:256], lnT[:, c2, :], Wq.k(c2)[:, c2, 512:768], start=(c2 == 0), stop=(c2 == 1))
            p.mm(pk0, lnT[:, 2, :], Wkv.k(0)[:, 0, 0:512])
            p.mm(pk1, lnT[:, 2, :], Wkv.k(0)[:, 0, 512:1024])
            qff = qf.re("p h e -> p (h e)")
            kvff = kvf.re("p h e -> p (h e)")
            p.copy(qff[:, 0:512], pq0, eng="act")
            p.copy(qff[:, 512:768], pq1[:, 0:256], eng="act")
            p.copy(kvff[:, 0:512], pk0, eng="act")
            p.copy(kvff[:, 512:1024], pk1, eng="act")
            p.act(junk[:, 0:768], qff, AF.Square)
            p.reduce(s8, junk[:, 0:768].re("p (h e) -> p h e", e=96))
            rsqrt_mean(p, rq, s8, 96, s8)
            p.tt(qf, qf, rq.ub(2, [128, 8, 96]), ALU.mult)
            p.tt(qf, qf, gq, ALU.mult)
            cosb = cs[:, 0, j, :].ub(1, [128, 8, 16])
            sinb = cs[:, 1, j, :].ub(1, [128, 8, 16])
            rot_pair(p, qb3, qf, 64, 16, cosb, sinb, ra, rb)
            p.copy(qb3[:, :, 0:64], qf[:, :, 0:64], eng="pool")
            p.act(junk, kvff, AF.Square)
            p.reduce(s8, junk.re("p (h e) -> p h e", e=128)[:, :, 0:64])
            p.act(junk[:, 0:32], kpe, AF.Square, accum=sp1)
            p.ts(s8, s8, sp1[:, 0:1], None, ALU.add)
            rsqrt_mean(p, rk, s8, 96, s8)
            p.tt(tk, kvf[:, :, 0:64], rk.ub(2, [128, 8, 64]), ALU.mult)
            p.tt(kb3[:, :, 0:64], tk, gk[:, :, 0:64], ALU.mult)
            p.tt(kg[:, 0, :], kpe, gk[:, 0, 64:96], ALU.mult)
            rot_pair(p, R, kg, 0, 16, cs[:, 0, j, :].ub(1, [128, 1, 16]), cs[:, 1, j, :].ub(1, [128, 1, 16]),
                     ra[:, 0:1, :], rb[:, 0:1, :])
            p.tt(kb3[:, :, 64:96], R[:, 0, :].ub(1, [128, 8, 32]), rk.ub(2, [128, 8, 32]), ALU.mult)
            p.copy(vb[:, j, :].re("p (h e) -> p h e", e=64), kvf[:, :, 64:128], eng="pool")
            ptq = PS[0].bc(BF16)
            ptk = PS[1].bc(BF16)
            for h in range(8):
                p.tr(ptq[0:96, h * 128:(h + 1) * 128], qb3[:, h, :], C.ident_b)
            for h in range(8):
                p.tr(ptk[0:96, h * 128:(h + 1) * 128], kb3[:, h, :], C.ident_b)
            p.copy(qTb[0:96, :, j * 128:(j + 1) * 128], ptq[0:96, :].re("p (h t) -> p h t", t=128), eng="act")
            p.copy(kTb[0:96, :, j * 128:(j + 1) * 128], ptk[0:96, :].re("p (h t) -> p h t", t=128), eng="act")
        sl = slice(b * TB, (b + 1) * TB)
        p.dma("sp", C.QTc.k(b)[:, :, sl].re("h e t -> e h t"), qTb[0:96])
        p.dma("sp", C.KTc.k(b)[:, :, sl].re("h e t -> e h t"), kTb[0:96])
        p.dma("sp", C.Vs.k(b)[sl, :].re("(j q) e -> q j e", q=128), vb)
        p.dma("sp", C.UT.k(b)[:, sl].re("(m q) t -> q m t", q=128), uTb)
    p.flush()


def s5_phase(p, C, i, S):
    TB = 512
    NBk = S // TB
    names = ["lre", "lim", "lst", "step", "rho", "th", "cth", "sth", "lbr", "lbi", "nr", "den", "kr", "ki", "t1", "t2", "t3"]
    T = {n: p.sb("s5_" + n, [128, 16], F32) for n in names}
    tiI = p.sb("s5_ti", [128, 16], I32)
    p.dma("sp", T["lre"], C.s5lre[i])
    p.dma("sp", T["lim"], C.s5lim[i])
    p.dma("sp", T["lst"], C.s5lst[i])
    p.act(T["step"], T["lst"], AF.Exp)
    p.tt(T["t1"], T["lre"], T["step"], ALU.mult)
    p.act(T["rho"], T["t1"], AF.Exp)
    p.tt(T["th"], T["lim"], T["step"], ALU.mult)
    thr = p.sb("s5_thr", [128, 16], F32)
    p.ts(T["t1"], T["th"], 1.0 / TWO_PI, None, ALU.mult)
    p.copy(tiI, T["t1"])
    p.copy(T["t1"], tiI)
    p.stt(thr, T["t1"], -CW1, T["th"], ALU.mult, ALU.add)
    p.stt(thr, T["t1"], -CW2, thr, ALU.mult, ALU.add)
    sincos(p, T["sth"], T["cth"], thr, T["t1"], T["t2"], tiI)
    p.tt(T["lbr"], T["rho"], T["cth"], ALU.mult)
    p.tt(T["lbi"], T["rho"], T["sth"], ALU.mult)
    p.ts(T["nr"], T["lbr"], -1.0, None, ALU.add)
    p.tt(T["t1"], T["lre"], T["lre"], ALU.mult)
    p.tt(T["t2"], T["lim"], T["lim"], ALU.mult)
    p.tt(T["den"], T["t1"], T["t2"], ALU.add)
    p.recip(T["den"], T["den"])
    p.tt(T["t1"], T["nr"], T["lre"], ALU.mult)
    p.tt(T["t2"], T["lbi"], T["lim"], ALU.mult)
    p.tt(T["t1"], T["t1"], T["t2"], ALU.add)
    p.tt(T["kr"], T["t1"], T["den"], ALU.mult)
    p.tt(T["t1"], T["lbi"], T["lre"], ALU.mult)
    p.tt(T["t2"], T["nr"], T["lim"], ALU.mult)
    p.tt(T["t1"], T["t1"], T["t2"], ALU.subtract)
    p.tt(T["ki"], T["t1"], T["den"], ALU.mult)
    bre = p.sb("s5_bre", [128, 16, 16], F32)
    bim = p.sb("s5_bim", [128, 16, 16], F32)
    bbr = p.sb("s5_bbr", [128, 16, 16], F32)
    bbi = p.sb("s5_bbi", [128, 16, 16], F32)
    bt = p.sb("s5_bt", [128, 16, 16], F32)
    p.dma("sp", bre, C.s5bre[i])
    p.dma("sp", bim, C.s5bim[i])
    krb, kib = T["kr"].ub(2, [128, 16, 16]), T["ki"].ub(2, [128, 16, 16])
    p.tt(bbr, bre, krb, ALU.mult)
    p.tt(bt, bim, kib, ALU.mult)
    p.tt(bbr, bbr, bt, ALU.subtract)
    p.tt(bbi, bim, krb, ALU.mult)
    p.tt(bt, bre, kib, ALU.mult)
    p.tt(bbi, bbi, bt, ALU.add)
    PS = make_psum(p)
    LB = p.sb("s5_LB", [128, 16, 2, 128], BF16)
    blk = p.sb("s5_blk", [128, 2, 128], F32)
    for dg in range(16):
        gq_ = dg % 4
        p.memset(blk, 0.0)
        for ri, src in enumerate((bbr, bbi)):
            for g2 in range(2):
                c0 = (gq_ * 2 + g2) * 16
                p.copy(blk[g2 * 64:(g2 + 1) * 64, ri, c0:c0 + 16], src[g2 * 64:(g2 + 1) * 64, dg, :])
        for ri in range(2):
            ps = PS[ri]
            p.tr(ps[:, 0:128], blk[:, ri, :], C.ident_f)
            p.copy(LB[:, dg, ri, :], ps[:, 0:128])
    CT = p.sb("s5_CT", [128, 16, 2, 128], BF16)
    cst = p.sb("s5_cst", [128, 16, 128], F32)
    p.dma("sp", cst, C.s5ctr[i])
    p.copy(CT[:, :, 0, :], cst)
    p.dma("sp", cst, C.s5cti[i])
    p.ts(CT[:, :, 1, :], cst, -1.0, None, ALU.mult)
    NJ = 513
    iot = p.sb("s5_iot", [128, NJ], F32)
    p.dma("sp", iot, C.iota)
    RC = p.sb("s5_RC", [128, 16, NJ], F32)
    RS = p.sb("s5_RS", [128, 16, NJ], F32)
    a1 = p.sb("s5_a1", [128, NJ], F32)
    a2 = p.sb("s5_a2", [128, NJ], F32)
    a3 = p.sb("s5_a3", [128, NJ], F32)
    ai = p.sb("s5_ai", [128, NJ], I32)
    for dg in range(16):
        p.ts(a1, iot, thr[:, dg:dg + 1], None, ALU.mult)
        sincos(p, RS[:, dg, :], RC[:, dg, :], a1, a2, a3, ai)
    dsk = p.sb("s5_dsk", [128, 2], F32)
    bgl = p.sb("s5_bgl", [128, 2], F32)
    p.dma("sp", dsk, C.s5dsk[i])
    p.dma("sp", bgl, C.s5bgl[i])
    Wgl = p.sb("s5_Wgl", [128, 2, 256], BF16)
    stage = p.sb("stage", [128, 2, 256], F32)
    load_w_bf16(p, Wgl, C.d_w_glu[i], 256, 256, stage, 256)
    uT = p.sb("s5_uT", [128, 2, S], BF16)
    p.dma("sp", uT, C.UT.re("(m q) t -> q m t", q=128))
    carry = p.sb("s5_carry", [128, 16, 2], F32)
    cnew_ = [p.sb(f"s5_cnew{k}", [128, 4], F32) for k in range(2)]
    er_ = [p.sb(f"s5_er{k}", [128, TB], F32) for k in range(2)]
    ei_ = [p.sb(f"s5_ei{k}", [128, TB], F32) for k in range(2)]
    wr_ = [p.sb(f"s5_wr{k}", [128, TB], F32) for k in range(2)]
    wi_ = [p.sb(f"s5_wi{k}", [128, TB], F32) for k in range(2)]
    m1_ = [p.sb(f"s5_m1{k}", [128, TB], F32) for k in range(2)]
    m2_ = [p.sb(f"s5_m2{k}", [128, TB], F32) for k in range(2)]
    n1_ = [p.sb(f"s5_n1{k}", [128, TB], F32) for k in range(2)]
    n2_ = [p.sb(f"s5_n2{k}", [128, TB], F32) for k in range(2)]
    xr = [p.sb(f"s5_xr{k}", [128, TB], BF16) for k in range(2)]
    xi = [p.sb(f"s5_xi{k}", [128, TB], BF16) for k in range(2)]
    yf = p.sb("s5_yf", [128, 2, TB], F32)
    yv = p.sb("s5_yv", [128, 2, TB], F32)
    zb = p.sb("s5_zb", [128, 2, TB], BF16)
    zf = p.sb("s5_zf", [128, 2, TB], F32)
    g1 = p.sb("s5_g1", [128, TB], F32)
    g2t = p.sb("s5_g2", [128, TB], F32)
    dob = p.sb("s5_dob", [128, 2, TB], BF16)
    steps = []
    for d in range(2):
        order = range(NBk) if d == 0 else range(NBk - 1, -1, -1)
        for bi, b in enumerate(order):
            for gp in range(8):
                steps.append((d, bi, b, gp))

    def usl_of(d, b, m):
        t0_ = b * TB
        v = uT[:, m, t0_:t0_ + TB]
        return v if d == 0 else v[:, ::-1]

    def emit_bu(si):
        d, bi, b, gp = steps[si]
        dg = d * 8 + gp
        m = gp // 4
        pr_, pi_ = PS[0 + (gp % 2) * 2], PS[1 + (gp % 2) * 2]
        p.mm(pr_, LB[:, dg, 0, :], usl_of(d, b, m))
        p.mm(pi_, LB[:, dg, 1, :], usl_of(d, b, m))

    emit_bu(0)
    for si, (d, bi, b, gp) in enumerate(steps):
        if True:
            t0 = b * TB
            if gp == 0:
                py = [PS[4], PS[5]]
                cnt = [0, 0]
            if si + 1 < len(steps):
                emit_bu(si + 1)
            if True:
                dg = d * 8 + gp
                m = gp // 4
                er, ei, wr, wi = er_[gp % 2], ei_[gp % 2], wr_[gp % 2], wi_[gp % 2]
                m1, m2, n1, n2 = m1_[gp % 2], m2_[gp % 2], n1_[gp % 2], n2_[gp % 2]
                pr_, pi_ = PS[0 + (gp % 2) * 2], PS[1 + (gp % 2) * 2]
                cb_, sb_ = RC[:, dg, 0:TB], RS[:, dg, 0:TB]
                p.tt(m1, pr_, cb_, ALU.mult)
                p.tt(m2, pi_, sb_, ALU.mult)
                p.tt(er, m1, m2, ALU.add)
                p.tt(m1, pi_, cb_, ALU.mult)
                p.tt(m2, pr_, sb_, ALU.mult)
                p.tt(ei, m1, m2, ALU.subtract)
                rho_b = T["rho"][:, dg:dg + 1].bcast([128, TB])
                for (w_, e_, ci) in ((wr, er, 0), (wi, ei, 1)):
                    init = 0.0 if bi == 0 else carry.k(dg)[:, dg, ci:ci + 1]
                    oo, a_, b_ = w_.ap, rho_b.ap, e_.ap
                    ini = init if bi == 0 else init.ap
                    rd = [T["rho"], e_] + ([] if bi == 0 else [carry.k(dg)])
                    p.add("dve", (lambda e, oo=oo, a_=a_, b_=b_, ini=ini: e.tensor_tensor_scan(oo, a_, b_, ini, ALU.mult, ALU.add)),
                          rd, [w_])
                c5, s5 = RC[:, dg, TB:TB + 1], RS[:, dg, TB:TB + 1]
                cnew = cnew_[gp % 2]
                p.ts(cnew[:, 0:1], wr[:, TB - 1:TB], c5, None, ALU.mult)
                p.ts(cnew[:, 1:2], wi[:, TB - 1:TB], s5, None, ALU.mult)
                p.ts(cnew[:, 2:3], wr[:, TB - 1:TB], s5, None, ALU.mult)
                p.ts(cnew[:, 3:4], wi[:, TB - 1:TB], c5, None, ALU.mult)
                p.tt(carry.k(dg)[:, dg, 0:1], cnew[:, 0:1], cnew[:, 1:2], ALU.subtract)
                p.tt(carry.k(dg)[:, dg, 1:2], cnew[:, 2:3], cnew[:, 3:4], ALU.add)
                xr_, xi_ = xr[gp % 2], xi[gp % 2]
                p.tt(n1, wr, cb_, ALU.mult, eng="pool")
                p.tt(n2, wi, sb_, ALU.mult, eng="pool")
                p.tt(xr_, n1, n2, ALU.subtract, eng="pool")
                p.tt(n1, wi, cb_, ALU.mult, eng="pool")
                p.tt(n2, wr, sb_, ALU.mult, eng="pool")
                p.tt(xi_, n1, n2, ALU.add, eng="pool")
                xr_o = xr_ if d == 0 else xr_[:, ::-1]
                xi_o = xi_ if d == 0 else xi_[:, ::-1]
                p.mm(py[m], CT[:, dg, 0, :], xr_o, start=(cnt[m] == 0), stop=False)
                p.mm(py[m], CT[:, dg, 1, :], xi_o, start=False, stop=(cnt[m] == 3))
                cnt[m] += 1
            if gp != 7:
                continue
            if d == 0:
                for m in range(2):
                    p.copy(yf[:, m, :], py[m], eng="act")
                p.dma("sp", C.YF.k(b)[:, t0:t0 + TB].re("(m q) t -> q m t", q=128), yf)
            else:
                p.dma("sp", yf, C.YF.k(b)[:, t0:t0 + TB].re("(m q) t -> q m t", q=128))
                for m in range(2):
                    p.tt(yv[:, m, :], py[m], yf[:, m, :], ALU.add)
                    p.stt(yv[:, m, :], uT[:, m, t0:t0 + TB], dsk[:, m:m + 1], yv[:, m, :], ALU.mult, ALU.add)
                    p.tt(g1, yv[:, m, :], yv[:, m, :], ALU.mult)
                    p.ts(g1, g1, 0.044715, 1.0, ALU.mult, ALU.add)
                    p.tt(g1, g1, yv[:, m, :], ALU.mult)
                    p.act(g2t, g1, AF.Sigmoid, scale=1.5957691216057308)
                    p.tt(zf[:, m, :], yv[:, m, :], g2t, ALU.mult)
                    p.copy(zb[:, m, :], zf[:, m, :], eng="pool")
                for oc in range(2):
                    pg = PS[6 + oc]
                    for c in range(2):
                        p.mm(pg, Wgl.k(c)[:, c, oc * 128:(oc + 1) * 128], zb[:, c, :], start=(c == 0), stop=(c == 1))
                    p.act(g2t, pg, AF.Sigmoid, bias=bgl[:, oc:oc + 1])
                    p.tt(dob[:, oc, :], zf[:, oc, :], g2t, ALU.mult)
                p.dma("sp", C.DoT.k(b)[:, t0:t0 + TB].re("(m q) t -> q m t", q=128), dob)
    p.flush()


NE, NO = 2, 2
SAME_ENG_SYNC = True
W_SHAPES = {
    "ffn1_w_gate": (4, 1024, 2816), "ffn1_w_up": (4, 1024, 2816), "ffn1_w_down": (4, 2816, 1024),
    "ffn2_w_gate": (4, 1024, 2816), "ffn2_w_up": (4, 1024, 2816), "ffn2_w_down": (4, 2816, 1024),
    "ab_w_in": (2, 1024, 1792), "ab_w_out": (2, 768, 1024),
    "cd_w_in": (2, 1024, 672), "cd_w_out": (2, 768, 1024),
    "c_w_q_up": (2, 256, 768), "c_w_kv_up": (2, 128, 1024), "d_w_glu": (2, 256, 256),
}


def aux_shapes(S):
    NT, S2 = S // 128, S // 128
    return {
        "ident": ((128, 128), "f"), "gcols": ((128, 12, 8), "f"), "pos": ((128, NT), "i"),
        "gqk": ((2, 128, 16, 64), "f"), "fBD": ((2, 128, 4, 128), "f"),
        "glat": ((2, 128, 384), "f"), "gq96": ((2, 128, 8, 96), "f"), "gk96": ((2, 128, 8, 96), "f"),
        "s5lre": ((2, 128, 16), "f"), "s5lim": ((2, 128, 16), "f"), "s5lst": ((2, 128, 16), "f"),
        "s5bre": ((2, 128, 16, 16), "f"), "s5bim": ((2, 128, 16, 16), "f"),
        "s5ctr": ((2, 128, 16, 128), "f"), "s5cti": ((2, 128, 16, 128), "f"),
        "s5dsk": ((2, 128, 2), "f"), "s5bgl": ((2, 128, 2), "f"),
        "iota": ((128, 513), "f"), "dmask": ((20, 128, 512), "f"),
        "fW1": ((128, 256), "f"), "fT": ((S2, 2, 128), "f"), "fW3": ((S2, 2, 2 * S2), "f"),
    }


def build_program(S, layers=(0, 1, 2, 3), stop_after=None):
    nc = bass.Bass("TRN2", target_bir_lowering=False)
    p = Prog(nc)
    p.same_eng_sync = SAME_ENG_SYNC
    C = Ctx()
    NT = S // 128
    xin = p.dram("x", [S, D], F32, kind="ExternalInput")
    for n, shp in W_SHAPES.items():
        setattr(C, n, p.dram(n, list(shp), F32, kind="ExternalInput"))
    for n, (shp, ty) in aux_shapes(S).items():
        setattr(C, n, p.dram(n, list(shp), F32 if ty == "f" else I32, kind="ExternalInput"))
    out = p.dram("out", [S, D], F32, kind="ExternalOutput")
    C.xres = out
    C.cosA = p.dram("cosA", [128, NT, 8], F32)
    C.sinA = p.dram("sinA", [128, NT, 8], F32)
    C.cosC = p.dram("cosC", [128, NT, 16], F32)
    C.sinC = p.dram("sinC", [128, NT, 16], F32)
    C.QT = p.dram("QT", [512, S], BF16)
    C.KT = p.dram("KT", [512, S], BF16)
    C.Vs = p.dram("Vs", [S, 512], BF16)
    C.Fs = p.dram("Fs", [S, 256], BF16)
    C.AoT = p.dram("AoT", [512, S], BF16)
    C.ZrT = p.dram("ZrT", [256, S], BF16)
    C.ZiT = p.dram("ZiT", [256, S], BF16)
    C.QTc = p.dram("QTc", [8, 96, S], BF16)
    C.KTc = p.dram("KTc", [8, 96, S], BF16)
    C.UT = p.dram("UT", [256, S], BF16)
    C.YF = p.dram("YF", [256, S], F32)
    C.DoT = p.dram("DoT", [256, S], BF16)
    idf = p.sb("idf", [128, 128], F32, glob=True)
    C.ident_f = idf
    C.ident_b = p.sb("idb", [128, 128], BF16, glob=True)
    gcols = p.sb("gcols", [128, 12, 8], F32, glob=True)
    p.dma("sp", idf, C.ident)
    p.copy(C.ident_b, idf)
    p.dma("sp", gcols, C.gcols)
    posi = p.sb("posi", [128, NT], I32)
    posf = p.sb("posf", [128, NT], F32)
    p.dma("sp", posi, C.pos)
    p.copy(posf, posi)
    for (nf, rot, dc, ds) in ((8, 16, C.cosA, C.sinA), (16, 32, C.cosC, C.sinC)):
        ang = p.sb(f"ang{nf}", [128, NT, nf], F32)
        t1 = p.sb(f"rt1{nf}", [128, NT, nf], F32)
        t2 = p.sb(f"rt2{nf}", [128, NT, nf], F32)
        ti = p.sb(f"rti{nf}", [128, NT, nf], I32)
        so = p.sb(f"rso{nf}", [128, NT, nf], F32)
        co = p.sb(f"rco{nf}", [128, NT, nf], F32)
        invf = (500000.0 ** (-np.arange(0, rot, 2, dtype=np.float32) / np.float32(rot))).astype(np.float32)
        for f in range(nf):
            p.ts(ang[:, :, f], posf, float(invf[f]), None, ALU.mult)
        sincos(p, so, co, ang, t1, t2, ti)
        p.dma("sp", dc, co)
        p.dma("sp", ds, so)
    p.flush()
    first = True
    for L in layers:
        i = L // 2
        src = xin if first else out
        first = False
        ffn_phase(p, C, src, out, C.ffn1_w_gate[L], C.ffn1_w_up[L], C.ffn1_w_down[L], gcols[:, L * 3 + 0, :], S)
        if stop_after == (L, 0):
            break
        gmix = gcols[:, L * 3 + 1, :]
        if L % 2 == 0:
            even_proj_phase(p, C, i, gmix, S)
            attention_phase(p, C, S, 8, 64, lambda h: C.QT[h * 64:(h + 1) * 64, :], lambda h: C.KT[h * 64:(h + 1) * 64, :],
                            C.Vs, C.AoT, True)
            fnet_phase(p, C, S)
            outproj_phase(p, C, S, C.ab_w_out[i], [(C.AoT, 4), (C.ZrT, 2), (C.ZiT, 2)], fold=i)
        else:
            odd_proj_phase(p, C, i, gmix, S)
            attention_phase(p, C, S, 8, 96, lambda h: C.QTc[h], lambda h: C.KTc[h], C.Vs, C.AoT, False)
            s5_phase(p, C, i, S)
            outproj_phase(p, C, S, C.cd_w_out[i], [(C.AoT, 4), (C.DoT, 2)])
        if stop_after == (L, 1):
            break
        ffn_phase(p, C, out, out, C.ffn2_w_gate[L], C.ffn2_w_up[L], C.ffn2_w_down[L], gcols[:, L * 3 + 2, :], S)
    p.finish()
    return nc, p


def host_aux(inp, S, positions_row):
    f32 = np.float32
    NT = S // 128
    S2 = S // 128
    a = {}
    a["ident"] = np.eye(128, dtype=f32)
    g = np.zeros((128, 12, 8), f32)
    for L in range(4):
        for n, nm in enumerate(("ffn1_norm", "mix_norm", "ffn2_norm")):
            g[:, L * 3 + n, :] = np.asarray(inp[nm][L], f32).reshape(8, 128).T
    a["gcols"] = g
    a["pos"] = np.ascontiguousarray(np.asarray(positions_row, np.int32).reshape(NT, 128).T)
    gqk = np.zeros((2, 128, 16, 64), f32)
    fBD = np.zeros((2, 128, 4, 128), f32)
    cidx = np.arange(64)
    Cc = np.cos(2 * np.pi * np.outer(cidx, cidx) / 64).astype(f32)
    Sc = np.sin(2 * np.pi * np.outer(cidx, cidx) / 64).astype(f32)
    for i in range(2):
        gqk[i, :, 0:8, :] = np.asarray(inp["a_q_norm"][i], f32)[None, None, :]
        gqk[i, :, 8:16, :] = np.asarray(inp["a_k_norm"][i], f32)[None, None, :]
        bm = np.asarray(inp["b_w_mix"][i], f32)
        for pr in range(2):
            for g2 in range(2):
                fBD[i, g2 * 64:(g2 + 1) * 64, pr, g2 * 64:(g2 + 1) * 64] = bm[pr * 2 + g2].T
        for g2 in range(2):
            fBD[i, g2 * 64:(g2 + 1) * 64, 2, g2 * 64:(g2 + 1) * 64] = Cc
            fBD[i, g2 * 64:(g2 + 1) * 64, 3, g2 * 64:(g2 + 1) * 64] = Sc
    a["gqk"], a["fBD"] = gqk, fBD
    glat = np.zeros((2, 128, 384), f32)
    gq96 = np.zeros((2, 128, 8, 96), f32)
    gk96 = np.zeros((2, 128, 8, 96), f32)
    for i in range(2):
        glat[i, :, 0:256] = np.asarray(inp["c_q_lat_norm"][i], f32)[None, :]
        glat[i, :, 256:384] = np.asarray(inp["c_kv_lat_norm"][i], f32)[None, :]
        gq96[i] = np.asarray(inp["c_q_norm"][i], f32)[None, None, :]
        gk96[i] = np.asarray(inp["c_k_norm"][i], f32)[None, None, :]
    a["glat"], a["gq96"], a["gk96"] = glat, gq96, gk96

    def sl(arr):
        return np.ascontiguousarray(np.asarray(arr, f32).reshape(2, 8, 2, 64).transpose(2, 3, 0, 1).reshape(128, 16))
    a["s5lre"] = np.stack([sl(inp["d_lam_re"][i]) for i in range(2)])
    a["s5lim"] = np.stack([sl(inp["d_lam_im"][i]) for i in range(2)])
    a["s5lst"] = np.stack([sl(np.broadcast_to(np.asarray(inp["d_log_step"][i], f32)[:, :, None], (2, 16, 64))) for i in range(2)])

    def slb(arr):
        return np.ascontiguousarray(np.asarray(arr, f32).reshape(2, 8, 2, 64, 16).transpose(2, 3, 0, 1, 4).reshape(128, 16, 16))
    a["s5bre"] = np.stack([slb(inp["d_b_re"][i]) for i in range(2)])
    a["s5bim"] = np.stack([slb(inp["d_b_im"][i]) for i in range(2)])

    def slc(arr):
        arr = np.asarray(arr, f32)
        o = np.zeros((128, 16, 128), f32)
        for d in range(2):
            for gp in range(8):
                for g2 in range(2):
                    gg = gp * 2 + g2
                    c0 = ((gp % 4) * 2 + g2) * 16
                    o[g2 * 64:(g2 + 1) * 64, d * 8 + gp, c0:c0 + 16] = arr[d, gg].T
        return o
    a["s5ctr"] = np.stack([slc(inp["d_c_re"][i]) for i in range(2)])
    a["s5cti"] = np.stack([slc(inp["d_c_im"][i]) for i in range(2)])
    a["s5dsk"] = np.stack([np.asarray(inp["d_skip"][i], f32).reshape(2, 128).T for i in range(2)])
    a["s5bgl"] = np.stack([np.asarray(inp["d_b_glu"][i], f32).reshape(2, 128).T for i in range(2)])
    a["iota"] = np.broadcast_to(np.arange(513, dtype=f32)[None, :], (128, 513)).copy()
    ki = np.arange(128)[:, None]
    qi = np.arange(512)[None, :]
    dm = np.zeros((20, 128, 512), f32)
    for o in range(20):
        dl = (o * 128 - 1024) + ki - qi
        m = np.zeros_like(dl, dtype=f32)
        for dil in (1, 4, 16):
            m += ((dl % dil == 0) & (np.abs(dl) <= 64 * dil)).astype(f32)
        dm[o] = m
    a["dmask"] = dm
    s1 = np.arange(128)
    k1 = np.arange(128)
    ang = 2 * np.pi * np.outer(s1, k1) / 128
    nrm = 1.0 / np.sqrt(S * 64.0)
    a["fW1"] = (np.concatenate([np.cos(ang), -np.sin(ang)], axis=1) * nrm).astype(f32)
    s2 = np.arange(S2)
    phi = 2 * np.pi * np.outer(s2, k1) / S
    a["fT"] = np.stack([np.cos(phi), np.sin(phi)], axis=1).astype(f32)
    th = 2 * np.pi * np.outer(s2, np.arange(S2)) / S2
    a["fW3"] = np.stack([np.concatenate([np.cos(th), -np.sin(th)], 1), np.concatenate([np.sin(th), np.cos(th)], 1)], axis=1).astype(f32)
    for k in a:
        a[k] = np.ascontiguousarray(a[k])
    return a


def run_module(inp, S, layers=(0, 1, 2, 3), stop_after=None, trace=False):
    x = np.asarray(inp["x"], np.float32)
    B = x.shape[0]
    nc, p = build_program(S, layers, stop_after)
    base = {n: np.ascontiguousarray(np.asarray(inp[n], np.float32)) for n in W_SHAPES}
    in_maps = []
    for b in range(B):
        m = dict(base)
        m.update(host_aux(inp, S, np.asarray(inp["positions"])[b]))
        m["x"] = np.ascontiguousarray(x[b])
        in_maps.append(m)
    res = run_bass_kernel_spmd(nc, in_maps, core_ids=list(range(B)), trace=trace)
    return np.stack([r["out"] for r in res.results], axis=0), res, p


def kernel(**inputs):
    out, _, _ = run_module(inputs, 8192)
    return out.astype(np.float32)
```

```python
import numpy as np
from contextlib import ExitStack
import concourse.bass as bass
import concourse.mybir as mybir

F32 = mybir.dt.float32
BF16 = mybir.dt.bfloat16
I32 = mybir.dt.int32
AF = mybir.ActivationFunctionType
ALU = mybir.AluOpType
AX = mybir.AxisListType

EPOCH = 30000
N_DMA_SEMS = 40


class V:
    __slots__ = ("ap", "key")

    def __init__(self, ap, key):
        self.ap = ap
        self.key = key

    def __getitem__(self, idx):
        return V(self.ap[idx], self.key)

    def k(self, sub):
        return V(self.ap, (self.key, sub))

    def re(self, s, **kw):
        return V(self.ap.rearrange(s, **kw), self.key)

    def bc(self, dt):
        return V(self.ap.bitcast(dt), self.key)

    def bcast(self, shape):
        return V(self.ap.to_broadcast(list(shape)), self.key)

    def ub(self, axis, shape):
        return V(self.ap.unsqueeze(axis).to_broadcast(list(shape)), self.key)


class Op:
    __slots__ = ("eng", "emit", "deps", "sig", "waits", "is_dma", "dsem", "dval", "needs_sig", "n")


class Prog:
    COMPUTE = ("pe", "act", "dve", "pool")

    def __init__(self, nc):
        self.nc = nc
        self.ops = []
        self.last_w = {}
        self.rd_eng = {}
        self.rd_dma = {}
        self.es = ExitStack()
        self.same_eng_sync = True
        self._n = 0
        self.gs = ExitStack()
        self.NEP = {"pe": 6, "act": 6, "dve": 8, "pool": 3}
        self.sems = {e: [self.gs.enter_context(nc.semaphore(f"s_{e}_{i}")) for i in range(self.NEP[e])]
                     for e in self.COMPUTE}
        self.dsems = [self.gs.enter_context(nc.semaphore(f"s_dma_{i}")) for i in range(N_DMA_SEMS)]
        self.cnt = {e: 0 for e in self.COMPUTE}
        self.dval = [0] * N_DMA_SEMS
        self.rot = 0
        self.know = {}
        self.bar = {}
        self.stats = {e: 0 for e in ("pe", "act", "dve", "pool", "sp")}

    def sb(self, name, shape, dt, glob=False):
        self._u = getattr(self, "_u", 0) + 1
        name = f"{name}_u{self._u}"
        t = (self.gs if glob else self.es).enter_context(self.nc.sbuf_tensor(name, list(shape), dt))
        return V(t[:], name)

    def ps(self, name, shape, dt):
        self._u = getattr(self, "_u", 0) + 1
        name = f"{name}_u{self._u}"
        t = self.es.enter_context(self.nc.psum_tensor(name, list(shape), dt))
        return V(t[:], name)

    def dram(self, name, shape, dt, kind="Internal"):
        t = self.nc.dram_tensor(name, list(shape), dt, kind=kind)
        return V(t.ap(), name)

    def add(self, eng, emit, reads=(), writes=(), is_dma=False):
        op = Op()
        op.eng = eng
        op.emit = emit
        op.is_dma = is_dma
        op.sig = None
        op.needs_sig = False
        op.waits = None
        op.dsem = None
        op.dval = None
        op.n = self._n
        self._n += 1
        deps = {}
        rk = [v.key if isinstance(v, V) else v for v in reads]
        wk = [v.key if isinstance(v, V) else v for v in writes]
        for k in rk:
            w = self.last_w.get(k)
            if w is not None:
                deps[w.n] = w
        for k in wk:
            w = self.last_w.get(k)
            if w is not None:
                deps[w.n] = w
            for r in self.rd_eng.get(k, {}).values():
                deps[r.n] = r
            for r in self.rd_dma.get(k, ()):
                deps[r.n] = r
        for k in wk:
            self.last_w[k] = op
            self.rd_eng[k] = {}
            self.rd_dma[k] = []
        for k in rk:
            if k in wk:
                continue
            if is_dma:
                self.rd_dma.setdefault(k, []).append(op)
            else:
                self.rd_eng.setdefault(k, {})[eng] = op
        deps.pop(op.n, None)
        op.deps = list(deps.values())
        self.ops.append(op)
        return op

    def dma(self, q, out, in_, extra_reads=(), extra_writes=()):
        o, i = out.ap, in_.ap
        return self.add(q, lambda e: e.dma_start(out=o, in_=i), [in_] + list(extra_reads),
                        [out] + list(extra_writes), is_dma=True)

    def mm(self, out, lhsT, rhs, start=True, stop=True, **kw):
        o, l, r = out.ap, lhsT.ap, rhs.ap
        rd = [lhsT, rhs] + ([] if start else [out])
        return self.add("pe", lambda e: e.matmul(o, l, r, start=start, stop=stop, **kw), rd, [out])

    def tr(self, out, in_, ident):
        o, i, d = out.ap, in_.ap, ident.ap
        return self.add("pe", lambda e: e.transpose(o, i, d), [in_, ident], [out])

    def act(self, out, in_, func, scale=1.0, bias=0.0, eng="act", accum=None):
        o, i = out.ap, in_.ap
        rd = [in_]
        wr = [out]
        kw = {}
        if isinstance(scale, V):
            rd.append(scale)
            kw["scale"] = scale.ap
        else:
            kw["scale"] = scale
        if isinstance(bias, V):
            rd.append(bias)
            kw["bias"] = bias.ap
        else:
            kw["bias"] = bias
        if accum is not None:
            kw["accum_out"] = accum.ap
            wr.append(accum)
        return self.add(eng, lambda e: e.activation(o, i, func, **kw), rd, wr)

    def tt(self, out, in0, in1, op, eng="dve"):
        o, a, b = out.ap, in0.ap, in1.ap
        return self.add(eng, lambda e: e.tensor_tensor(o, a, b, op), [in0, in1], [out])

    def ts(self, out, in0, s1, s2=None, op0=ALU.mult, op1=None, eng="dve"):
        o, a = out.ap, in0.ap
        rd = [in0]
        a1 = s1
        a2 = s2
        if isinstance(s1, V):
            rd.append(s1)
            a1 = s1.ap
        if isinstance(s2, V):
            rd.append(s2)
            a2 = s2.ap
        if op1 is None:
            return self.add(eng, lambda e: e.tensor_scalar(o, a, a1, None, op0), rd, [out])
        return self.add(eng, lambda e: e.tensor_scalar(o, a, a1, a2, op0, op1), rd, [out])

    def stt(self, out, in0, scalar, in1, op0, op1):
        o, a, b = out.ap, in0.ap, in1.ap
        rd = [in0, in1]
        s = scalar
        if isinstance(scalar, V):
            rd.append(scalar)
            s = scalar.ap
        return self.add("dve", lambda e: e.scalar_tensor_tensor(o, a, s, b, op0, op1), rd, [out])

    def copy(self, out, in_, eng="dve"):
        o, i = out.ap, in_.ap
        if eng == "act":
            return self.add(eng, lambda e: e.copy(o, i), [in_], [out])
        return self.add(eng, lambda e: e.tensor_copy(o, i), [in_], [out])

    def memset(self, out, val, eng="dve"):
        o = out.ap
        return self.add(eng, lambda e: e.memset(o, val), [], [out])

    def reduce(self, out, in_, op=ALU.add, axis=AX.X, eng="dve"):
        o, i = out.ap, in_.ap
        return self.add(eng, lambda e: e.tensor_reduce(o, i, axis, op), [in_], [out])

    def recip(self, out, in_):
        o, i = out.ap, in_.ap
        return self.add("dve", lambda e: e.reciprocal(o, i), [in_], [out])

    def _need_wait(self, op, d):
        if d.is_dma:
            return True
        if d.eng != op.eng:
            return True
        if op.is_dma:
            return True
        if op.eng == "pe":
            return False
        if op.eng == "pool":
            return True
        return self.same_eng_sync

    def _semval(self, key, val):
        if isinstance(key, tuple):
            return (self.dsems[key[1]], val)
        sig = val - 1
        assert sig // EPOCH < self.NEP[key], ("too many signals", key, sig)
        return (self.sems[key][sig // EPOCH], sig % EPOCH + 1)

    def flush(self):
        nc = self.nc
        ops = self.ops
        last = {}
        for op in ops:
            for d in op.deps:
                if self._need_wait(op, d):
                    d.needs_sig = True
            if not op.is_dma:
                last[op.eng] = op
        for op in last.values():
            op.needs_sig = True
        for op in ops:
            if not op.is_dma and op.needs_sig:
                op.sig = self.cnt[op.eng]
                self.cnt[op.eng] += 1
        seen_first = set()
        for op in ops:
            K = self.know.setdefault(op.eng, {})
            need = {}
            if op.eng not in seen_first:
                seen_first.add(op.eng)
                for k, v in self.bar.pop(op.eng, {}).items():
                    need[k] = max(need.get(k, 0), v)
            if op.is_dma:
                s = self.rot
                self.rot = (self.rot + 1) % N_DMA_SEMS
                if self.dval[s] > 0:
                    need[("d", s)] = max(need.get(("d", s), 0), self.dval[s])
                self.dval[s] += 16
                op.dsem = s
                op.dval = self.dval[s]
            for d in op.deps:
                if not self._need_wait(op, d):
                    continue
                if d.is_dma:
                    key, val = ("d", d.dsem), d.dval
                else:
                    key, val = d.eng, d.sig + 1
                if need.get(key, 0) < val:
                    need[key] = val
            waits = []
            for key, val in need.items():
                if K.get(key, 0) >= val:
                    continue
                K[key] = val
                waits.append(self._semval(key, val))
            op.waits = waits
            self.stats[op.eng] += 1
        by = {}
        for op in ops:
            by.setdefault(op.eng, []).append(op)
        sems, dsems = self.sems, self.dsems

        def run(engname, e):
            for op in by.get(engname, []):
                for (s, v) in op.waits:
                    e.wait_ge(s, v)
                ins = op.emit(e)
                if op.is_dma:
                    ins.then_inc(dsems[op.dsem], 16)
                elif op.sig is not None:
                    ins.then_inc(sems[op.eng][op.sig // EPOCH], 1)

        with nc.Block() as block:
            @block.sync
            def _(e):
                run("sp", e)

            @block.tensor
            def _(e):
                run("pe", e)

            @block.scalar
            def _(e):
                run("act", e)

            @block.vector
            def _(e):
                run("dve", e)

            @block.gpsimd
            def _(e):
                run("pool", e)
        front = {}
        for e in self.COMPUTE:
            if self.cnt[e] > 0:
                front[e] = self.cnt[e]
        for s in range(N_DMA_SEMS):
            if self.dval[s] > 0:
                front[("d", s)] = self.dval[s]
        self.bar = {e: dict(front) for e in ("pe", "act", "dve", "pool", "sp")}
        self.ops = []
        self.last_w = {}
        self.rd_eng = {}
        self.rd_dma = {}
        self.es.close()
        self.es = ExitStack()

    def finish(self):
        nc = self.nc
        self.flush()
        need = self.bar["sp"]
        K = self.know.setdefault("sp", {})
        dsems = self.dsems
        waits = [self._semval(k, v) for k, v in need.items() if K.get(k, 0) < v]
        with nc.Block() as block:
            @block.sync
            def _(e):
                for (s, v) in waits:
                    e.wait_ge(s, v)

from concourse.bass_utils import run_bass_kernel_spmd
import math

D = 1024
DFF = 2816
EPS = 1e-6
TWO_PI = 2.0 * math.pi
CW1 = 6.28125
CW2 = TWO_PI - 6.28125


class Ctx:
    pass


def make_psum(p):
    return [p.ps(f"psb{i}", [128, 512], F32) for i in range(8)]


def sincos(p, out_sin, out_cos, ang, t1, t2, ti):
    p.ts(t1, ang, 1.0 / TWO_PI, None, ALU.mult)
    p.copy(ti, t1)
    p.copy(t1, ti)
    p.stt(t2, t1, -CW1, ang, ALU.mult, ALU.add)
    p.stt(t2, t1, -CW2, t2, ALU.mult, ALU.add)
    p.ts(t1, t2, 3.1415925, -3.1415925, ALU.min, ALU.max)
    p.act(out_sin, t1, AF.Sin)
    p.ts(t1, t2, math.pi / 2, None, ALU.add)
    p.ts(ti.bc(F32), t1, math.pi, None, ALU.is_gt)
    p.stt(t1, ti.bc(F32), -TWO_PI, t1, ALU.mult, ALU.add)
    p.ts(t1, t1, 3.1415925, -3.1415925, ALU.min, ALU.max)
    p.act(out_cos, t1, AF.Sin)


def load_w_bf16(p, dst, src, rows, cols, stage, colchunk, tagi=[0]):
    nrc = rows // 128
    engs = ("dve", "pool", "act")
    for c in range(nrc):
        for c0 in range(0, cols, colchunk):
            w = min(colchunk, cols - c0)
            i = tagi[0]
            tagi[0] += 1
            st = stage.k(i % 2)[:, (i % 2), 0:w]
            p.dma("sp", st, src[c * 128:(c + 1) * 128, c0:c0 + w])
            p.copy(dst.k(c)[:, c, c0:c0 + w], st, eng=engs[i % 3])


def norm_T(p, C, xb, nt, gcol, hT, xn, ss, pst):
    for j in range(nt):
        p.act(xn[:, j, :], xb[:, j, :], AF.Square, accum=ss[:, j:j + 1])
    p.ts(ss[:, 4:4 + nt], ss[:, 0:nt], 1.0 / D, EPS, ALU.mult, ALU.add)
    p.act(ss[:, 4:4 + nt], ss[:, 4:4 + nt], AF.Sqrt)
    p.recip(ss[:, 8:8 + nt], ss[:, 4:4 + nt])
    for j in range(nt):
        p.ts(xn[:, j, :], xb[:, j, :], ss[:, 8 + j:9 + j], None, ALU.mult)
    for c in range(8):
        ps = pst[c % 2]
        pv = ps.bc(BF16)
        for j in range(nt):
            p.tr(pv[:, j * 128:(j + 1) * 128], xn[:, j, c * 128:(c + 1) * 128], C.ident_b)
        p.act(hT[:, c, 0:nt * 128], pv[:, 0:nt * 128], AF.Copy, scale=gcol[:, c:c + 1],
              eng=("act" if c % 2 == 0 else "act"))


def ffn_phase(p, C, x_src, x_dst, wg, wu, wd, gcol, S, TB=512):
    nt = TB // 128
    NF = DFF // 128
    Wg = p.sb("Wg", [128, 8, DFF], BF16)
    Wu = p.sb("Wu", [128, 8, DFF], BF16)
    Wd = p.sb("Wd", [128, NF, D], BF16)
    stage = p.sb("stage", [128, 2, 704], F32)
    xbs = [p.sb(f"xb{i}", [128, nt, D], F32) for i in range(2)]
    xn = p.sb("xn", [128, 2, D], BF16)
    hT = p.sb("hT", [128, 8, TB], BF16)
    actT = p.sb("actT", [128, NF, TB], BF16)
    sg = [p.sb(f"sg{i}", [128, TB], BF16) for i in range(2)]
    ss = p.sb("ss", [128, 12], F32)
    PS = make_psum(p)
    load_w_bf16(p, Wg, wg, D, DFF, stage, 704)
    load_w_bf16(p, Wu, wu, D, DFF, stage, 704)
    load_w_bf16(p, Wd, wd, DFF, D, stage, 512)
    for b in range(S // TB):
        xb = xbs[b % 2]
        rows = x_src.k(b)[b * TB:(b + 1) * TB, :].re("(j q) d -> q j d", q=128)
        p.dma("sp", xb, rows)
        for hh in range(nt // 2):
            norm_T(p, C, xb[:, 2 * hh:2 * hh + 2], 2, gcol, hT[:, :, hh * 256:(hh + 1) * 256], xn, ss, PS[0:2])
        for f in range(NF):
            pg = PS[2 + (f % 2)]
            pu = PS[4 + (f % 2)]
            for c in range(8):
                p.mm(pg[:, 0:TB], Wg.k(c)[:, c, f * 128:(f + 1) * 128], hT[:, c, :], start=(c == 0), stop=(c == 7))
            for c in range(8):
                p.mm(pu[:, 0:TB], Wu.k(c)[:, c, f * 128:(f + 1) * 128], hT[:, c, :], start=(c == 0), stop=(c == 7))
            s = sg[f % 2]
            p.act(s, pg[:, 0:TB], AF.Silu)
            p.tt(actT.k(f)[:, f, :], s, pu[:, 0:TB], ALU.mult)
        for j in range(nt):
            for dh in range(2):
                po = PS[6 + ((j * 2 + dh) % 2)]
                for f in range(NF):
                    p.mm(po, actT.k(f)[:, f, j * 128:(j + 1) * 128], Wd.k(f)[:, f, dh * 512:(dh + 1) * 512],
                         start=(f == 0), stop=(f == NF - 1))
                p.stt(xb[:, j, dh * 512:(dh + 1) * 512], po, 0.5, xb[:, j, dh * 512:(dh + 1) * 512], ALU.mult, ALU.add)
        p.dma("sp", x_dst.k(b)[b * TB:(b + 1) * TB, :].re("(j q) d -> q j d", q=128), xb)
    p.flush()


def load_block(p, C, x, b, TB, xb):
    p.dma("sp", xb, x.k(("r", b))[b * TB:(b + 1) * TB, :].re("(j q) d -> q j d", q=128))


def rot_pair(p, out3, in3, lo, half, cosb, sinb, ra, rb):
    t1 = in3[:, :, lo:lo + half]
    t2 = in3[:, :, lo + half:lo + 2 * half]
    p.tt(ra, t1, cosb, ALU.mult)
    p.tt(rb, t2, sinb, ALU.mult)
    p.tt(out3[:, :, lo:lo + half], ra, rb, ALU.subtract)
    p.tt(ra, t2, cosb, ALU.mult)
    p.tt(rb, t1, sinb, ALU.mult)
    p.tt(out3[:, :, lo + half:lo + 2 * half], ra, rb, ALU.add)


def rsqrt_mean(p, out, ssum, n, tmp):
    p.ts(tmp, ssum, 1.0 / n, EPS, ALU.mult, ALU.add)
    p.act(tmp, tmp, AF.Sqrt)
    p.recip(out, tmp)


def interleave(gens):
    gens = list(gens)
    while gens:
        for g in list(gens):
            try:
                next(g)
            except StopIteration:
                gens.remove(g)


def even_proj_phase(p, C, i, gcol, S):
    TB, nt = 512, 4
    Win = p.sb("Win", [128, 8, 1792], BF16)
    stage = p.sb("stage", [128, 2, 1792], F32)
    load_w_bf16(p, Win, C.ab_w_in[i], D, 1792, stage, 1792)
    gqk = p.sb("gqk", [128, 16, 64], F32)
    p.dma("sp", gqk, C.gqk[i])
    p.ts(gqk[:, 0:8, :], gqk[:, 0:8, :], 0.125, None, ALU.mult)
    xbs = [p.sb(f"xb{k}", [128, nt, D], F32) for k in range(2)]
    xn = p.sb("xn", [128, nt, D], BF16)
    hT = p.sb("hT", [128, 8, TB], BF16)
    ss = p.sb("ss", [128, 12], F32)
    cs = p.sb("cs", [128, 2, nt, 8], F32)
    TMP = []
    for k in range(2):
        TMP.append(dict(
            sq=p.sb(f"sq{k}", [128, 1024], F32), qn=p.sb(f"qn{k}", [128, 16, 64], F32),
            qb=p.sb(f"qb{k}", [128, 16, 64], BF16), ssq=p.sb(f"ssq{k}", [128, 16], F32),
            rs=p.sb(f"rs{k}", [128, 16], F32), ra=p.sb(f"ra{k}", [128, 16, 8], F32),
            rb=p.sb(f"rb{k}", [128, 16, 8], F32)))
    qkT = p.sb("qkT", [128, 8, TB], BF16)
    vb = p.sb("vb", [128, nt, 512], BF16)
    fb = p.sb("fb", [128, nt, 256], BF16)
    PS = make_psum(p)
    for b in range(S // TB):
        xb = xbs[b % 2]
        load_block(p, C, C.xres, b, TB, xb)
        p.dma("sp", cs[:, 0], C.cosA[:, b * nt:(b + 1) * nt, :])
        p.dma("sp", cs[:, 1], C.sinA[:, b * nt:(b + 1) * nt, :])
        norm_T(p, C, xb, nt, gcol, hT, xn, ss, PS[0:2])

        def tile(j):
            B = TMP[j % 2]
            sq, qn, qb, ssq, rs, ra, rb = B["sq"], B["qn"], B["qb"], B["ssq"], B["rs"], B["ra"], B["rb"]
            pq, pk = (PS[2], PS[3]) if j % 2 == 0 else (PS[4], PS[5])
            pv, pf = PS[6], PS[7]
            for (ps, c0, w) in ((pq, 0, 512), (pk, 512, 512), (pv, 1024, 512), (pf, 1536, 256)):
                for c in range(8):
                    p.mm(ps[:, 0:w], hT[:, c, j * 128:(j + 1) * 128], Win.k(c)[:, c, c0:c0 + w],
                         start=(c == 0), stop=(c == 7))
            p.copy(vb.k(j)[:, j, :], pv, eng="act")
            p.copy(fb.k(j)[:, j, :], pf[:, 0:256], eng="act")
            yield
            p.act(sq[:, 0:512], pq, AF.Square)
            p.act(sq[:, 512:1024], pk, AF.Square)
            yield
            p.reduce(ssq, sq.re("p (h e) -> p h e", e=64))
            yield
            p.ts(ssq, ssq, 1.0 / 64, EPS, ALU.mult, ALU.add)
            yield
            p.act(ssq, ssq, AF.Sqrt)
            yield
            p.recip(rs, ssq)
            yield
            p.tt(qn[:, 0:8, :], pq.re("p (h e) -> p h e", e=64), rs[:, 0:8].ub(2, [128, 8, 64]), ALU.mult)
            yield
            p.tt(qn[:, 8:16, :], pk.re("p (h e) -> p h e", e=64), rs[:, 8:16].ub(2, [128, 8, 64]), ALU.mult)
            yield
            p.tt(qn, qn, gqk, ALU.mult)
            yield
            cosb = cs[:, 0, j, :].ub(1, [128, 16, 8])
            sinb = cs[:, 1, j, :].ub(1, [128, 16, 8])
            t1, t2 = qn[:, :, 0:8], qn[:, :, 8:16]
            p.copy(qb[:, :, 16:64], qn[:, :, 16:64], eng="pool")
            p.tt(ra, t1, cosb, ALU.mult)
            yield
            p.tt(rb, t2, sinb, ALU.mult)
            yield
            p.tt(qb[:, :, 0:8], ra, rb, ALU.subtract)
            yield
            p.tt(ra, t2, cosb, ALU.mult)
            yield
            p.tt(rb, t1, sinb, ALU.mult)
            yield
            p.tt(qb[:, :, 8:16], ra, rb, ALU.add)
            yield
            pt = PS[j % 2].bc(BF16)
            qbf = qb.re("p h e -> p (h e)")
            for c in range(8):
                p.tr(pt[:, c * 128:(c + 1) * 128], qbf[:, c * 128:(c + 1) * 128], C.ident_b)
            p.copy(qkT.k(j)[:, :, j * 128:(j + 1) * 128], pt.re("p (c t) -> p c t", t=128), eng="act")

        interleave([tile(0), tile(1)])
        interleave([tile(2), tile(3)])
        sl = slice(b * TB, (b + 1) * TB)
        ak = lambda t: [t.k(j) for j in range(nt)]
        p.dma("sp", C.QT.k(b)[:, sl].re("(c q) t -> q c t", q=128), qkT[:, 0:4, :], extra_reads=ak(qkT))
        p.dma("sp", C.KT.k(b)[:, sl].re("(c q) t -> q c t", q=128), qkT[:, 4:8, :], extra_reads=ak(qkT))
        p.dma("sp", C.Vs.k(b)[sl, :].re("(j q) e -> q j e", q=128), vb, extra_reads=ak(vb))
        p.dma("sp", C.Fs.k(b)[sl, :].re("(j q) e -> q j e", q=128), fb, extra_reads=ak(fb))
    p.flush()


def attention_phase(p, C, S, nheads, dk, qsrc, ksrc, Vs, OT, masked):
    NB, NQ = S // 128, S // 512
    qt = [p.sb(f"qt{k}", [128, S], BF16) for k in range(2)]
    kt = [p.sb(f"kt{k}", [128, S], BF16) for k in range(2)]
    vx = [p.sb(f"vx{k}", [128, NB, 128], BF16) for k in range(2)]
    pex = [p.sb(f"pex{k}", [128, 512], BF16) for k in range(4)]
    den = p.sb("den", [128, 512], F32)
    rden = p.sb("rden", [64, 512], F32)
    ot = [p.sb(f"ot{k}", [64, 512], BF16) for k in range(2)]
    onesf = p.sb("onesf", [128, 64], F32)
    p.memset(onesf, 1.0)
    for k in range(2):
        p.memset(vx[k][:, :, 64:128], 1.0)
    KD = dk
    if dk == 64:
        KD = 128
        for k in range(2):
            p.memset(qt[k][64:128, :], 0.0, eng="pool")
            p.memset(kt[k][64:128, :], 0.0, eng="pool")
    if masked:
        mk = p.sb("mk", [128, 20, 512], BF16)
        mst = p.sb("mst", [128, 2, 512], F32)
        for o in range(20):
            p.dma("sp", mst.k(o % 2)[:, o % 2, :], C.dmask[o])
            p.copy(mk.k(o)[:, o, :], mst.k(o % 2)[:, o % 2, :], eng=("dve" if o % 2 else "pool"))
    PS = make_psum(p)
    NBUF, LA = 4, 3
    tiles = []
    for h in range(nheads):
        for qb in range(NQ):
            q0 = qb * 512
            if masked:
                kbs = [kb for kb in range(NB) if -1024 <= kb * 128 - q0 <= 1408]
            else:
                kbs = list(range(NB))
            for idx, kb in enumerate(kbs):
                tiles.append((h, qb, kb, idx == 0, idx == len(kbs) - 1))
    n = len(tiles)
    loaded = set()
    for t in range(n + LA):
        if t < n:
            h, qb, kb, first, last = tiles[t]
            q_, k_, v_ = qt[h % 2], kt[h % 2], vx[h % 2]
            if h not in loaded:
                loaded.add(h)
                p.dma("sp", q_[0:dk, :], qsrc(h))
                p.dma("sp", k_[0:dk, :], ksrc(h))
                p.dma("sp", v_[:, :, 0:64], Vs[:, h * 64:(h + 1) * 64].re("(n q) e -> q n e", q=128))
            q0 = qb * 512
            ps = PS[t % NBUF]
            pe_ = pex[t % NBUF]
            p.mm(ps, k_[0:KD, kb * 128:(kb + 1) * 128], q_[0:KD, q0:q0 + 512])
            p.act(pe_, ps, AF.Exp)
            if masked:
                o = (kb * 128 - q0 + 1024) // 128
                p.tt(pe_, pe_, mk.k(o)[:, o, :], ALU.mult)
        u = t - LA
        if u >= 0:
            h, qb, kb, first, last = tiles[u]
            v_ = vx[h % 2]
            q0 = qb * 512
            po = PS[4 + (qb % 2)]
            p.mm(po, v_[:, kb, :], pex[u % NBUF], start=first, stop=last)
            if last:
                p.copy(den[64:65, :], po[64:65, :], eng="act")
                pb = PS[6 + (qb % 2)]
                p.mm(pb[0:64, :], onesf[64:65, 0:64], den[64:65, :])
                p.recip(rden, pb[0:64, :])
                o_ = ot[qb % 2]
                p.tt(o_, po[0:64, :], rden, ALU.mult)
                p.dma("sp", OT.k((h, qb))[h * 64:(h + 1) * 64, q0:q0 + 512], o_)
    p.flush()


def fnet_phase(p, C, S):
    S2 = S // 128
    ya = p.sb("ya", [128, S2, 256], BF16)
    p.dma("sp", ya.re("p a c -> p (a c)"), C.Fs.re("(p a) c -> p (a c)", p=128))
    w1f = p.sb("w1f", [128, 256], F32)
    w1 = p.sb("w1", [128, 256], BF16)
    p.dma("sp", w1f, C.fW1)
    p.copy(w1, w1f)
    tw = p.sb("tw", [S2, 2, 128], F32)
    p.dma("sp", tw, C.fT)
    w3f = p.sb("w3f", [S2, 2, 2 * S2], F32)
    w3 = p.sb("w3", [S2, 2, 2 * S2], BF16)
    p.dma("sp", w3f, C.fW3)
    p.copy(w3, w3f)
    PS = make_psum(p)
    ta = [p.sb(f"ta{k}", [S2, 2, 128], F32) for k in range(4)]
    z2r = p.sb("z2r", [S2, 128, 128], BF16)
    z2i = p.sb("z2i", [S2, 128, 128], BF16)
    zt = p.sb("zt", [128, 2, S2, 128], BF16)
    for half in range(2):
        for cp in range(64):
            ps = PS[cp % 2]
            for c2 in range(2):
                ch = half * 128 + cp * 2 + c2
                p.mm(ps[0:S2, c2 * 256:(c2 + 1) * 256], ya[:, :, ch], w1)
            pv = ps[0:S2, :].re("p (c x k) -> p c x k", c=2, x=2)
            zr, zi = pv[:, :, 0, :], pv[:, :, 1, :]
            tc = tw[:, 0, :].ub(1, [S2, 2, 128])
            tsn = tw[:, 1, :].ub(1, [S2, 2, 128])
            p.tt(ta[0], zr, tc, ALU.mult)
            p.tt(ta[1], zi, tsn, ALU.mult)
            p.tt(ta[2], zi, tc, ALU.mult)
            p.tt(ta[3], zr, tsn, ALU.mult)
            cl = cp * 2
            p.tt(z2r[:, :, cl:cl + 2].re("p k c -> p c k"), ta[0], ta[1], ALU.add, eng="pool")
            p.tt(z2i[:, :, cl:cl + 2].re("p k c -> p c k"), ta[2], ta[3], ALU.subtract, eng="pool")
        for kg in range(32):
            ps = PS[2 + (kg % 2)]
            for kk in range(4):
                k1 = kg * 4 + kk
                o = ps[:, kk * 128:kk * 128 + 2 * S2]
                p.mm(o, z2r[:, k1, :], w3[:, 0, :], start=True, stop=False)
                p.mm(o, z2i[:, k1, :], w3[:, 1, :], start=False, stop=True)
            src = ps.re("p (q x) -> p q x", x=128)[:, :, 0:2 * S2]
            dst = zt[:, :, :, kg * 4:(kg + 1) * 4].re("p r k q -> p q (r k)")
            p.copy(dst, src, eng=("act" if kg % 2 else "dve"))
        p.dma("sp", C.ZrT[half * 128:(half + 1) * 128, :], zt[:, 0].re("p k q -> p (k q)"))
        p.dma("sp", C.ZiT[half * 128:(half + 1) * 128, :], zt[:, 1].re("p k q -> p (k q)"))
    p.flush()


def outproj_phase(p, C, S, wout, srcs, fold=None):
    TB, nt = 512, 4
    stage = p.sb("stage", [128, 2, 1024], F32)
    chunks = []
    for (t, n) in srcs:
        for c in range(n):
            chunks.append((t, c))
    NCH = len(chunks)
    W = p.sb("W", [128, NCH, 1024], BF16)
    PS = make_psum(p)
    if fold is None:
        load_w_bf16(p, W, wout, NCH * 128, 1024, stage, 1024)
    else:
        i = fold
        wtmp = p.sb("wtmp", [128, 6, 1024], BF16)
        load_w_bf16(p, wtmp, wout, 768, 1024, stage, 1024)
        for c in range(4):
            p.copy(W.k(c)[:, c, :], wtmp.k(c)[:, c, :])
        cf = p.sb("cf", [128, 4, 128], F32)
        cb = p.sb("cb", [128, 4, 128], BF16)
        p.dma("sp", cf, C.fBD[i])
        p.copy(cb, cf)
        m1 = p.sb("m1", [128, 2, 1024], BF16)
        for pr in range(2):
            for dh in range(2):
                ps = PS[dh]
                p.mm(ps, cb[:, pr, :], wtmp.k(4 + pr)[:, 4 + pr, dh * 512:(dh + 1) * 512])
                p.copy(m1[:, pr, dh * 512:(dh + 1) * 512], ps)
        for ri in range(2):
            for pr in range(2):
                for dh in range(2):
                    ps = PS[2 + dh]
                    p.mm(ps, cb[:, 2 + ri, :], m1[:, pr, dh * 512:(dh + 1) * 512])
                    c = 4 + ri * 2 + pr
                    p.copy(W.k(c)[:, c, dh * 512:(dh + 1) * 512], ps)
    xbs = [p.sb(f"xb{k}", [128, nt, D], F32) for k in range(2)]
    mT = [p.sb(f"mT{k}", [128, NCH, TB], BF16) for k in range(2)]
    for b in range(S // TB):
        xb, m = xbs[b % 2], mT[b % 2]
        load_block(p, C, C.xres, b, TB, xb)
        for ci, (t, c) in enumerate(chunks):
            p.dma("sp", m.k(ci)[:, ci, :], t[c * 128:(c + 1) * 128, b * TB:(b + 1) * TB])
        for j in range(nt):
            for dh in range(2):
                ps = PS[4 + ((j * 2 + dh) % 4)]
                for ci in range(NCH):
                    p.mm(ps, m.k(ci)[:, ci, j * 128:(j + 1) * 128], W.k(ci)[:, ci, dh * 512:(dh + 1) * 512],
                         start=(ci == 0), stop=(ci == NCH - 1))
                p.tt(xb[:, j, dh * 512:(dh + 1) * 512], ps, xb[:, j, dh * 512:(dh + 1) * 512], ALU.add)
        p.dma("sp", C.xres.k(("r", b))[b * TB:(b + 1) * TB, :].re("(j q) d -> q j d", q=128), xb)
    p.flush()


def odd_proj_phase(p, C, i, gcol, S):
    TB, nt = 512, 4
    Win = p.sb("Win", [128, 8, 672], BF16)
    Wq = p.sb("Wq", [128, 2, 768], BF16)
    Wkv = p.sb("Wkv", [128, 1, 1024], BF16)
    stage = p.sb("stage", [128, 2, 1024], F32)
    load_w_bf16(p, Win, C.cd_w_in[i], D, 672, stage, 672)
    load_w_bf16(p, Wq, C.c_w_q_up[i], 256, 768, stage, 768)
    load_w_bf16(p, Wkv, C.c_w_kv_up[i], 128, 1024, stage, 1024)
    glat = p.sb("glat", [128, 384], F32)
    p.dma("sp", glat, C.glat[i])
    gq = p.sb("gq", [128, 8, 96], F32)
    gk = p.sb("gk", [128, 8, 96], F32)
    p.dma("sp", gq, C.gq96[i])
    p.dma("sp", gk, C.gk96[i])
    p.ts(gq, gq, 96.0 ** -0.5, None, ALU.mult)
    xbs = [p.sb(f"xb{k}", [128, nt, D], F32) for k in range(2)]
    xn = p.sb("xn", [128, nt, D], BF16)
    hT = p.sb("hT", [128, 8, TB], BF16)
    ss = p.sb("ss", [128, 12], F32)
    cs = p.sb("cs", [128, 2, nt, 16], F32)
    TMP = []
    for k in range(2):
        TMP.append(dict(
            junk=p.sb(f"junk{k}", [128, 1024], F32), junq=p.sb(f"junq{k}", [128, 768], F32),
            ssl=p.sb(f"ssl{k}", [128, 4], F32), rl=p.sb(f"rl{k}", [128, 2], F32),
            qln=p.sb(f"qln{k}", [128, 384], F32), lnb=p.sb(f"lnb{k}", [128, 384], BF16),
            lnT=p.sb(f"lnT{k}", [128, 3, 128], BF16), kpe=p.sb(f"kpe{k}", [128, 32], F32),
            kg=p.sb(f"kg{k}", [128, 1, 32], F32), R=p.sb(f"R{k}", [128, 1, 32], F32),
            qf=p.sb(f"qf{k}", [128, 8, 96], F32), kvf=p.sb(f"kvf{k}", [128, 8, 128], F32),
            s8q=p.sb(f"s8q{k}", [128, 8], F32), s8k=p.sb(f"s8k{k}", [128, 8], F32),
            rq=p.sb(f"rq{k}", [128, 8], F32), rk=p.sb(f"rk{k}", [128, 8], F32),
            sp1=p.sb(f"sp1{k}", [128, 1], F32), jpe=p.sb(f"jpe{k}", [128, 32], F32),
            ra=p.sb(f"ra{k}", [128, 8, 16], F32), rb=p.sb(f"rb{k}", [128, 8, 16], F32),
            ra2=p.sb(f"ra2{k}", [128, 1, 16], F32), rb2=p.sb(f"rb2{k}", [128, 1, 16], F32),
            tk=p.sb(f"tk{k}", [128, 8, 64], F32),
            qb3=p.sb(f"qb3{k}", [128, 8, 96], BF16), kb3=p.sb(f"kb3{k}", [128, 8, 96], BF16)))
    qTb = p.sb("qTb", [128, 8, TB], BF16)
    kTb = p.sb("kTb", [128, 8, TB], BF16)
    vb = p.sb("vb", [128, nt, 512], BF16)
    uTb = p.sb("uTb", [128, 2, TB], BF16)
    PS = make_psum(p)
    for b in range(S // TB):
        xb = xbs[b % 2]
        load_block(p, C, C.xres, b, TB, xb)
        p.dma("sp", cs[:, 0], C.cosC[:, b * nt:(b + 1) * nt, :])
        p.dma("sp", cs[:, 1], C.sinC[:, b * nt:(b + 1) * nt, :])
        norm_T(p, C, xb, nt, gcol, hT, xn, ss, PS[0:2])
        for m in range(2):
            pu = PS[3]
            for c in range(8):
                p.mm(pu, Win.k(c)[:, c, 416 + m * 128:416 + (m + 1) * 128], hT[:, c, :], start=(c == 0), stop=(c == 7))
            p.copy(uTb[:, m, :], pu, eng="act")

        def tile(j):
            B = TMP[j % 2]
            junk, junq, ssl, rl, qln, lnb, lnT, kpe = (B[n] for n in ("junk", "junq", "ssl", "rl", "qln", "lnb", "lnT", "kpe"))
            kg, R, qf, kvf, s8q, s8k, rq, rk = (B[n] for n in ("kg", "R", "qf", "kvf", "s8q", "s8k", "rq", "rk"))
            sp1, jpe, ra, rb, ra2, rb2, tk, qb3, kb3 = (B[n] for n in ("sp1", "jpe", "ra", "rb", "ra2", "rb2", "tk", "qb3", "kb3"))
            pl = PS[2 + (j % 2)]
            for c in range(8):
                p.mm(pl[:, 0:416], hT[:, c, j * 128:(j + 1) * 128], Win.k(c)[:, c, 0:416], start=(c == 0), stop=(c == 7))
            yield
            p.act(junk[:, 0:256], pl[:, 0:256], AF.Square, accum=ssl[:, 0:1])
            p.act(junk[:, 256:384], pl[:, 256:384], AF.Square, accum=ssl[:, 1:2])
            p.copy(kpe, pl[:, 384:416], eng="act")
            yield
            p.ts(ssl[:, 2:3], ssl[:, 0:1], 1.0 / 256, EPS, ALU.mult, ALU.add)
            p.ts(ssl[:, 3:4], ssl[:, 1:2], 1.0 / 128, EPS, ALU.mult, ALU.add)
            yield
            p.act(ssl[:, 2:4], ssl[:, 2:4], AF.Sqrt)
            yield
            p.recip(rl, ssl[:, 2:4])
            yield
            p.ts(qln[:, 0:256], pl[:, 0:256], rl[:, 0:1], None, ALU.mult)
            p.ts(qln[:, 256:384], pl[:, 256:384], rl[:, 1:2], None, ALU.mult)
            yield
            p.tt(lnb, qln, glat, ALU.mult)
            yield
            pt = PS[j % 2].bc(BF16)
            for c in range(3):
                p.tr(pt[:, c * 128:(c + 1) * 128], lnb[:, c * 128:(c + 1) * 128], C.ident_b)
            p.copy(lnT, pt[:, 0:384].re("p (c t) -> p c t", t=128), eng="act")
            yield
            pq0, pq1, pk0, pk1 = PS[4], PS[5], PS[6], PS[7]
            for c2 in range(2):
                p.mm(pq0, lnT[:, c2, :], Wq.k(c2)[:, c2, 0:512], start=(c2 == 0), stop=(c2 == 1))
            for c2 in range(2):
                p.mm(pq1[:, 0:256], lnT[:, c2, :], Wq.k(c2)[:, c2, 512:768], start=(c2 == 0), stop=(c2 == 1))
            p.mm(pk0, lnT[:, 2, :], Wkv.k(0)[:, 0, 0:512])
            p.mm(pk1, lnT[:, 2, :], Wkv.k(0)[:, 0, 512:1024])
            qff = qf.re("p h e -> p (h e)")
            kvff = kvf.re("p h e -> p (h e)")
            p.copy(qff[:, 0:512], pq0, eng="act")
            p.copy(qff[:, 512:768], pq1[:, 0:256], eng="act")
            p.copy(kvff[:, 0:512], pk0, eng="act")
            p.copy(kvff[:, 512:1024], pk1, eng="act")
            yield
            p.act(junq, qff, AF.Square)
            p.act(junk, kvff, AF.Square)
            p.act(jpe, kpe, AF.Square, accum=sp1)
            yield
            p.reduce(s8q, junq.re("p (h e) -> p h e", e=96))
            p.reduce(s8k, junk.re("p (h e) -> p h e", e=128)[:, :, 0:64])
            yield
            p.ts(s8q, s8q, 1.0 / 96, EPS, ALU.mult, ALU.add)
            p.stt(s8k, s8k, sp1[:, 0:1], s8k, ALU.add, ALU.bypass) if False else p.ts(s8k, s8k, sp1[:, 0:1], None, ALU.add)
            yield
            p.act(s8q, s8q, AF.Sqrt)
            p.ts(s8k, s8k, 1.0 / 96, EPS, ALU.mult, ALU.add)
            yield
            p.recip(rq, s8q)
            p.act(s8k, s8k, AF.Sqrt)
            yield
            p.tt(qf, qf, rq.ub(2, [128, 8, 96]), ALU.mult)
            p.recip(rk, s8k)
            yield
            p.tt(qf, qf, gq, ALU.mult)
            p.tt(tk, kvf[:, :, 0:64], rk.ub(2, [128, 8, 64]), ALU.mult)
            p.tt(kg[:, 0, :], kpe, gk[:, 0, 64:96], ALU.mult)
            yield
            cosb = cs[:, 0, j, :].ub(1, [128, 8, 16])
            sinb = cs[:, 1, j, :].ub(1, [128, 8, 16])
            cos1 = cs[:, 0, j, :].ub(1, [128, 1, 16])
            sin1 = cs[:, 1, j, :].ub(1, [128, 1, 16])
            p.copy(qb3[:, :, 0:64], qf[:, :, 0:64], eng="pool")
            p.copy(vb.k(j)[:, j, :].re("p (h e) -> p h e", e=64), kvf[:, :, 64:128], eng="pool")
            p.tt(kb3[:, :, 0:64], tk, gk[:, :, 0:64], ALU.mult)
            q1, q2 = qf[:, :, 64:80], qf[:, :, 80:96]
            k1, k2 = kg[:, :, 0:16], kg[:, :, 16:32]
            p.tt(ra, q1, cosb, ALU.mult)
            p.tt(ra2, k1, cos1, ALU.mult)
            yield
            p.tt(rb, q2, sinb, ALU.mult)
            p.tt(rb2, k2, sin1, ALU.mult)
            yield
            p.tt(qb3[:, :, 64:80], ra, rb, ALU.subtract)
            p.tt(R[:, :, 0:16], ra2, rb2, ALU.subtract)
            yield
            p.tt(ra, q2, cosb, ALU.mult)
            p.tt(ra2, k2, cos1, ALU.mult)
            yield
            p.tt(rb, q1, sinb, ALU.mult)
            p.tt(rb2, k1, sin1, ALU.mult)
            yield
            p.tt(qb3[:, :, 80:96], ra, rb, ALU.add)
            p.tt(R[:, :, 16:32], ra2, rb2, ALU.add)
            yield
            p.tt(kb3[:, :, 64:96], R[:, 0, :].ub(1, [128, 8, 32]), rk.ub(2, [128, 8, 32]), ALU.mult)
            yield
            ptq = PS[0].bc(BF16)
            ptk = PS[1].bc(BF16)
            for h in range(8):
                p.tr(ptq[0:96, h * 128:(h + 1) * 128], qb3[:, h, :], C.ident_b)
            for h in range(8):
                p.tr(ptk[0:96, h * 128:(h + 1) * 128], kb3[:, h, :], C.ident_b)
            p.copy(qTb.k(j)[0:96, :, j * 128:(j + 1) * 128], ptq[0:96, :].re("p (h t) -> p h t", t=128), eng="act")
            p.copy(kTb.k(j)[0:96, :, j * 128:(j + 1) * 128], ptk[0:96, :].re("p (h t) -> p h t", t=128), eng="act")

        interleave([tile(0), tile(1)])
        interleave([tile(2), tile(3)])
        sl = slice(b * TB, (b + 1) * TB)
        ak = lambda t: [t.k(j) for j in range(nt)]
        p.dma("sp", C.QTc.k(b)[:, :, sl].re("h e t -> e h t"), qTb[0:96], extra_reads=ak(qTb))
        p.dma("sp", C.KTc.k(b)[:, :, sl].re("h e t -> e h t"), kTb[0:96], extra_reads=ak(kTb))
        p.dma("sp", C.Vs.k(b)[sl, :].re("(j q) e -> q j e", q=128), vb, extra_reads=ak(vb))
        p.dma("sp", C.UT.k(b)[:, sl].re("(m q) t -> q m t", q=128), uTb)
    p.flush()


def s5_phase(p, C, i, S):
    TB = 512
    NBk = S // TB
    names = ["lre", "lim", "lst", "step", "rho", "th", "cth", "sth", "lbr", "lbi", "nr", "den", "kr", "ki", "t1", "t2", "t3"]
    T = {n: p.sb("s5_" + n, [128, 16], F32) for n in names}
    tiI = p.sb("s5_ti", [128, 16], I32)
    p.dma("sp", T["lre"], C.s5lre[i])
    p.dma("sp", T["lim"], C.s5lim[i])
    p.dma("sp", T["lst"], C.s5lst[i])
    p.act(T["step"], T["lst"], AF.Exp)
    p.tt(T["t1"], T["lre"], T["step"], ALU.mult)
    p.act(T["rho"], T["t1"], AF.Exp)
    p.tt(T["th"], T["lim"], T["step"], ALU.mult)
    thr = p.sb("s5_thr", [128, 16], F32)
    p.ts(T["t1"], T["th"], 1.0 / TWO_PI, None, ALU.mult)
    p.copy(tiI, T["t1"])
    p.copy(T["t1"], tiI)
    p.stt(thr, T["t1"], -CW1, T["th"], ALU.mult, ALU.add)
    p.stt(thr, T["t1"], -CW2, thr, ALU.mult, ALU.add)
    sincos(p, T["sth"], T["cth"], thr, T["t1"], T["t2"], tiI)
    p.tt(T["lbr"], T["rho"], T["cth"], ALU.mult)
    p.tt(T["lbi"], T["rho"], T["sth"], ALU.mult)
    p.ts(T["nr"], T["lbr"], -1.0, None, ALU.add)
    p.tt(T["t1"], T["lre"], T["lre"], ALU.mult)
    p.tt(T["t2"], T["lim"], T["lim"], ALU.mult)
    p.tt(T["den"], T["t1"], T["t2"], ALU.add)
    p.recip(T["den"], T["den"])
    p.tt(T["t1"], T["nr"], T["lre"], ALU.mult)
    p.tt(T["t2"], T["lbi"], T["lim"], ALU.mult)
    p.tt(T["t1"], T["t1"], T["t2"], ALU.add)
    p.tt(T["kr"], T["t1"], T["den"], ALU.mult)
    p.tt(T["t1"], T["lbi"], T["lre"], ALU.mult)
    p.tt(T["t2"], T["nr"], T["lim"], ALU.mult)
    p.tt(T["t1"], T["t1"], T["t2"], ALU.subtract)
    p.tt(T["ki"], T["t1"], T["den"], ALU.mult)
    bre = p.sb("s5_bre", [128, 16, 16], F32)
    bim = p.sb("s5_bim", [128, 16, 16], F32)
    bbr = p.sb("s5_bbr", [128, 16, 16], F32)
    bbi = p.sb("s5_bbi", [128, 16, 16], F32)
    bt = p.sb("s5_bt", [128, 16, 16], F32)
    p.dma("sp", bre, C.s5bre[i])
    p.dma("sp", bim, C.s5bim[i])
    krb, kib = T["kr"].ub(2, [128, 16, 16]), T["ki"].ub(2, [128, 16, 16])
    p.tt(bbr, bre, krb, ALU.mult)
    p.tt(bt, bim, kib, ALU.mult)
    p.tt(bbr, bbr, bt, ALU.subtract)
    p.tt(bbi, bim, krb, ALU.mult)
    p.tt(bt, bre, kib, ALU.mult)
    p.tt(bbi, bbi, bt, ALU.add)
    PS = make_psum(p)
    LB = p.sb("s5_LB", [128, 16, 2, 128], BF16)
    blk = p.sb("s5_blk", [128, 2, 128], F32)
    for dg in range(16):
        gq_ = dg % 4
        p.memset(blk, 0.0)
        for ri, src in enumerate((bbr, bbi)):
            for g2 in range(2):
                c0 = (gq_ * 2 + g2) * 16
                p.copy(blk[g2 * 64:(g2 + 1) * 64, ri, c0:c0 + 16], src[g2 * 64:(g2 + 1) * 64, dg, :])
        for ri in range(2):
            ps = PS[ri]
            p.tr(ps[:, 0:128], blk[:, ri, :], C.ident_f)
            p.copy(LB[:, dg, ri, :], ps[:, 0:128])
    CT = p.sb("s5_CT", [128, 16, 2, 128], BF16)
    cst = p.sb("s5_cst", [128, 16, 128], F32)
    p.dma("sp", cst, C.s5ctr[i])
    p.copy(CT[:, :, 0, :], cst)
    p.dma("sp", cst, C.s5cti[i])
    p.ts(CT[:, :, 1, :], cst, -1.0, None, ALU.mult)
    NJ = 513
    iot = p.sb("s5_iot", [128, NJ], F32)
    p.dma("sp", iot, C.iota)
    RC = p.sb("s5_RC", [128, 16, NJ], F32)
    RS = p.sb("s5_RS", [128, 16, NJ], F32)
    a1 = p.sb("s5_a1", [128, NJ], F32)
    a2 = p.sb("s5_a2", [128, NJ], F32)
    a3 = p.sb("s5_a3", [128, NJ], F32)
    ai = p.sb("s5_ai", [128, NJ], I32)
    for dg in range(16):
        p.ts(a1, iot, thr[:, dg:dg + 1], None, ALU.mult)
        sincos(p, RS[:, dg, :], RC[:, dg, :], a1, a2, a3, ai)
    dsk = p.sb("s5_dsk", [128, 2], F32)
    bgl = p.sb("s5_bgl", [128, 2], F32)
    p.dma("sp", dsk, C.s5dsk[i])
    p.dma("sp", bgl, C.s5bgl[i])
    Wgl = p.sb("s5_Wgl", [128, 2, 256], BF16)
    stage = p.sb("stage", [128, 2, 256], F32)
    load_w_bf16(p, Wgl, C.d_w_glu[i], 256, 256, stage, 256)
    uT = p.sb("s5_uT", [128, 2, S], BF16)
    p.dma("sp", uT, C.UT.re("(m q) t -> q m t", q=128))
    carry = p.sb("s5_carry", [128, 16, 2], F32)
    cnew_ = [p.sb(f"s5_cnew{k}", [128, 4], F32) for k in range(2)]
    er_ = [p.sb(f"s5_er{k}", [128, TB], F32) for k in range(2)]
    ei_ = [p.sb(f"s5_ei{k}", [128, TB], F32) for k in range(2)]
    wr_ = [p.sb(f"s5_wr{k}", [128, TB], F32) for k in range(2)]
    wi_ = [p.sb(f"s5_wi{k}", [128, TB], F32) for k in range(2)]
    m1_ = [p.sb(f"s5_m1{k}", [128, TB], F32) for k in range(2)]
    m2_ = [p.sb(f"s5_m2{k}", [128, TB], F32) for k in range(2)]
    n1_ = [p.sb(f"s5_n1{k}", [128, TB], F32) for k in range(2)]
    n2_ = [p.sb(f"s5_n2{k}", [128, TB], F32) for k in range(2)]
    xr = [p.sb(f"s5_xr{k}", [128, TB], BF16) for k in range(2)]
    xi = [p.sb(f"s5_xi{k}", [128, TB], BF16) for k in range(2)]
    yf = p.sb("s5_yf", [128, 2, TB], F32)
    yv = p.sb("s5_yv", [128, 2, TB], F32)
    zb = p.sb("s5_zb", [128, 2, TB], BF16)
    zf = p.sb("s5_zf", [128, 2, TB], F32)
    g1 = p.sb("s5_g1", [128, TB], F32)
    g2t = p.sb("s5_g2", [128, TB], F32)
    dob = p.sb("s5_dob", [128, 2, TB], BF16)
    steps = []
    for d in range(2):
        order = range(NBk) if d == 0 else range(NBk - 1, -1, -1)
        for bi, b in enumerate(order):
            for gp in range(8):
                steps.append((d, bi, b, gp))

    def usl_of(d, b, m):
        t0_ = b * TB
        v = uT[:, m, t0_:t0_ + TB]
        return v if d == 0 else v[:, ::-1]

    def emit_bu(si):
        d, bi, b, gp = steps[si]
        dg = d * 8 + gp
        m = gp // 4
        pr_, pi_ = PS[0 + (gp % 2) * 2], PS[1 + (gp % 2) * 2]
        p.mm(pr_, LB[:, dg, 0, :], usl_of(d, b, m))
        p.mm(pi_, LB[:, dg, 1, :], usl_of(d, b, m))

    emit_bu(0)
    for si, (d, bi, b, gp) in enumerate(steps):
        if True:
            t0 = b * TB
            if gp == 0:
                py = [PS[4], PS[5]]
                cnt = [0, 0]
            if si + 1 < len(steps):
                emit_bu(si + 1)
            if True:
                dg = d * 8 + gp
                m = gp // 4
                er, ei, wr, wi = er_[gp % 2], ei_[gp % 2], wr_[gp % 2], wi_[gp % 2]
                m1, m2, n1, n2 = m1_[gp % 2], m2_[gp % 2], n1_[gp % 2], n2_[gp % 2]
                pr_, pi_ = PS[0 + (gp % 2) * 2], PS[1 + (gp % 2) * 2]
                cb_, sb_ = RC[:, dg, 0:TB], RS[:, dg, 0:TB]
                p.tt(m1, pr_, cb_, ALU.mult)
                p.tt(m2, pi_, sb_, ALU.mult)
                p.tt(er, m1, m2, ALU.add)
                p.tt(m1, pi_, cb_, ALU.mult)
                p.tt(m2, pr_, sb_, ALU.mult)
                p.tt(ei, m1, m2, ALU.subtract)
                rho_b = T["rho"][:, dg:dg + 1].bcast([128, TB])
                for (w_, e_, ci) in ((wr, er, 0), (wi, ei, 1)):
                    init = 0.0 if bi == 0 else carry.k(dg)[:, dg, ci:ci + 1]
                    oo, a_, b_ = w_.ap, rho_b.ap, e_.ap
                    ini = init if bi == 0 else init.ap
                    rd = [T["rho"], e_] + ([] if bi == 0 else [carry.k(dg)])
                    p.add("dve", (lambda e, oo=oo, a_=a_, b_=b_, ini=ini: e.tensor_tensor_scan(oo, a_, b_, ini, ALU.mult, ALU.add)),
                          rd, [w_])
                c5, s5 = RC[:, dg, TB:TB + 1], RS[:, dg, TB:TB + 1]
                cnew = cnew_[gp % 2]
                p.ts(cnew[:, 0:1], wr[:, TB - 1:TB], c5, None, ALU.mult)
                p.ts(cnew[:, 1:2], wi[:, TB - 1:TB], s5, None, ALU.mult)
                p.ts(cnew[:, 2:3], wr[:, TB - 1:TB], s5, None, ALU.mult)
                p.ts(cnew[:, 3:4], wi[:, TB - 1:TB], c5, None, ALU.mult)
                p.tt(carry.k(dg)[:, dg, 0:1], cnew[:, 0:1], cnew[:, 1:2], ALU.subtract)
                p.tt(carry.k(dg)[:, dg, 1:2], cnew[:, 2:3], cnew[:, 3:4], ALU.add)
                xr_, xi_ = xr[gp % 2], xi[gp % 2]
                p.tt(n1, wr, cb_, ALU.mult, eng="pool")
                p.tt(n2, wi, sb_, ALU.mult, eng="pool")
                p.tt(xr_, n1, n2, ALU.subtract, eng="pool")
                p.tt(n1, wi, cb_, ALU.mult, eng="pool")
                p.tt(n2, wr, sb_, ALU.mult, eng="pool")
                p.tt(xi_, n1, n2, ALU.add, eng="pool")
                xr_o = xr_ if d == 0 else xr_[:, ::-1]
                xi_o = xi_ if d == 0 else xi_[:, ::-1]
                p.mm(py[m], CT[:, dg, 0, :], xr_o, start=(cnt[m] == 0), stop=False)
                p.mm(py[m], CT[:, dg, 1, :], xi_o, start=False, stop=(cnt[m] == 3))
                cnt[m] += 1
            if gp != 7:
                continue
            if d == 0:
                for m in range(2):
                    p.copy(yf[:, m, :], py[m], eng="act")
                p.dma("sp", C.YF.k(b)[:, t0:t0 + TB].re("(m q) t -> q m t", q=128), yf)
            else:
                p.dma("sp", yf, C.YF.k(b)[:, t0:t0 + TB].re("(m q) t -> q m t", q=128))
                for m in range(2):
                    p.tt(yv[:, m, :], py[m], yf[:, m, :], ALU.add)
                    p.stt(yv[:, m, :], uT[:, m, t0:t0 + TB], dsk[:, m:m + 1], yv[:, m, :], ALU.mult, ALU.add)
                    p.tt(g1, yv[:, m, :], yv[:, m, :], ALU.mult)
                    p.ts(g1, g1, 0.044715, 1.0, ALU.mult, ALU.add)
                    p.tt(g1, g1, yv[:, m, :], ALU.mult)
                    p.act(g2t, g1, AF.Sigmoid, scale=1.5957691216057308)
                    p.tt(zf[:, m, :], yv[:, m, :], g2t, ALU.mult)
                    p.copy(zb[:, m, :], zf[:, m, :], eng="pool")
                for oc in range(2):
                    pg = PS[6 + oc]
                    for c in range(2):
                        p.mm(pg, Wgl.k(c)[:, c, oc * 128:(oc + 1) * 128], zb[:, c, :], start=(c == 0), stop=(c == 1))
                    p.act(g2t, pg, AF.Sigmoid, bias=bgl[:, oc:oc + 1])
                    p.tt(dob[:, oc, :], zf[:, oc, :], g2t, ALU.mult)
                p.dma("sp", C.DoT.k(b)[:, t0:t0 + TB].re("(m q) t -> q m t", q=128), dob)
    p.flush()


NE, NO = 2, 2
SAME_ENG_SYNC = True
W_SHAPES = {
    "ffn1_w_gate": (4, 1024, 2816), "ffn1_w_up": (4, 1024, 2816), "ffn1_w_down": (4, 2816, 1024),
    "ffn2_w_gate": (4, 1024, 2816), "ffn2_w_up": (4, 1024, 2816), "ffn2_w_down": (4, 2816, 1024),
    "ab_w_in": (2, 1024, 1792), "ab_w_out": (2, 768, 1024),
    "cd_w_in": (2, 1024, 672), "cd_w_out": (2, 768, 1024),
    "c_w_q_up": (2, 256, 768), "c_w_kv_up": (2, 128, 1024), "d_w_glu": (2, 256, 256),
}


def aux_shapes(S):
    NT, S2 = S // 128, S // 128
    return {
        "ident": ((128, 128), "f"), "gcols": ((128, 12, 8), "f"), "pos": ((128, NT), "i"),
        "gqk": ((2, 128, 16, 64), "f"), "fBD": ((2, 128, 4, 128), "f"),
        "glat": ((2, 128, 384), "f"), "gq96": ((2, 128, 8, 96), "f"), "gk96": ((2, 128, 8, 96), "f"),
        "s5lre": ((2, 128, 16), "f"), "s5lim": ((2, 128, 16), "f"), "s5lst": ((2, 128, 16), "f"),
        "s5bre": ((2, 128, 16, 16), "f"), "s5bim": ((2, 128, 16, 16), "f"),
        "s5ctr": ((2, 128, 16, 128), "f"), "s5cti": ((2, 128, 16, 128), "f"),
        "s5dsk": ((2, 128, 2), "f"), "s5bgl": ((2, 128, 2), "f"),
        "iota": ((128, 513), "f"), "dmask": ((20, 128, 512), "f"),
        "fW1": ((128, 256), "f"), "fT": ((S2, 2, 128), "f"), "fW3": ((S2, 2, 2 * S2), "f"),
    }


def build_program(S, layers=(0, 1, 2, 3), stop_after=None):
    nc = bass.Bass("TRN2", target_bir_lowering=False)
    p = Prog(nc)
    p.same_eng_sync = SAME_ENG_SYNC
    C = Ctx()
    NT = S // 128
    xin = p.dram("x", [S, D], F32, kind="ExternalInput")
    for n, shp in W_SHAPES.items():
        setattr(C, n, p.dram(n, list(shp), F32, kind="ExternalInput"))
    for n, (shp, ty) in aux_shapes(S).items():
        setattr(C, n, p.dram(n, list(shp), F32 if ty == "f" else I32, kind="ExternalInput"))
    out = p.dram("out", [S, D], F32, kind="ExternalOutput")
    C.xres = out
    C.cosA = p.dram("cosA", [128, NT, 8], F32)
    C.sinA = p.dram("sinA", [128, NT, 8], F32)
    C.cosC = p.dram("cosC", [128, NT, 16], F32)
    C.sinC = p.dram("sinC", [128, NT, 16], F32)
    C.QT = p.dram("QT", [512, S], BF16)
    C.KT = p.dram("KT", [512, S], BF16)
    C.Vs = p.dram("Vs", [S, 512], BF16)
    C.Fs = p.dram("Fs", [S, 256], BF16)
    C.AoT = p.dram("AoT", [512, S], BF16)
    C.ZrT = p.dram("ZrT", [256, S], BF16)
    C.ZiT = p.dram("ZiT", [256, S], BF16)
    C.QTc = p.dram("QTc", [8, 96, S], BF16)
    C.KTc = p.dram("KTc", [8, 96, S], BF16)
    C.UT = p.dram("UT", [256, S], BF16)
    C.YF = p.dram("YF", [256, S], F32)
    C.DoT = p.dram("DoT", [256, S], BF16)
    idf = p.sb("idf", [128, 128], F32, glob=True)
    C.ident_f = idf
    C.ident_b = p.sb("idb", [128, 128], BF16, glob=True)
    gcols = p.sb("gcols", [128, 12, 8], F32, glob=True)
    p.dma("sp", idf, C.ident)
    p.copy(C.ident_b, idf)
    p.dma("sp", gcols, C.gcols)
    posi = p.sb("posi", [128, NT], I32)
    posf = p.sb("posf", [128, NT], F32)
    p.dma("sp", posi, C.pos)
    p.copy(posf, posi)
    for (nf, rot, dc, ds) in ((8, 16, C.cosA, C.sinA), (16, 32, C.cosC, C.sinC)):
        ang = p.sb(f"ang{nf}", [128, NT, nf], F32)
        t1 = p.sb(f"rt1{nf}", [128, NT, nf], F32)
        t2 = p.sb(f"rt2{nf}", [128, NT, nf], F32)
        ti = p.sb(f"rti{nf}", [128, NT, nf], I32)
        so = p.sb(f"rso{nf}", [128, NT, nf], F32)
        co = p.sb(f"rco{nf}", [128, NT, nf], F32)
        invf = (500000.0 ** (-np.arange(0, rot, 2, dtype=np.float32) / np.float32(rot))).astype(np.float32)
        for f in range(nf):
            p.ts(ang[:, :, f], posf, float(invf[f]), None, ALU.mult)
        sincos(p, so, co, ang, t1, t2, ti)
        p.dma("sp", dc, co)
        p.dma("sp", ds, so)
    p.flush()
    first = True
    for L in layers:
        i = L // 2
        src = xin if first else out
        first = False
        ffn_phase(p, C, src, out, C.ffn1_w_gate[L], C.ffn1_w_up[L], C.ffn1_w_down[L], gcols[:, L * 3 + 0, :], S)
        if stop_after == (L, 0):
            break
        gmix = gcols[:, L * 3 + 1, :]
        if L % 2 == 0:
            even_proj_phase(p, C, i, gmix, S)
            attention_phase(p, C, S, 8, 64, lambda h: C.QT[h * 64:(h + 1) * 64, :], lambda h: C.KT[h * 64:(h + 1) * 64, :],
                            C.Vs, C.AoT, True)
            fnet_phase(p, C, S)
            outproj_phase(p, C, S, C.ab_w_out[i], [(C.AoT, 4), (C.ZrT, 2), (C.ZiT, 2)], fold=i)
        else:
            odd_proj_phase(p, C, i, gmix, S)
            attention_phase(p, C, S, 8, 96, lambda h: C.QTc[h], lambda h: C.KTc[h], C.Vs, C.AoT, False)
            s5_phase(p, C, i, S)
            outproj_phase(p, C, S, C.cd_w_out[i], [(C.AoT, 4), (C.DoT, 2)])
        if stop_after == (L, 1):
            break
        ffn_phase(p, C, out, out, C.ffn2_w_gate[L], C.ffn2_w_up[L], C.ffn2_w_down[L], gcols[:, L * 3 + 2, :], S)
    p.finish()
    return nc, p


def host_aux(inp, S, positions_row):
    f32 = np.float32
    NT = S // 128
    S2 = S // 128
    a = {}
    a["ident"] = np.eye(128, dtype=f32)
    g = np.zeros((128, 12, 8), f32)
    for L in range(4):
        for n, nm in enumerate(("ffn1_norm", "mix_norm", "ffn2_norm")):
            g[:, L * 3 + n, :] = np.asarray(inp[nm][L], f32).reshape(8, 128).T
    a["gcols"] = g
    a["pos"] = np.ascontiguousarray(np.asarray(positions_row, np.int32).reshape(NT, 128).T)
    gqk = np.zeros((2, 128, 16, 64), f32)
    fBD = np.zeros((2, 128, 4, 128), f32)
    cidx = np.arange(64)
    Cc = np.cos(2 * np.pi * np.outer(cidx, cidx) / 64).astype(f32)
    Sc = np.sin(2 * np.pi * np.outer(cidx, cidx) / 64).astype(f32)
    for i in range(2):
        gqk[i, :, 0:8, :] = np.asarray(inp["a_q_norm"][i], f32)[None, None, :]
        gqk[i, :, 8:16, :] = np.asarray(inp["a_k_norm"][i], f32)[None, None, :]
        bm = np.asarray(inp["b_w_mix"][i], f32)
        for pr in range(2):
            for g2 in range(2):
                fBD[i, g2 * 64:(g2 + 1) * 64, pr, g2 * 64:(g2 + 1) * 64] = bm[pr * 2 + g2].T
        for g2 in range(2):
            fBD[i, g2 * 64:(g2 + 1) * 64, 2, g2 * 64:(g2 + 1) * 64] = Cc
            fBD[i, g2 * 64:(g2 + 1) * 64, 3, g2 * 64:(g2 + 1) * 64] = Sc
    a["gqk"], a["fBD"] = gqk, fBD
    glat = np.zeros((2, 128, 384), f32)
    gq96 = np.zeros((2, 128, 8, 96), f32)
    gk96 = np.zeros((2, 128, 8, 96), f32)
    for i in range(2):
        glat[i, :, 0:256] = np.asarray(inp["c_q_lat_norm"][i], f32)[None, :]
        glat[i, :, 256:384] = np.asarray(inp["c_kv_lat_norm"][i], f32)[None, :]
        gq96[i] = np.asarray(inp["c_q_norm"][i], f32)[None, None, :]
        gk96[i] = np.asarray(inp["c_k_norm"][i], f32)[None, None, :]
    a["glat"], a["gq96"], a["gk96"] = glat, gq96, gk96

    def sl(arr):
        return np.ascontiguousarray(np.asarray(arr, f32).reshape(2, 8, 2, 64).transpose(2, 3, 0, 1).reshape(128, 16))
    a["s5lre"] = np.stack([sl(inp["d_lam_re"][i]) for i in range(2)])
    a["s5lim"] = np.stack([sl(inp["d_lam_im"][i]) for i in range(2)])
    a["s5lst"] = np.stack([sl(np.broadcast_to(np.asarray(inp["d_log_step"][i], f32)[:, :, None], (2, 16, 64))) for i in range(2)])

    def slb(arr):
        return np.ascontiguousarray(np.asarray(arr, f32).reshape(2, 8, 2, 64, 16).transpose(2, 3, 0, 1, 4).reshape(128, 16, 16))
    a["s5bre"] = np.stack([slb(inp["d_b_re"][i]) for i in range(2)])
    a["s5bim"] = np.stack([slb(inp["d_b_im"][i]) for i in range(2)])

    def slc(arr):
        arr = np.asarray(arr, f32)
        o = np.zeros((128, 16, 128), f32)
        for d in range(2):
            for gp in range(8):
                for g2 in range(2):
                    gg = gp * 2 + g2
                    c0 = ((gp % 4) * 2 + g2) * 16
                    o[g2 * 64:(g2 + 1) * 64, d * 8 + gp, c0:c0 + 16] = arr[d, gg].T
        return o
    a["s5ctr"] = np.stack([slc(inp["d_c_re"][i]) for i in range(2)])
    a["s5cti"] = np.stack([slc(inp["d_c_im"][i]) for i in range(2)])
    a["s5dsk"] = np.stack([np.asarray(inp["d_skip"][i], f32).reshape(2, 128).T for i in range(2)])
    a["s5bgl"] = np.stack([np.asarray(inp["d_b_glu"][i], f32).reshape(2, 128).T for i in range(2)])
    a["iota"] = np.broadcast_to(np.arange(513, dtype=f32)[None, :], (128, 513)).copy()
    ki = np.arange(128)[:, None]
    qi = np.arange(512)[None, :]
    dm = np.zeros((20, 128, 512), f32)
    for o in range(20):
        dl = (o * 128 - 1024) + ki - qi
        m = np.zeros_like(dl, dtype=f32)
        for dil in (1, 4, 16):
            m += ((dl % dil == 0) & (np.abs(dl) <= 64 * dil)).astype(f32)
        dm[o] = m
    a["dmask"] = dm
    s1 = np.arange(128)
    k1 = np.arange(128)
    ang = 2 * np.pi * np.outer(s1, k1) / 128
    nrm = 1.0 / np.sqrt(S * 64.0)
    a["fW1"] = (np.concatenate([np.cos(ang), -np.sin(ang)], axis=1) * nrm).astype(f32)
    s2 = np.arange(S2)
    phi = 2 * np.pi * np.outer(s2, k1) / S
    a["fT"] = np.stack([np.cos(phi), np.sin(phi)], axis=1).astype(f32)
    th = 2 * np.pi * np.outer(s2, np.arange(S2)) / S2
    a["fW3"] = np.stack([np.concatenate([np.cos(th), -np.sin(th)], 1), np.concatenate([np.sin(th), np.cos(th)], 1)], axis=1).astype(f32)
    for k in a:
        a[k] = np.ascontiguousarray(a[k])
    return a


def run_module(inp, S, layers=(0, 1, 2, 3), stop_after=None, trace=False):
    x = np.asarray(inp["x"], np.float32)
    B = x.shape[0]
    nc, p = build_program(S, layers, stop_after)
    base = {n: np.ascontiguousarray(np.asarray(inp[n], np.float32)) for n in W_SHAPES}
    in_maps = []
    for b in range(B):
        m = dict(base)
        m.update(host_aux(inp, S, np.asarray(inp["positions"])[b]))
        m["x"] = np.ascontiguousarray(x[b])
        in_maps.append(m)
    res = run_bass_kernel_spmd(nc, in_maps, core_ids=list(range(B)), trace=trace)
    return np.stack([r["out"] for r in res.results], axis=0), res, p


def kernel(**inputs):
    out, _, _ = run_module(inputs, 8192)
    return out.astype(np.float32)
```
